# Optimizing a Trainium2 kernel written in Bass

```python
import jax, jax.numpy as jnp
from jax import lax
import numpy as np

D_MODEL = 2048
BATCH = 4
SEQ = 4096
DEPTH = 1

CHUNK = 64
D_MIX = D_MODEL
CONV_WIDTH = D_MIX // 2
CONV_GROUPS = 8
CONV_KERNEL = 31
DN_HEADS = 8
DN_HEAD_DIM = (D_MIX - CONV_WIDTH) // DN_HEADS
DN_WIDTH = DN_HEADS * DN_HEAD_DIM
SHORT_CONV = 4
D_FF = ((8 * D_MODEL // 3 + 255) // 256) * 256
N_MOD = 9
EPS = 1e-6
IN_SPLITS = [CONV_WIDTH, CONV_WIDTH, DN_WIDTH, DN_WIDTH, DN_WIDTH, DN_WIDTH, DN_HEADS, DN_HEADS]
IN_COLS = sum(IN_SPLITS)

kernel_name = "hybrid_conv_gdn_macaron_adaln"


def rmsnorm(x, w):
    xf = x.astype(jnp.float32)
    y = xf * lax.rsqrt(jnp.mean(xf * xf, axis=-1, keepdims=True) + EPS)
    return (y * w.astype(jnp.float32)).astype(x.dtype)


def modulate(h, shift, scale):
    return h * (1 + scale[:, None, :]) + shift[:, None, :]


def swiglu(h, wg, wu, wd):
    return (jax.nn.silu(h @ wg) * (h @ wu)) @ wd


def causal_dwconv(x, w):
    k = w.shape[0]
    return lax.conv_general_dilated(
        x, w[:, None, :].astype(x.dtype), window_strides=(1,), padding=[(k - 1, 0)],
        dimension_numbers=("NWC", "WIO", "NWC"), feature_group_count=x.shape[-1])


def l2norm(t):
    return t * lax.rsqrt(jnp.sum(t * t, axis=-1, keepdims=True) + EPS)


def conformer_conv_group(a, gate, w_dw, b_dw, ln_w, ln_b):
    h = a * jax.nn.sigmoid(gate)
    h = causal_dwconv(h, w_dw) + b_dw.astype(h.dtype)
    hf = h.astype(jnp.float32)
    mu = jnp.mean(hf, axis=-1, keepdims=True)
    var = jnp.mean(jnp.square(hf - mu), axis=-1, keepdims=True)
    hf = (hf - mu) * lax.rsqrt(var + EPS) * ln_w.astype(jnp.float32) + ln_b.astype(jnp.float32)
    return jax.nn.silu(hf).astype(a.dtype)


def gated_deltanet_group(q, k, v, z, b_raw, a_raw, w_short, a_log, dt_bias, onorm_w):
    B, S, _ = q.shape
    H, Dh, C = DN_HEADS, DN_HEAD_DIM, CHUNK
    N = S // C
    f32 = jnp.float32
    qkv = jax.nn.silu(causal_dwconv(jnp.concatenate([q, k, v], axis=-1), w_short))
    q, k, v = jnp.split(qkv.astype(f32), 3, axis=-1)

    def heads(t):
        return t.reshape(B, N, C, H, Dh).transpose(0, 3, 1, 2, 4)

    def hc(t):
        return t.reshape(B, N, C, H).transpose(0, 3, 1, 2)

    q = l2norm(heads(q)) * (Dh ** -0.5)
    k = l2norm(heads(k))
    v = heads(v)
    beta = hc(jax.nn.sigmoid(b_raw.astype(f32)))
    g = hc(-jnp.exp(a_log.astype(f32)) * jax.nn.softplus(a_raw.astype(f32) + dt_bias.astype(f32)))
    G = jnp.cumsum(g, axis=-1)

    idx = jnp.arange(C)
    tril = idx[:, None] >= idx[None, :]
    strict = idx[:, None] > idx[None, :]
    decay = jnp.exp(jnp.where(tril, G[..., :, None] - G[..., None, :], -jnp.inf))

    kb = k * beta[..., None]
    L = jnp.einsum("bhncd,bhnsd->bhncs", kb, k) * jnp.where(strict, decay, 0.0)
    rhs = jnp.concatenate([v * beta[..., None], kb * jnp.exp(G)[..., None]], axis=-1)
    sol = lax.linalg.triangular_solve(L + jnp.eye(C, dtype=f32), rhs, left_side=True,
                                      lower=True, unit_diagonal=True)
    u, w = jnp.split(sol, 2, axis=-1)
    a_intra = jnp.einsum("bhncd,bhnsd->bhncs", q, k) * decay

    def step(state, xs):
        q_n, k_n, u_n, w_n, a_n, g_n = xs
        v_new = u_n - jnp.einsum("bhcd,bhde->bhce", w_n, state)
        o = (jnp.einsum("bhcd,bhde->bhce", q_n * jnp.exp(g_n)[..., None], state)
             + jnp.einsum("bhcs,bhse->bhce", a_n, v_new))
        g_last = g_n[..., -1]
        state = (state * jnp.exp(g_last)[..., None, None]
                 + jnp.einsum("bhcd,bhce->bhde", k_n * jnp.exp(g_last[..., None] - g_n)[..., None], v_new))
        return state, o

    xs = tuple(jnp.moveaxis(t, 2, 0) for t in (q, k, u, w, a_intra, G))
    state0 = jnp.zeros((B, H, Dh, Dh), f32)
    _, o = lax.scan(step, state0, xs)
    o = o.transpose(1, 0, 3, 2, 4).reshape(B, S, H, Dh)
    o = o * lax.rsqrt(jnp.mean(o * o, axis=-1, keepdims=True) + EPS) * onorm_w.astype(f32)
    o = o * jax.nn.silu(z.astype(f32).reshape(B, S, H, Dh))
    return o.reshape(B, S, DN_WIDTH).astype(z.dtype)


def setup_inputs(seed: int = 0) -> dict:
    key = jax.random.key(seed)
    ks = jax.random.split(key, 32)
    f32 = jnp.float32
    nrm = lambda k, shape, s: jax.random.normal(k, shape, f32) * s
    D, L = D_MODEL, DEPTH
    dt = jnp.exp(jax.random.uniform(ks[16], (L, DN_HEADS), f32, np.log(1e-3), np.log(1e-1)))
    return {
        "x": nrm(ks[0], (BATCH, SEQ, D), 1.0),
        "c": nrm(ks[1], (BATCH, D), 1.0),
        "w_ada": nrm(ks[2], (L, D, N_MOD * D), D ** -0.5),
        "b_ada": nrm(ks[3], (L, N_MOD * D), 0.01),
        "ffn1_norm": 1.0 + nrm(ks[4], (L, D), 0.01),
        "ffn1_wg": nrm(ks[5], (L, D, D_FF), D ** -0.5),
        "ffn1_wu": nrm(ks[6], (L, D, D_FF), D ** -0.5),
        "ffn1_wd": nrm(ks[7], (L, D_FF, D), D_FF ** -0.5),
        "mix_norm": 1.0 + nrm(ks[8], (L, D), 0.01),
        "w_in": nrm(ks[9], (L, D, IN_COLS), D ** -0.5),
        "w_dw": nrm(ks[10], (L, CONV_KERNEL, CONV_WIDTH), CONV_KERNEL ** -0.5),
        "b_dw": nrm(ks[11], (L, CONV_WIDTH), 0.01),
        "conv_ln_w": 1.0 + nrm(ks[12], (L, CONV_WIDTH), 0.01),
        "conv_ln_b": nrm(ks[13], (L, CONV_WIDTH), 0.01),
        "w_short": nrm(ks[14], (L, SHORT_CONV, 3 * DN_WIDTH), SHORT_CONV ** -0.5),
        "a_log": jnp.log(jax.random.uniform(ks[15], (L, DN_HEADS), f32, 1.0, 16.0)),
        "dt_bias": dt + jnp.log(-jnp.expm1(-dt)),
        "dn_norm_w": 1.0 + nrm(ks[17], (L, DN_HEAD_DIM), 0.01),
        "w_out": nrm(ks[18], (L, D_MIX, D), D_MIX ** -0.5),
        "ffn2_norm": 1.0 + nrm(ks[19], (L, D), 0.01),
        "ffn2_wg": nrm(ks[20], (L, D, D_FF), D ** -0.5),
        "ffn2_wu": nrm(ks[21], (L, D, D_FF), D ** -0.5),
        "ffn2_wd": nrm(ks[22], (L, D_FF, D), D_FF ** -0.5),
        "final_norm": 1.0 + nrm(ks[23], (D,), 0.01),
    }


def reference(x, c, w_ada, b_ada, ffn1_norm, ffn1_wg, ffn1_wu, ffn1_wd, mix_norm, w_in,
              w_dw, b_dw, conv_ln_w, conv_ln_b, w_short, a_log, dt_bias, dn_norm_w, w_out,
              ffn2_norm, ffn2_wg, ffn2_wu, ffn2_wd, final_norm):
    B = x.shape[0]
    split_idx = list(np.cumsum(IN_SPLITS)[:-1])
    for l in range(DEPTH):
        mods = (jax.nn.silu(c) @ w_ada[l] + b_ada[l]).reshape(B, N_MOD, D_MODEL)
        h = modulate(rmsnorm(x, ffn1_norm[l]), mods[:, 0], mods[:, 1])
        x = x + 0.5 * mods[:, 2][:, None, :] * swiglu(h, ffn1_wg[l], ffn1_wu[l], ffn1_wd[l])
        h = modulate(rmsnorm(x, mix_norm[l]), mods[:, 3], mods[:, 4])
        p = h @ w_in[l]
        ca, cg, q, k, v, z, b_raw, a_raw = jnp.split(p, split_idx, axis=-1)
        y_conv = conformer_conv_group(ca, cg, w_dw[l], b_dw[l], conv_ln_w[l], conv_ln_b[l])
        y_dn = gated_deltanet_group(q, k, v, z, b_raw, a_raw, w_short[l], a_log[l],
                                    dt_bias[l], dn_norm_w[l])
        y = jnp.concatenate([y_conv, y_dn], axis=-1) @ w_out[l]
        x = x + mods[:, 5][:, None, :] * y
        h = modulate(rmsnorm(x, ffn2_norm[l]), mods[:, 6], mods[:, 7])
        x = x + 0.5 * mods[:, 8][:, None, :] * swiglu(h, ffn2_wg[l], ffn2_wu[l], ffn2_wd[l])
    return rmsnorm(x, final_norm)
```

```python
import numpy as np
import os
CUT = int(os.environ.get('KCUT', '99'))
from contextlib import ExitStack
import concourse.bass as bass
import concourse.mybir as mybir
from concourse.bass_utils import run_bass_kernel_spmd

F32 = mybir.dt.float32
BF16 = mybir.dt.bfloat16
AF = mybir.ActivationFunctionType
ALU = mybir.AluOpType
NEG = -1.0e9
EPS = 1e-6


class Buf:
    __slots__ = ("name", "w", "r", "dsem", "psum")

    def __init__(self, name, psum=False):
        self.name = name
        self.psum = psum
        self.w = None
        self.r = []
        self.dsem = None


class V:
    __slots__ = ("b", "ap")

    def __init__(self, b, ap):
        self.b = b
        self.ap = ap


class DSem:
    def __init__(self, sem):
        self.sem = sem
        self.count = 0


class Rec:
    __slots__ = ("eng", "fn", "deps", "sig", "idx", "dma", "dval")

    def __init__(self, eng, fn, deps, sig, dma=None):
        self.eng = eng
        self.fn = fn
        self.deps = deps
        self.sig = sig
        self.dma = dma
        self.dval = 0
        self.idx = -1


class Sched:
    ENGS = ("pe", "dve", "act", "pool", "sp")

    def __init__(self, nc, es):
        self.nc = nc
        self.es = es
        self.q = {e: [] for e in self.ENGS}
        self.esem = {e: es.enter_context(nc.semaphore("s_" + e)) for e in ("pe", "dve", "act", "pool")}
        self.nsem = 0

    def _deps(self, eng, reads, writes, is_dma):
        deps = []
        for v in reads:
            w = v.b.w
            if w is not None:
                deps.append((w, "raw"))
            if v.b.psum:
                for r in v.b.r:
                    if r.eng != eng:
                        deps.append((r, "rr"))
        for v in writes:
            b = v.b
            if b.w is not None:
                deps.append((b.w, "waw"))
            for r in b.r:
                deps.append((r, "war"))
        out = []
        for (d, kind) in deps:
            if d.dma is None and not is_dma and d.eng == eng:
                if eng == "pe":
                    continue
            if d.dma is not None:
                out.append((d, d.dma.count))
            else:
                out.append((d, None))
        return out

    def _commit(self, rec, reads, writes):
        for v in writes:
            v.b.w = rec
            v.b.r = []
        for v in reads:
            if v.b.w is not rec:
                v.b.r.append(rec)

    def op(self, eng, fn, reads=(), writes=(), sig=True):
        rec = Rec(eng, fn, self._deps(eng, reads, writes, False), sig)
        rec.idx = len(self.q[eng])
        self.q[eng].append(rec)
        self._commit(rec, reads, writes)
        return rec

    def dma(self, queue, out, in_, reads, writes, key, slow=False):
        if key.dsem is None:
            key.dsem = DSem(self.es.enter_context(self.nc.semaphore("d%d" % self.nsem)))
            self.nsem += 1
        ds = key.dsem
        rec = Rec(queue, lambda e: e.dma_start(out=out, in_=in_, allow_slow_non_contiguous=slow), self._deps(queue, reads, writes, True), True, dma=ds)
        ds.count += 16
        rec.dval = ds.count
        rec.idx = len(self.q[queue])
        self.q[queue].append(rec)
        self._commit(rec, reads, writes)
        return rec

    def emit(self, final_waits=()):
        nc = self.nc
        cnt = {}
        for e in ("pe", "dve", "act", "pool"):
            arr = []
            c = 0
            for r in self.q[e]:
                if r.dma is None and r.sig:
                    c += 1
                arr.append(c)
            res = [0] * len(arr)
            nxt = None
            for i in range(len(arr) - 1, -1, -1):
                r = self.q[e][i]
                if r.dma is None and r.sig:
                    nxt = arr[i]
                res[i] = nxt
            cnt[e] = res
        handles = {"pe": nc.tensor, "dve": nc.vector, "act": nc.scalar, "pool": nc.gpsimd, "sp": nc.sync}
        with nc.Block() as block:
            def run(e, h):
                waited = {}
                for r in self.q[e]:
                    for (d, dv) in r.deps:
                        if d.dma is not None:
                            sem, val = d.dma.sem, dv
                        else:
                            val = cnt[d.eng][d.idx]
                            assert val is not None, "dep on op with no later signal"
                            sem = self.esem[d.eng]
                        k = id(sem)
                        if waited.get(k, 0) >= val:
                            continue
                        waited[k] = val
                        h.wait_ge(sem, val)
                    ins = r.fn(h)
                    if r.dma is not None:
                        ins.then_inc(r.dma.sem, 16)
                    elif r.sig:
                        ins.then_inc(self.esem[e], 1)
                if e == "sp":
                    for b in final_waits:
                        h.wait_ge(b.dsem.sem, b.dsem.count)

            @block.tensor
            def _(h):
                run("pe", h)

            @block.vector
            def _(h):
                run("dve", h)

            @block.scalar
            def _(h):
                run("act", h)

            @block.gpsimd
            def _(h):
                run("pool", h)

            @block.sync
            def _(h):
                run("sp", h)


def make_cfg(D, DFF, CW, NH, T):
    return dict(D=D, DFF=DFF, CW=CW, NH=NH, T=T, KD=D // 128, KF=DFF // 128, KC=CW // 128,
                TT=min(512, T), INC=2 * CW + 4 * NH * 128 + 2 * NH)


FULL = make_cfg(2048, 5632, 1024, 8, 4096)


def build(cfg, stop=None):
    D, DFF, CW, NH, T = cfg["D"], cfg["DFF"], cfg["CW"], cfg["NH"], cfg["T"]
    KD, KF, KC, TT, INC = cfg["KD"], cfg["KF"], cfg["KC"], cfg["TT"], cfg["INC"]
    NT = T // TT
    NB = TT // 128
    KY = KC + NH
    DNW = NH * 128
    nc = bass.Bass("TRN2", target_bir_lowering=False)
    es = ExitStack()
    S = Sched(nc, es)

    def din(name, shape, dt=F32):
        return nc.dram_tensor(name, list(shape), dt, kind="ExternalInput").ap()

    x_d = din("x", [T, D])
    y_d = nc.dram_tensor("y", [T, D], F32, kind="ExternalOutput").ap()
    cT_d = din("cT", [128, KD])
    wada_d = din("w_ada", [D, 9 * D])
    badaT_d = din("b_adaT", [128, 9 * KD])
    normT_d = din("normT", [128, 4, KD])
    wg_d = [din("ffn1_wg", [D, DFF]), din("ffn2_wg", [D, DFF])]
    wu_d = [din("ffn1_wu", [D, DFF]), din("ffn2_wu", [D, DFF])]
    wd_d = [din("ffn1_wd", [DFF, D]), din("ffn2_wd", [DFF, D])]
    win_d = din("w_in", [D, INC])
    wout_d = din("w_out", [D, D])
    wdwT_d = din("w_dwT", [128, KC, 31])
    cvec_d = din("cvecT", [128, 3, KC])
    wshT_d = din("w_shT", [128, 3 * NH, 4])
    hvec_d = din("hvec", [128, 2, NH])
    onw_d = din("onwT", [128, 1])
    cst_d = din("consts", [128, 12, 128])
    rmask_d = din("rmask", [128, TT])

    SBTOT = [0]

    def sb(name, shape, dt=F32):
        t = es.enter_context(nc.sbuf_tensor("sb_" + name, list(shape), dt))
        nbytes = int(np.prod(shape[1:])) * (2 if dt == BF16 else 4)
        SBTOT[0] += nbytes
        if os.environ.get("KSB"):
            print("SB", name, nbytes, SBTOT[0])
        return t, Buf(name)

    cst, cst_b = sb("cst", [128, 12, 128])
    rmask, rmask_b = sb("rmask", [128, TT])
    ones_bf, ones_bf_b = sb("ones_bf", [128, 128], BF16)
    normT, normT_b = sb("normT", [128, 4, KD])
    wdwT, wdwT_b = sb("wdwT", [128, KC, 31])
    cvec, cvec_b = sb("cvec", [128, 3, KC])
    wshT, wshT_b = sb("wshT", [128, 3 * NH, 4])
    hvec, hvec_b = sb("hvec", [128, 2, NH])
    negA, negA_b = sb("negA", [128, NH])
    onw, onw_b = sb("onw", [128, 1])
    modsT, modsT_b = sb("modsT", [128, 9 * KD])
    Acoef, Acoef_b = sb("Acoef", [128, 3, KD])
    Gcoef, Gcoef_b = sb("Gcoef", [128, 3, KD])
    finw_b = normT_b
    xT, xT_b = sb("xT", [128, KD, TT])
    hT, hT_b = sb("hT", [128, KD, TT], BF16)
    KFH = KF // 2
    actT = [sb("actT%d" % i, [128, max(KFH, KD), TT], BF16) for i in range(1)]
    yT, yT_b = sb("yT", [128, KY, TT], BF16)
    sq, sq_b = actT[0]
    rstd, rstd_b = sb("rstd", [128, TT])
    xin_all, xin_all_b = sb("xin_all", [128, 2, D])
    xin = [(xin_all[:, i, :], xin_all_b) for i in range(2)]
    Sst = [sb("Sst%d" % h, [128, 128]) for h in range(NH)]
    gtail = [sb("gtail%d" % j, [128, 30]) for j in range(KC)]
    ptail = [sb("ptail%d" % j, [128, 3]) for j in range(3 * NH)]
    NWA, NWD = 4, 2
    wa = [sb("wa%d" % i, [128, KD, 128], BF16) for i in range(NWA)]
    wdp = [sb("wd%d" % i, [128, KF // 2, 128], BF16) for i in range(NWD)]
    wa_i = [0]
    wd_i = [0]
    psum = []
    for i in range(8):
        t = es.enter_context(nc.psum_tensor("ps%d" % i, [128, 512], F32))
        psum.append((t, Buf("ps%d" % i, psum=True)))
    ps_i = [0]

    def PS():
        t, b = psum[ps_i[0] % 8]
        ps_i[0] += 1
        return t, b

    IDN = V(cst_b, cst[:, 0, :])
    ONES = V(cst_b, cst[:, 1, :])
    NEGS = V(cst_b, cst[:, 2, :])
    NEGST = V(cst_b, cst[:, 3, :])
    NEGTT = V(cst_b, cst[:, 4, :])
    BD16 = V(cst_b, cst[:, 5, :])
    LLm = [V(cst_b, cst[:, 6 + i, :]) for i in range(3)]
    URm = [V(cst_b, cst[:, 9 + i, :]) for i in range(3)]

    def mm(out, lhsT, rhs, start=True, stop=True, sig=None):
        if sig is None:
            sig = stop
        return S.op("pe", lambda e: e.matmul(out.ap, lhsT.ap, rhs.ap, start=start, stop=stop),
                    reads=[lhsT, rhs], writes=[out], sig=sig)

    def tr(out, in_):
        return S.op("pe", lambda e: e.transpose(out.ap, in_.ap, IDN.ap), reads=[in_, IDN], writes=[out])

    def act(out, in_, func, bias=None, scale=None, accum=None, eng="act"):
        rd = [in_]
        kw = {}
        if bias is not None:
            if isinstance(bias, V):
                rd.append(bias)
                kw["bias"] = bias.ap
            else:
                kw["bias"] = float(bias)
        if scale is not None:
            if isinstance(scale, V):
                rd.append(scale)
                kw["scale"] = scale.ap
            else:
                kw["scale"] = float(scale)
        wr = [out]
        if accum is not None:
            kw["accum_out"] = accum.ap
            wr.append(accum)
        return S.op("act", lambda e: e.activation(out.ap, in_.ap, func, **kw), reads=rd, writes=wr)

    def tt(out, a, b, op, eng="dve"):
        return S.op(eng, lambda e: e.tensor_tensor(out.ap, a.ap, b.ap, op), reads=[a, b], writes=[out])

    def ts(out, a, s1, s2, op0, op1=None, eng="dve"):
        rd = [a]
        s1a = s1.ap if isinstance(s1, V) else s1
        s2a = s2.ap if isinstance(s2, V) else s2
        if isinstance(s1, V):
            rd.append(s1)
        if isinstance(s2, V):
            rd.append(s2)
        if op1 is None:
            return S.op(eng, lambda e: e.tensor_scalar(out.ap, a.ap, s1a, None, op0), reads=rd, writes=[out])
        return S.op(eng, lambda e: e.tensor_scalar(out.ap, a.ap, s1a, s2a, op0, op1), reads=rd, writes=[out])

    def stt(out, a, s, b, op0, op1):
        rd = [a, b]
        sa = s.ap if isinstance(s, V) else s
        if isinstance(s, V):
            rd.append(s)
        return S.op("dve", lambda e: e.scalar_tensor_tensor(out.ap, a.ap, sa, b.ap, op0, op1), reads=rd, writes=[out])

    def cp(out, in_, eng="dve"):
        if eng == "act":
            return act(out, in_, AF.Copy)
        return S.op(eng, lambda e: e.tensor_copy(out.ap, in_.ap), reads=[in_], writes=[out])

    def ld(out, src_ap, queue="sp"):
        return S.dma(queue, out.ap, src_ap, reads=[], writes=[out], key=out.b)

    ld(V(cst_b, cst[:]), cst_d)
    ld(V(rmask_b, rmask[:]), rmask_d)
    ld(V(normT_b, normT[:]), normT_d)
    ld(V(wdwT_b, wdwT[:]), wdwT_d)
    ld(V(cvec_b, cvec[:]), cvec_d)
    ld(V(wshT_b, wshT[:]), wshT_d)
    ld(V(hvec_b, hvec[:]), hvec_d)
    ld(V(onw_b, onw[:]), onw_d)
    cp(V(ones_bf_b, ones_bf[:]), ONES)
    act(V(negA_b, negA[:]), V(hvec_b, hvec[:, 0, :]), AF.Exp)
    ts(V(negA_b, negA[:]), V(negA_b, negA[:]), -1.0, None, ALU.mult)
    for h in range(NH):
        S.op("dve", lambda e, h=h: e.memset(Sst[h][0][:], 0.0), writes=[V(Sst[h][1], Sst[h][0][:])])
    for j in range(KC):
        S.op("dve", lambda e, j=j: e.memset(gtail[j][0][:], 0.0), writes=[V(gtail[j][1], gtail[j][0][:])])
    for j in range(3 * NH):
        S.op("dve", lambda e, j=j: e.memset(ptail[j][0][:], 0.0), writes=[V(ptail[j][1], ptail[j][0][:])])

    scT, scT_b = sb("scT", [128, KD])
    badaT, badaT_b = sb("badaT", [128, 9 * KD])
    ld(V(scT_b, scT[:]), cT_d)
    ld(V(badaT_b, badaT[:]), badaT_d)
    act(V(scT_b, scT[:]), V(scT_b, scT[:]), AF.Silu)
    wada_v = wada_d.rearrange("(k p) n -> p k n", p=128)
    NMC = 9 * KD
    pm, pm_b = PS()
    for j in range(NMC):
        wt0, wb = xin[j % 2]
        wt = wt0[:].rearrange("p (k n) -> p k n", k=KD)
        S.dma("sp", wt, wada_v[:, :, j * 128:(j + 1) * 128], reads=[], writes=[V(wb, wt)], key=wb)
        for kc in range(KD):
            mm(V(pm_b, pm[:, j:j + 1]), V(wb, wt[:, kc, :]), V(scT_b, scT[:, kc:kc + 1]),
               start=(kc == 0), stop=(kc == KD - 1))
    tt(V(modsT_b, modsT[:]), V(pm_b, pm[:, 0:NMC]), V(badaT_b, badaT[:]), ALU.add)
    for i in range(3):
        sc = V(modsT_b, modsT[:, (3 * i + 1) * KD:(3 * i + 2) * KD])
        gt = V(modsT_b, modsT[:, (3 * i + 2) * KD:(3 * i + 3) * KD])
        stt(V(Acoef_b, Acoef[:, i, :]), sc, 1.0, V(normT_b, normT[:, i, :]), ALU.add, ALU.mult)
        ts(V(Gcoef_b, Gcoef[:, i, :]), gt, 0.5 if i != 1 else 1.0, None, ALU.mult)

    def shift_col(i, kc):
        return V(modsT_b, modsT[:, 3 * i * KD + kc:3 * i * KD + kc + 1])

    def rms_rstd():
        for kc in range(KD):
            act(V(sq_b, sq[:, kc, :]), V(xT_b, xT[:, kc, :]), AF.Square)
        p, pb = PS()
        for kc in range(KD):
            mm(V(pb, p[:, 0:TT]), V(ones_bf_b, ones_bf[:]), V(sq_b, sq[:, kc, :]), start=(kc == 0), stop=(kc == KD - 1))
        act(V(rstd_b, rstd[:]), V(pb, p[:, 0:TT]), AF.Sqrt, bias=EPS, scale=1.0 / D)
        S.op("dve", lambda e: e.reciprocal(rstd[:], rstd[:]), reads=[V(rstd_b, rstd[:])], writes=[V(rstd_b, rstd[:])])

    tmpn, tmpn_b = sb("tmpn", [128, TT])
    cacc, cacc_b = tmpn, tmpn_b

    def rms_mod(i):
        rms_rstd()
        for kc in range(KD):
            tt(V(tmpn_b, tmpn[:]), V(xT_b, xT[:, kc, :]), V(rstd_b, rstd[:]), ALU.mult)
            act(V(hT_b, hT[:, kc, :]), V(tmpn_b, tmpn[:]), AF.Identity,
                bias=shift_col(i, kc), scale=V(Acoef_b, Acoef[:, i, kc:kc + 1]))

    def wload(dst_pool, idx, src, K):
        t, b = dst_pool[idx[0] % len(dst_pool)]
        idx[0] += 1
        S.dma("pool", t[:, 0:K, :], src, reads=[], writes=[V(b, t[:])], key=b)
        return t, b

    sg, sg_b = sb("sg", [128, TT])

    def ffn(f, i):
        aT, aT_b = actT[0]
        wgv = wg_d[f].rearrange("(k p) n -> p k n", p=128)
        wuv = wu_d[f].rearrange("(k p) n -> p k n", p=128)
        wdv = wd_d[f].rearrange("(k p) n -> p k n", p=128)
        for hf in range(2):
            for jj in range(KFH):
                j = hf * KFH + jj
                gt_, gb_ = wload(wa, wa_i, wgv[:, :, j * 128:(j + 1) * 128], KD)
                ut_, ub_ = wload(wa, wa_i, wuv[:, :, j * 128:(j + 1) * 128], KD)
                pg, pgb = PS()
                pu, pub = PS()
                for kc in range(KD):
                    mm(V(pgb, pg[:, 0:TT]), V(gb_, gt_[:, kc, :]), V(hT_b, hT[:, kc, :]), start=(kc == 0), stop=(kc == KD - 1))
                for kc in range(KD):
                    mm(V(pub, pu[:, 0:TT]), V(ub_, ut_[:, kc, :]), V(hT_b, hT[:, kc, :]), start=(kc == 0), stop=(kc == KD - 1))
                act(V(sg_b, sg[:]), V(pgb, pg[:, 0:TT]), AF.Silu)
                tt(V(aT_b, aT[:, jj, :]), V(pub, pu[:, 0:TT]), V(sg_b, sg[:]), ALU.mult)
            for m in range(KD):
                dt_, db_ = wload(wdp, wd_i, wdv[:, hf * KFH:(hf + 1) * KFH, m * 128:(m + 1) * 128], KFH)
                pd, pdb = PS()
                for kf in range(KFH):
                    mm(V(pdb, pd[:, 0:TT]), V(db_, dt_[:, kf, :]), V(aT_b, aT[:, kf, :]), start=(kf == 0), stop=(kf == KFH - 1))
                stt(V(xT_b, xT[:, m, :]), V(pdb, pd[:, 0:TT]), V(Gcoef_b, Gcoef[:, i, m:m + 1]), V(xT_b, xT[:, m, :]),
                    ALU.mult, ALU.add)

    glu, glu_b = sb("glu", [128, 30 + TT])
    assert KC * TT <= 2 * D
    ypre = xin_all[:].rearrange("p a d -> p (a d)")[:, 0:KC * TT].rearrange("p (k t) -> p k t", k=KC)
    ypre_b = xin_all_b
    ysq2 = [sb("ysq%d" % i, [128, TT]) for i in range(2)]
    pre, pre_b = glu, glu_b
    qkv = [sb("qkv%d" % i, [128, TT]) for i in range(3)]
    zs, zs_b = sb("zs", [128, TT])
    abT, abT_b = sb("abT", [64, TT])
    wab, wab_b = sb("wab", [128, KD, 32], BF16)
    RB = {n: sb("RB_" + n, [128, TT]) for n in ("lnb", "beta", "G", "lsk", "lsq", "R", "P", "Q", "Kd")}
    RB["t"] = RB["lsq"]
    RB["E"] = RB["lsq"]
    RB["g"] = RB["Kd"]
    RB["nEP"] = RB["lnb"]
    RB["nR"] = RB["lsk"]
    stk = [sb("stk%d" % i, [128, TT]) for i in range(2)]
    cols = [sb("cols%d" % i, [128, 128]) for i in range(2)]
    ktok, ktok_b = sb("ktok", [128, 128])
    vtok, vtok_b = sb("vtok", [128, 128])
    kd_t, kd_b = sb("kd_t", [128, 128])
    mA, mA_b = sb("mA", [128, 128])
    mB, mB_b = sb("mB", [128, 128])
    NAt, NA_b = sb("NAt", [128, 256])
    NBt, NB_b = sb("NBt", [128, 256])
    tA, tA_b = sb("tA", [128, 128])
    tB, tB_b = sb("tB", [128, 128])
    tY, tY_b = sb("tY", [128, 256])
    mean_t, mean_b = RB["R"]
    rs2, rs2_b = RB["P"]
    aTt, aTt_b = sb("aTt", [128, 128])
    qeT, qeT_b = sb("qeT", [128, 128])
    TbT, TbT_b = sb("TbT", [128, 128])
    TwT, TwT_b = sb("TwT", [128, 128])
    u_t, u_b = sb("u_t", [128, 128])
    nwT, nwT_b = sb("nwT", [128, 128])
    vnew, vnew_b = sb("vnew", [128, 128])
    on_t, on_b = TbT, TbT_b
    sml, sml_b = sb("sml", [128, 8])
    for i in range(2):
        S.op("dve", lambda e, i=i: e.memset(stk[i][0][:], 0.0), writes=[V(stk[i][1], stk[i][0][:])])
    S.op("dve", lambda e: e.memset(abT[:], 0.0), writes=[V(abT_b, abT[:])])
    selm, selm_b = sb("selm", [64, 2, 128])

    def build_sel(h):
        for r in range(2):
            rr = h if r == 0 else 32 + h
            ts(V(selm_b, selm[:, r, :]), V(cst_b, cst[0:64, 1, :]), V(cst_b, cst[0:64, 0, rr:rr + 1]), None, ALU.mult)

    win_v = win_d.rearrange("(k p) n -> p k n", p=128)
    wout_v = wout_d.rearrange("(k p) n -> p k n", p=128)
    QB = 2 * CW

    def proj(col0):
        wt, wb = wload(wa, wa_i, win_v[:, :, col0:col0 + 128], KD)
        p, pb = PS()
        for kc in range(KD):
            mm(V(pb, p[:, 0:TT]), V(wb, wt[:, kc, :]), V(hT_b, hT[:, kc, :]), start=(kc == 0), stop=(kc == KD - 1))
        return V(pb, p[:, 0:TT])

    def rb(n, sl=None):
        t, b = RB[n]
        return V(b, t[:] if sl is None else t[:, sl])

    def mixer():
        for j in range(KC):
            pa = proj(j * 128)
            pgt = proj(CW + j * 128)
            act(V(sg_b, sg[:]), pgt, AF.Sigmoid)
            cp(V(glu_b, glu[:, 0:30]), V(gtail[j][1], gtail[j][0][:]))
            tt(V(glu_b, glu[:, 30:30 + TT]), pa, V(sg_b, sg[:]), ALU.mult)
            cp(V(gtail[j][1], gtail[j][0][:]), V(glu_b, glu[:, TT:TT + 30]))
            for k in range(31):
                wk = V(wdwT_b, wdwT[:, j, k:k + 1])
                if k == 0:
                    ts(V(cacc_b, cacc[:]), V(glu_b, glu[:, 0:TT]), wk, V(cvec_b, cvec[:, 0, j:j + 1]), ALU.mult, ALU.add)
                elif k < 30:
                    stt(V(cacc_b, cacc[:]), V(glu_b, glu[:, k:k + TT]), wk, V(cacc_b, cacc[:]), ALU.mult, ALU.add)
                else:
                    stt(V(ypre_b, ypre[:, j, :]), V(glu_b, glu[:, k:k + TT]), wk, V(cacc_b, cacc[:]), ALU.mult, ALU.add)
        pmn, pmn_b = PS()
        pvr, pvr_b = PS()
        for j in range(KC):
            mm(V(pmn_b, pmn[:, 0:TT]), ONES, V(ypre_b, ypre[:, j, :]), start=(j == 0), stop=(j == KC - 1))
        for j in range(KC):
            yq, yqb = ysq2[j % 2]
            act(V(yqb, yq[:]), V(ypre_b, ypre[:, j, :]), AF.Square)
            mm(V(pvr_b, pvr[:, 0:TT]), ONES, V(yqb, yq[:]), start=(j == 0), stop=(j == KC - 1), sig=True)
        ts(V(mean_b, mean_t[:]), V(pmn_b, pmn[:, 0:TT]), 1.0 / CW, None, ALU.mult)
        tt(V(rs2_b, rs2[:]), V(mean_b, mean_t[:]), V(mean_b, mean_t[:]), ALU.mult)
        stt(V(rs2_b, rs2[:]), V(pvr_b, pvr[:, 0:TT]), 1.0 / CW, V(rs2_b, rs2[:]), ALU.mult, ALU.subtract)
        act(V(rs2_b, rs2[:]), V(rs2_b, rs2[:]), AF.Sqrt, bias=EPS, scale=1.0)
        S.op("dve", lambda e: e.reciprocal(rs2[:], rs2[:]), reads=[V(rs2_b, rs2[:])], writes=[V(rs2_b, rs2[:])])
        for j in range(KC):
            tt(V(cacc_b, cacc[:]), V(ypre_b, ypre[:, j, :]), V(mean_b, mean_t[:]), ALU.subtract)
            tt(V(cacc_b, cacc[:]), V(cacc_b, cacc[:]), V(rs2_b, rs2[:]), ALU.mult)
            act(V(yT_b, yT[:, j, :]), V(cacc_b, cacc[:]), AF.Silu, bias=V(cvec_b, cvec[:, 2, j:j + 1]),
                scale=V(cvec_b, cvec[:, 1, j:j + 1]))

        if CUT <= 1:
            return
        S.dma("pool", wab[:, :, 0:NH], win_v[:, :, QB + 4 * DNW:QB + 4 * DNW + NH], reads=[], writes=[V(wab_b, wab[:])], key=wab_b, slow=True)
        S.dma("pool", wab[:, :, 16:16 + NH], win_v[:, :, QB + 4 * DNW + NH:QB + 4 * DNW + 2 * NH], reads=[], writes=[V(wab_b, wab[:])], key=wab_b, slow=True)
        pab, pab_b = PS()
        for kc in range(KD):
            mm(V(pab_b, pab[0:NH, 0:TT]), V(wab_b, wab[:, kc, 0:NH]), V(hT_b, hT[:, kc, :]), start=(kc == 0), stop=(kc == KD - 1))
        cp(V(abT_b, abT[0:NH, :]), V(pab_b, pab[0:NH, 0:TT]))
        pab2, pab2_b = PS()
        for kc in range(KD):
            mm(V(pab2_b, pab2[32:32 + NH, 0:TT]), V(wab_b, wab[:, kc, 16:16 + NH]), V(hT_b, hT[:, kc, :]), start=(kc == 0), stop=(kc == KD - 1))
        cp(V(abT_b, abT[32:32 + NH, :]), V(pab2_b, pab2[32:32 + NH, 0:TT]))

        if CUT <= 2:
            return
        for h in range(NH):
            for i3 in range(3):
                pp = proj(QB + i3 * DNW + h * 128)
                ci = i3 * NH + h
                cp(V(pre_b, pre[:, 0:3]), V(ptail[ci][1], ptail[ci][0][:]))
                cp(V(pre_b, pre[:, 3:3 + TT]), pp, eng="act")
                cp(V(ptail[ci][1], ptail[ci][0][:]), V(pre_b, pre[:, TT:TT + 3]))
                for k in range(4):
                    wk = V(wshT_b, wshT[:, ci, k:k + 1])
                    if k == 0:
                        ts(V(cacc_b, cacc[:]), V(pre_b, pre[:, 0:TT]), wk, None, ALU.mult)
                    else:
                        stt(V(cacc_b, cacc[:]), V(pre_b, pre[:, k:k + TT]), wk, V(cacc_b, cacc[:]), ALU.mult, ALU.add)
                act(V(qkv[i3][1], qkv[i3][0][:]), V(cacc_b, cacc[:]), AF.Silu)
            pz = proj(QB + 3 * DNW + h * 128)
            act(V(zs_b, zs[:]), pz, AF.Silu)
            qT, kT, vT = [V(qkv[i][1], qkv[i][0][:]) for i in range(3)]
            if CUT <= 3:
                continue
            pb1, pb1_b = PS()
            build_sel(h)
            mm(V(pb1_b, pb1[:, 0:TT]), V(selm_b, selm[:, 0, :]), V(abT_b, abT[:]))
            pa1, pa1_b = PS()
            mm(V(pa1_b, pa1[:, 0:TT]), V(selm_b, selm[:, 1, :]), V(abT_b, abT[:]))
            act(rb("t"), V(pb1_b, pb1[:, 0:TT]), AF.Exp, scale=-1.0)
            act(rb("lnb"), rb("t"), AF.Ln, bias=1.0)
            ts(rb("lnb"), rb("lnb"), -1.0, None, ALU.mult)
            act(rb("beta"), rb("lnb"), AF.Exp)
            act(rb("t"), V(pa1_b, pa1[:, 0:TT]), AF.Exp, bias=V(hvec_b, hvec[:, 1, h:h + 1]))
            act(rb("g"), rb("t"), AF.Ln, bias=1.0)
            ts(rb("g"), rb("g"), V(negA_b, negA[:, h:h + 1]), None, ALU.mult)
            S.op("dve", lambda e: e.tensor_tensor_scan(RB["G"][0][:], rmask[:], RB["g"][0][:], 0.0, ALU.mult, ALU.add),
                 reads=[V(rmask_b, rmask[:]), rb("g")], writes=[rb("G")])
            if CUT <= 4:
                continue
            for (src, dst) in ((kT, "lsk"), (qT, "lsq")):
                act(V(cacc_b, cacc[:]), src, AF.Square)
                pss, pss_b = PS()
                mm(V(pss_b, pss[:, 0:TT]), ONES, V(cacc_b, cacc[:]))
                act(rb(dst), V(pss_b, pss[:, 0:TT]), AF.Ln, bias=EPS)
            stt(rb("R"), rb("lsk"), 0.5, rb("G"), ALU.mult, ALU.add)
            stt(rb("P"), rb("lsk"), -0.5, rb("G"), ALU.mult, ALU.add)
            tt(rb("P"), rb("P"), rb("lnb"), ALU.add)
            stt(rb("Q"), rb("lsq"), -0.5, rb("G"), ALU.mult, ALU.add)
            ts(rb("Q"), rb("Q"), float(np.log(128.0 ** -0.5)), None, ALU.add)
            act(rb("E"), rb("Q"), AF.Exp)
            act(rb("nEP"), rb("P"), AF.Exp)
            ts(rb("nEP"), rb("nEP"), -1.0, None, ALU.mult)
            ts(rb("nR"), rb("R"), -1.0, None, ALU.mult)
            for blk in range(NB):
                bs = slice(blk * 128, (blk + 1) * 128)
                last = blk * 128 + 127
                glast = V(RB["G"][1], RB["G"][0][:, last:last + 1])
                act(rb("Kd", bs), rb("nR", bs), AF.Exp, bias=glast)
            if CUT <= 5:
                continue
            for (pi, nme) in ((0, "nR"), (32, "P"), (64, "beta"), (96, "nEP")):
                cp(V(stk[0][1], stk[0][0][pi:pi + 1, :]), V(RB[nme][1], RB[nme][0][pi:pi + 1, :]))
            cp(V(stk[1][1], stk[1][0][0:1, :]), V(RB["Kd"][1], RB["Kd"][0][0:1, :]))

            if CUT <= 6:
                continue
            Sb, Sbb = Sst[h]
            SV = V(Sbb, Sb[:])
            for blk in range(NB):
                bs = slice(blk * 128, (blk + 1) * 128)
                last = blk * 128 + 127
                for i in range(2):
                    pc, pcb = PS()
                    tr(V(pcb, pc[:, 0:128]), V(stk[i][1], stk[i][0][:, bs]))
                    cp(V(cols[i][1], cols[i][0][:]), V(pcb, pc[:, 0:128]), eng="act")
                c0, c0b = cols[0]
                cnR = V(c0b, c0[:, 0:1])
                cP = V(c0b, c0[:, 32:33])
                cbeta = V(c0b, c0[:, 64:65])
                cnEP = V(c0b, c0[:, 96:97])
                cKd = V(cols[1][1], cols[1][0][:, 0:1])
                qTb = V(qkv[0][1], qkv[0][0][:, bs])
                kTb = V(qkv[1][1], qkv[1][0][:, bs])
                vTb = V(qkv[2][1], qkv[2][0][:, bs])
                ptk, ptk_b = PS()
                tr(V(ptk_b, ptk[:, 0:128]), kTb)
                tr(V(ptk_b, ptk[:, 128:256]), vTb)
                cp(V(ktok_b, ktok[:]), V(ptk_b, ptk[:, 0:128]), eng="act")
                cp(V(vtok_b, vtok[:]), V(ptk_b, ptk[:, 128:256]), eng="act")
                ts(V(kd_b, kd_t[:]), V(ptk_b, ptk[:, 0:128]), cKd, None, ALU.mult)
                if CUT <= 7:
                    continue
                pkk, pkk_b = PS()
                mm(V(pkk_b, pkk[:, 0:128]), kTb, kTb)
                mm(V(pkk_b, pkk[:, 128:256]), kTb, qTb)
                tt(V(mA_b, mA[:]), NEGS, rb("R", bs), ALU.subtract)
                act(V(mA_b, mA[:]), V(mA_b, mA[:]), AF.Exp, bias=cP)
                tt(V(mB_b, mB[:]), NEGST, rb("P", bs), ALU.add)
                act(V(mB_b, mB[:]), V(mB_b, mB[:]), AF.Exp, bias=cnR)
                tt(V(aTt_b, aTt[:]), NEGTT, rb("Q", bs), ALU.add)
                act(V(aTt_b, aTt[:]), V(aTt_b, aTt[:]), AF.Exp, bias=cnR)
                stt(V(mA_b, mA[:]), V(pkk_b, pkk[:, 0:128]), -1.0, V(mA_b, mA[:]), ALU.mult, ALU.mult)
                stt(V(mB_b, mB[:]), V(pkk_b, pkk[:, 0:128]), -1.0, V(mB_b, mB[:]), ALU.mult, ALU.mult)
                tt(V(aTt_b, aTt[:]), V(pkk_b, pkk[:, 128:256]), V(aTt_b, aTt[:]), ALU.mult)
                tt(V(qeT_b, qeT[:]), qTb, rb("E", bs), ALU.mult)
                if CUT <= 8:
                    continue
                XTv = V(NA_b, NAt[:, 128:256])
                XUv = V(NB_b, NBt[:, 128:256])
                tt(V(NA_b, NAt[:, 0:128]), V(mA_b, mA[:]), BD16, ALU.mult)
                tt(V(NB_b, NBt[:, 0:128]), V(mB_b, mB[:]), BD16, ALU.mult)
                cp(XTv, IDN)
                cp(XUv, IDN)
                for lev in range(4):
                    p1, p1b = PS()
                    p2, p2b = PS()
                    if lev == 3:
                        mm(V(p1b, p1[:, 128:256]), V(NA_b, NAt[:, 0:128]), XUv)
                        mm(V(p2b, p2[:, 128:256]), V(NB_b, NBt[:, 0:128]), XTv)
                    else:
                        mm(V(p1b, p1[:, 0:256]), V(NA_b, NAt[:, 0:128]), V(NB_b, NBt[:, 0:256]))
                        mm(V(p2b, p2[:, 0:256]), V(NB_b, NBt[:, 0:128]), V(NA_b, NAt[:, 0:256]))
                        cp(V(NB_b, NBt[:, 0:128]), V(p1b, p1[:, 0:128]), eng="act")
                        cp(V(NA_b, NAt[:, 0:128]), V(p2b, p2[:, 0:128]), eng="act")
                    tt(XUv, V(p1b, p1[:, 128:256]), XUv, ALU.add)
                    tt(XTv, V(p2b, p2[:, 128:256]), XTv, ALU.add)
                for li in range(3):
                    lastm = (li == 2)
                    tt(V(tA_b, tA[:]), V(mA_b, mA[:]), LLm[li], ALU.mult)
                    pY, pYb = PS()
                    mm(V(pYb, pY[:, 0:128]), V(tA_b, tA[:]), XUv)
                    if not lastm:
                        tt(V(tB_b, tB[:]), V(mB_b, mB[:]), URm[li], ALU.mult)
                        mm(V(pYb, pY[:, 128:256]), V(tB_b, tB[:]), XTv)
                        cp(V(tY_b, tY[:, 0:256]), V(pYb, pY[:, 0:256]), eng="act")
                    else:
                        cp(V(tY_b, tY[:, 0:128]), V(pYb, pY[:, 0:128]), eng="act")
                    pZ, pZb = PS()
                    mm(V(pZb, pZ[:, 0:128]), XTv, V(tY_b, tY[:, 0:128]))
                    if not lastm:
                        mm(V(pZb, pZ[:, 128:256]), XUv, V(tY_b, tY[:, 128:256]))
                    tt(XUv, V(pZb, pZ[:, 0:128]), XUv, ALU.add)
                    if not lastm:
                        tt(XTv, V(pZb, pZ[:, 128:256]), XTv, ALU.add)
                XT = XUv
                ts(V(TbT_b, TbT[:]), XT, cbeta, None, ALU.mult)
                ts(V(TwT_b, TwT[:]), XT, cnEP, None, ALU.mult)
                pu_, pu_b = PS()
                mm(V(pu_b, pu_[:, 0:128]), V(TbT_b, TbT[:]), V(vtok_b, vtok[:]))
                mm(V(pu_b, pu_[:, 128:256]), V(ktok_b, ktok[:]), V(TwT_b, TwT[:]))
                cp(V(u_b, u_t[:]), V(pu_b, pu_[:, 0:128]), eng="act")
                cp(V(nwT_b, nwT[:]), V(pu_b, pu_[:, 128:256]), eng="act")
                if CUT <= 9:
                    continue
                pv, pvb = PS()
                mm(V(pvb, pv[:, 0:128]), V(nwT_b, nwT[:]), SV)
                tt(V(vnew_b, vnew[:]), V(pvb, pv[:, 0:128]), V(u_b, u_t[:]), ALU.add)
                po, pob = PS()
                mm(V(pob, po[:, 0:128]), V(qeT_b, qeT[:]), SV, start=True, stop=False)
                mm(V(pob, po[:, 0:128]), V(aTt_b, aTt[:]), V(vnew_b, vnew[:]), start=False, stop=True)
                pds, pdsb = PS()
                mm(V(pdsb, pds[:, 0:128]), V(kd_b, kd_t[:]), V(vnew_b, vnew[:]))
                act(V(sml_b, sml[:, 0:1]), V(RB["G"][1], RB["G"][0][:, last:last + 1]), AF.Exp)
                stt(SV, SV, V(sml_b, sml[:, 0:1]), V(pdsb, pds[:, 0:128]), ALU.mult, ALU.add)
                if CUT <= 10:
                    continue
                act(V(on_b, on_t[:]), V(pob, po[:, 0:128]), AF.Square, accum=V(sml_b, sml[:, 1:2]))
                act(V(sml_b, sml[:, 2:3]), V(sml_b, sml[:, 1:2]), AF.Sqrt, bias=EPS, scale=1.0 / 128)
                S.op("dve", lambda e: e.reciprocal(sml[:, 3:4], sml[:, 2:3]), reads=[V(sml_b, sml[:, 2:3])], writes=[V(sml_b, sml[:, 3:4])])
                ts(V(on_b, on_t[:]), V(pob, po[:, 0:128]), V(sml_b, sml[:, 3:4]), None, ALU.mult)
                pt2, pt2b = PS()
                tr(V(pt2b, pt2[:, 0:128]), V(on_b, on_t[:]))
                stt(V(yT_b, yT[:, KC + h, bs]), V(pt2b, pt2[:, 0:128]), V(onw_b, onw[:, 0:1]), V(zs_b, zs[:, bs]), ALU.mult, ALU.mult)

        if CUT <= 11:
            return
        for m in range(KD):
            wt, wb = wload(wa, wa_i, wout_v[:, :, m * 128:(m + 1) * 128], KY)
            p, pb = PS()
            for kc in range(KY):
                mm(V(pb, p[:, 0:TT]), V(wb, wt[:, kc, :]), V(yT_b, yT[:, kc, :]), start=(kc == 0), stop=(kc == KY - 1))
            stt(V(xT_b, xT[:, m, :]), V(pb, p[:, 0:TT]), V(Gcoef_b, Gcoef[:, 1, m:m + 1]), V(xT_b, xT[:, m, :]), ALU.mult, ALU.add)

    out_bufs = []
    for t in range(NT):
        for tb in range(NB):
            xt, xb = xin[tb % 2]
            r0 = t * TT + tb * 128
            ld(V(xb, xt[:]), x_d[r0:r0 + 128, :])
            for k0 in range(0, KD, 4):
                p, pb = PS()
                nk = min(4, KD - k0)
                for kk in range(nk):
                    tr(V(pb, p[:, kk * 128:(kk + 1) * 128]), V(xb, xt[:, (k0 + kk) * 128:(k0 + kk + 1) * 128]))
                S.op("dve", lambda e, p=p, k0=k0, nk=nk, tb=tb: e.tensor_copy(
                    xT[:, k0:k0 + nk, tb * 128:(tb + 1) * 128], p[:, 0:nk * 128].rearrange("p (k t) -> p k t", k=nk)),
                    reads=[V(pb, p[:])], writes=[V(xT_b, xT[:])])
        rms_mod(0)
        ffn(0, 0)
        if stop != "ffn1":
            rms_mod(1)
            mixer()
            if stop != "mixer":
                rms_mod(2)
                ffn(1, 2)
        if stop is None:
            rms_rstd()
            for kc in range(KD):
                tt(V(tmpn_b, tmpn[:]), V(xT_b, xT[:, kc, :]), V(rstd_b, rstd[:]), ALU.mult)
                ts(V(xT_b, xT[:, kc, :]), V(tmpn_b, tmpn[:]), V(normT_b, normT[:, 3, kc:kc + 1]), None, ALU.mult)
        for tb in range(NB):
            xt, xb = xin[tb % 2]
            r0 = t * TT + tb * 128
            for k0 in range(0, KD, 4):
                p, pb = PS()
                nk = min(4, KD - k0)
                for kk in range(nk):
                    tr(V(pb, p[:, kk * 128:(kk + 1) * 128]), V(xT_b, xT[:, k0 + kk, tb * 128:(tb + 1) * 128]))
                cp(V(xb, xt[:, k0 * 128:(k0 + nk) * 128]), V(pb, p[:, 0:nk * 128]), eng="act")
            S.dma("sp", y_d[r0:r0 + 128, :], xt[:], reads=[V(xb, xt[:])], writes=[], key=xb)
            if xb not in out_bufs:
                out_bufs.append(xb)
    S.emit(final_waits=out_bufs)
    es.close()
    return nc


def host_inputs(cfg, b, x, c, w_ada, b_ada, ffn1_norm, ffn1_wg, ffn1_wu, ffn1_wd, mix_norm, w_in,
                w_dw, b_dw, conv_ln_w, conv_ln_b, w_short, a_log, dt_bias, dn_norm_w, w_out,
                ffn2_norm, ffn2_wg, ffn2_wu, ffn2_wd, final_norm):
    KD, KC, NH, TT = cfg["KD"], cfg["KC"], cfg["NH"], cfg["TT"]
    f = np.float32
    A = np.ascontiguousarray

    def fm(v, k):
        return A(np.asarray(v, f).reshape(k, 128).T)

    consts = np.zeros((128, 12, 128), f)
    idx = np.arange(128)
    consts[:, 0, :] = np.eye(128, dtype=f)
    consts[:, 1, :] = 1.0
    consts[:, 2, :] = np.where(idx[:, None] > idx[None, :], 0.0, NEG)
    consts[:, 3, :] = np.where(idx[None, :] > idx[:, None], 0.0, NEG)
    consts[:, 4, :] = np.where(idx[None, :] >= idx[:, None], 0.0, NEG)
    consts[:, 5, :] = (idx[:, None] // 16 == idx[None, :] // 16)
    for i, bsz in enumerate((16, 32, 64)):
        ll = ((idx[:, None] // (2 * bsz) == idx[None, :] // (2 * bsz)) & (idx[:, None] % (2 * bsz) >= bsz)
              & (idx[None, :] % (2 * bsz) < bsz)).astype(f)
        consts[:, 6 + i, :] = ll
        consts[:, 9 + i, :] = ll.T
    rmask = np.ones((128, TT), f)
    rmask[:, ::128] = 0.0
    return {
        "x": A(np.asarray(x[b], f)),
        "cT": fm(c[b], KD),
        "w_ada": A(np.asarray(w_ada[0], f)),
        "b_adaT": fm(b_ada[0], 9 * KD),
        "normT": A(np.stack([fm(ffn1_norm[0], KD), fm(mix_norm[0], KD), fm(ffn2_norm[0], KD), fm(final_norm, KD)], axis=1)),
        "ffn1_wg": A(np.asarray(ffn1_wg[0], f)), "ffn1_wu": A(np.asarray(ffn1_wu[0], f)), "ffn1_wd": A(np.asarray(ffn1_wd[0], f)),
        "ffn2_wg": A(np.asarray(ffn2_wg[0], f)), "ffn2_wu": A(np.asarray(ffn2_wu[0], f)), "ffn2_wd": A(np.asarray(ffn2_wd[0], f)),
        "w_in": A(np.asarray(w_in[0], f)),
        "w_out": A(np.asarray(w_out[0], f)),
        "w_dwT": A(np.asarray(w_dw[0], f).T.reshape(KC, 128, 31).transpose(1, 0, 2)),
        "cvecT": A(np.stack([fm(b_dw[0], KC), fm(conv_ln_w[0], KC), fm(conv_ln_b[0], KC)], axis=1)),
        "w_shT": A(np.asarray(w_short[0], f).T.reshape(3 * NH, 128, 4).transpose(1, 0, 2)),
        "hvec": A(np.broadcast_to(np.stack([np.asarray(a_log[0], f), np.asarray(dt_bias[0], f)], axis=0)[None], (128, 2, NH))),
        "onwT": A(np.asarray(dn_norm_w[0], f).reshape(128, 1)),
        "consts": consts,
        "rmask": rmask,
    }


_NC_CACHE = {}


def kernel(**inputs):
    cfg = FULL
    B = inputs["x"].shape[0]
    if "nc" not in _NC_CACHE:
        _NC_CACHE["nc"] = build(cfg)
    nc = _NC_CACHE["nc"]
    n = 8
    in_maps = [host_inputs(cfg, i % B, **inputs) for i in range(n)]
    res = run_bass_kernel_spmd(nc, in_maps, core_ids=list(range(n)))
    out = np.stack([res.results[i]["y"] for i in range(B)], axis=0)
    return out.astype(np.float32)
```

```python
import numpy as np
import os
CUT = int(os.environ.get('KCUT', '99'))
from contextlib import ExitStack
import concourse.bass as bass
import concourse.mybir as mybir
from concourse.bass_utils import run_bass_kernel_spmd

F32 = mybir.dt.float32
BF16 = mybir.dt.bfloat16
AF = mybir.ActivationFunctionType
ALU = mybir.AluOpType
NEG = -1.0e9
EPS = 1e-6


class Buf:
    __slots__ = ("name", "w", "r", "dsem", "psum")

    def __init__(self, name, psum=False):
        self.name = name
        self.psum = psum
        self.w = None
        self.r = []
        self.dsem = None


class V:
    __slots__ = ("b", "ap")

    def __init__(self, b, ap):
        self.b = b
        self.ap = ap


class DSem:
    def __init__(self, sem):
        self.sem = sem
        self.count = 0


class Rec:
    __slots__ = ("eng", "fn", "deps", "sig", "idx", "dma", "dval")

    def __init__(self, eng, fn, deps, sig, dma=None):
        self.eng = eng
        self.fn = fn
        self.deps = deps
        self.sig = sig
        self.dma = dma
        self.dval = 0
        self.idx = -1


class Sched:
    ENGS = ("pe", "dve", "act", "pool", "sp")

    def __init__(self, nc, es):
        self.nc = nc
        self.es = es
        self.q = {e: [] for e in self.ENGS}
        self.esem = {e: es.enter_context(nc.semaphore("s_" + e)) for e in ("pe", "dve", "act", "pool")}
        self.nsem = 0

    def _deps(self, eng, reads, writes, is_dma):
        deps = []
        for v in reads:
            w = v.b.w
            if w is not None:
                deps.append((w, "raw"))
            if v.b.psum:
                for r in v.b.r:
                    if r.eng != eng:
                        deps.append((r, "rr"))
        for v in writes:
            b = v.b
            if b.w is not None:
                deps.append((b.w, "waw"))
            for r in b.r:
                deps.append((r, "war"))
        out = []
        for (d, kind) in deps:
            if d.dma is None and not is_dma and d.eng == eng:
                if eng == "pe":
                    continue
            if d.dma is not None:
                out.append((d, d.dma.count))
            else:
                out.append((d, None))
        return out

    def _commit(self, rec, reads, writes):
        for v in writes:
            v.b.w = rec
            v.b.r = []
        for v in reads:
            if v.b.w is not rec:
                v.b.r.append(rec)

    def op(self, eng, fn, reads=(), writes=(), sig=True):
        rec = Rec(eng, fn, self._deps(eng, reads, writes, False), sig)
        rec.idx = len(self.q[eng])
        self.q[eng].append(rec)
        self._commit(rec, reads, writes)
        return rec

    def dma(self, queue, out, in_, reads, writes, key, slow=False):
        if key.dsem is None:
            key.dsem = DSem(self.es.enter_context(self.nc.semaphore("d%d" % self.nsem)))
            self.nsem += 1
        ds = key.dsem
        rec = Rec(queue, lambda e: e.dma_start(out=out, in_=in_, allow_slow_non_contiguous=slow), self._deps(queue, reads, writes, True), True, dma=ds)
        ds.count += 16
        rec.dval = ds.count
        rec.idx = len(self.q[queue])
        self.q[queue].append(rec)
        self._commit(rec, reads, writes)
        return rec

    def emit(self, final_waits=()):
        nc = self.nc
        cnt = {}
        for e in ("pe", "dve", "act", "pool"):
            arr = []
            c = 0
            for r in self.q[e]:
                if r.dma is None and r.sig:
                    c += 1
                arr.append(c)
            res = [0] * len(arr)
            nxt = None
            for i in range(len(arr) - 1, -1, -1):
                r = self.q[e][i]
                if r.dma is None and r.sig:
                    nxt = arr[i]
                res[i] = nxt
            cnt[e] = res
        handles = {"pe": nc.tensor, "dve": nc.vector, "act": nc.scalar, "pool": nc.gpsimd, "sp": nc.sync}
        with nc.Block() as block:
            def run(e, h):
                waited = {}
                for r in self.q[e]:
                    for (d, dv) in r.deps:
                        if d.dma is not None:
                            sem, val = d.dma.sem, dv
                        else:
                            val = cnt[d.eng][d.idx]
                            assert val is not None, "dep on op with no later signal"
                            sem = self.esem[d.eng]
                        k = id(sem)
                        if waited.get(k, 0) >= val:
                            continue
                        waited[k] = val
                        h.wait_ge(sem, val)
                    ins = r.fn(h)
                    if r.dma is not None:
                        ins.then_inc(r.dma.sem, 16)
                    elif r.sig:
                        ins.then_inc(self.esem[e], 1)
                if e == "sp":
                    for b in final_waits:
                        h.wait_ge(b.dsem.sem, b.dsem.count)

            @block.tensor
            def _(h):
                run("pe", h)

            @block.vector
            def _(h):
                run("dve", h)

            @block.scalar
            def _(h):
                run("act", h)

            @block.gpsimd
            def _(h):
                run("pool", h)

            @block.sync
            def _(h):
                run("sp", h)


def make_cfg(D, DFF, CW, NH, T):
    return dict(D=D, DFF=DFF, CW=CW, NH=NH, T=T, KD=D // 128, KF=DFF // 128, KC=CW // 128,
                TT=min(512, T), INC=2 * CW + 4 * NH * 128 + 2 * NH)


FULL = make_cfg(2048, 5632, 1024, 8, 4096)


def build(cfg, stop=None):
    D, DFF, CW, NH, T = cfg["D"], cfg["DFF"], cfg["CW"], cfg["NH"], cfg["T"]
    KD, KF, KC, TT, INC = cfg["KD"], cfg["KF"], cfg["KC"], cfg["TT"], cfg["INC"]
    NT = T // TT
    NB = TT // 128
    KY = KC + NH
    DNW = NH * 128
    nc = bass.Bass("TRN2", target_bir_lowering=False)
    es = ExitStack()
    S = Sched(nc, es)

    def din(name, shape, dt=F32):
        return nc.dram_tensor(name, list(shape), dt, kind="ExternalInput").ap()

    x_d = din("x", [T, D])
    y_d = nc.dram_tensor("y", [T, D], F32, kind="ExternalOutput").ap()
    cT_d = din("cT", [128, KD])
    wada_d = din("w_ada", [D, 9 * D])
    badaT_d = din("b_adaT", [128, 9 * KD])
    normT_d = din("normT", [128, 4, KD])
    wg_d = [din("ffn1_wg", [D, DFF]), din("ffn2_wg", [D, DFF])]
    wu_d = [din("ffn1_wu", [D, DFF]), din("ffn2_wu", [D, DFF])]
    wd_d = [din("ffn1_wd", [DFF, D]), din("ffn2_wd", [DFF, D])]
    win_d = din("w_in", [D, INC])
    wout_d = din("w_out", [D, D])
    wdwT_d = din("w_dwT", [128, KC, 31])
    cvec_d = din("cvecT", [128, 3, KC])
    wshT_d = din("w_shT", [128, 3 * NH, 4])
    hvec_d = din("hvec", [128, 2, NH])
    onw_d = din("onwT", [128, 1])
    cst_d = din("consts", [128, 12, 128])
    rmask_d = din("rmask", [128, TT])

    SBTOT = [0]

    def sb(name, shape, dt=F32):
        t = es.enter_context(nc.sbuf_tensor("sb_" + name, list(shape), dt))
        nbytes = int(np.prod(shape[1:])) * (2 if dt == BF16 else 4)
        SBTOT[0] += nbytes
        if os.environ.get("KSB"):
            print("SB", name, nbytes, SBTOT[0])
        return t, Buf(name)

    cst, cst_b = sb("cst", [128, 12, 128])
    rmask, rmask_b = sb("rmask", [128, TT])
    ones_bf, ones_bf_b = sb("ones_bf", [128, 128], BF16)
    normT, normT_b = sb("normT", [128, 4, KD])
    wdwT, wdwT_b = sb("wdwT", [128, KC, 31])
    cvec, cvec_b = sb("cvec", [128, 3, KC])
    wshT, wshT_b = sb("wshT", [128, 3 * NH, 4])
    hvec, hvec_b = sb("hvec", [128, 2, NH])
    negA, negA_b = sb("negA", [128, NH])
    onw, onw_b = sb("onw", [128, 1])
    modsT, modsT_b = sb("modsT", [128, 9 * KD])
    Acoef, Acoef_b = sb("Acoef", [128, 3, KD])
    Gcoef, Gcoef_b = sb("Gcoef", [128, 3, KD])
    finw_b = normT_b
    xT, xT_b = sb("xT", [128, KD, TT])
    hT, hT_b = sb("hT", [128, KD, TT], BF16)
    KFH = KF // 2
    actT = [sb("actT%d" % i, [128, max(KFH, KD), TT], BF16) for i in range(1)]
    yT, yT_b = sb("yT", [128, KY, TT], BF16)
    sq, sq_b = actT[0]
    rstd, rstd_b = sb("rstd", [128, TT])
    xin_all, xin_all_b = sb("xin_all", [128, 2, D])
    xin = [(xin_all[:, i, :], xin_all_b) for i in range(2)]
    Sst = [sb("Sst%d" % h, [128, 128]) for h in range(NH)]
    gtail = [sb("gtail%d" % j, [128, 30]) for j in range(KC)]
    ptail = [sb("ptail%d" % j, [128, 3]) for j in range(3 * NH)]
    NWA, NWD = 4, 2
    wa = [sb("wa%d" % i, [128, KD, 128], BF16) for i in range(NWA)]
    wdp = [sb("wd%d" % i, [128, KF // 2, 128], BF16) for i in range(NWD)]
    wa_i = [0]
    wd_i = [0]
    psum = []
    for i in range(8):
        t = es.enter_context(nc.psum_tensor("ps%d" % i, [128, 512], F32))
        psum.append((t, Buf("ps%d" % i, psum=True)))
    ps_i = [0]

    def PS():
        t, b = psum[ps_i[0] % 8]
        ps_i[0] += 1
        return t, b

    IDN = V(cst_b, cst[:, 0, :])
    ONES = V(cst_b, cst[:, 1, :])
    NEGS = V(cst_b, cst[:, 2, :])
    NEGST = V(cst_b, cst[:, 3, :])
    NEGTT = V(cst_b, cst[:, 4, :])
    BD16 = V(cst_b, cst[:, 5, :])
    LLm = [V(cst_b, cst[:, 6 + i, :]) for i in range(3)]
    URm = [V(cst_b, cst[:, 9 + i, :]) for i in range(3)]

    def mm(out, lhsT, rhs, start=True, stop=True, sig=None):
        if sig is None:
            sig = stop
        return S.op("pe", lambda e: e.matmul(out.ap, lhsT.ap, rhs.ap, start=start, stop=stop),
                    reads=[lhsT, rhs], writes=[out], sig=sig)

    def tr(out, in_):
        return S.op("pe", lambda e: e.transpose(out.ap, in_.ap, IDN.ap), reads=[in_, IDN], writes=[out])

    def act(out, in_, func, bias=None, scale=None, accum=None, eng="act"):
        rd = [in_]
        kw = {}
        if bias is not None:
            if isinstance(bias, V):
                rd.append(bias)
                kw["bias"] = bias.ap
            else:
                kw["bias"] = float(bias)
        if scale is not None:
            if isinstance(scale, V):
                rd.append(scale)
                kw["scale"] = scale.ap
            else:
                kw["scale"] = float(scale)
        wr = [out]
        if accum is not None:
            kw["accum_out"] = accum.ap
            wr.append(accum)
        return S.op("act", lambda e: e.activation(out.ap, in_.ap, func, **kw), reads=rd, writes=wr)

    def tt(out, a, b, op, eng="dve"):
        return S.op(eng, lambda e: e.tensor_tensor(out.ap, a.ap, b.ap, op), reads=[a, b], writes=[out])

    def ts(out, a, s1, s2, op0, op1=None, eng="dve"):
        rd = [a]
        s1a = s1.ap if isinstance(s1, V) else s1
        s2a = s2.ap if isinstance(s2, V) else s2
        if isinstance(s1, V):
            rd.append(s1)
        if isinstance(s2, V):
            rd.append(s2)
        if op1 is None:
            return S.op(eng, lambda e: e.tensor_scalar(out.ap, a.ap, s1a, None, op0), reads=rd, writes=[out])
        return S.op(eng, lambda e: e.tensor_scalar(out.ap, a.ap, s1a, s2a, op0, op1), reads=rd, writes=[out])

    def stt(out, a, s, b, op0, op1):
        rd = [a, b]
        sa = s.ap if isinstance(s, V) else s
        if isinstance(s, V):
            rd.append(s)
        return S.op("dve", lambda e: e.scalar_tensor_tensor(out.ap, a.ap, sa, b.ap, op0, op1), reads=rd, writes=[out])

    def cp(out, in_, eng="dve"):
        if eng == "act":
            return act(out, in_, AF.Copy)
        return S.op(eng, lambda e: e.tensor_copy(out.ap, in_.ap), reads=[in_], writes=[out])

    def ld(out, src_ap, queue="sp"):
        return S.dma(queue, out.ap, src_ap, reads=[], writes=[out], key=out.b)

    ld(V(cst_b, cst[:]), cst_d)
    ld(V(rmask_b, rmask[:]), rmask_d)
    ld(V(normT_b, normT[:]), normT_d)
    ld(V(wdwT_b, wdwT[:]), wdwT_d)
    ld(V(cvec_b, cvec[:]), cvec_d)
    ld(V(wshT_b, wshT[:]), wshT_d)
    ld(V(hvec_b, hvec[:]), hvec_d)
    ld(V(onw_b, onw[:]), onw_d)
    cp(V(ones_bf_b, ones_bf[:]), ONES)
    act(V(negA_b, negA[:]), V(hvec_b, hvec[:, 0, :]), AF.Exp)
    ts(V(negA_b, negA[:]), V(negA_b, negA[:]), -1.0, None, ALU.mult)
    for h in range(NH):
        S.op("dve", lambda e, h=h: e.memset(Sst[h][0][:], 0.0), writes=[V(Sst[h][1], Sst[h][0][:])])
    for j in range(KC):
        S.op("dve", lambda e, j=j: e.memset(gtail[j][0][:], 0.0), writes=[V(gtail[j][1], gtail[j][0][:])])
    for j in range(3 * NH):
        S.op("dve", lambda e, j=j: e.memset(ptail[j][0][:], 0.0), writes=[V(ptail[j][1], ptail[j][0][:])])

    scT, scT_b = sb("scT", [128, KD])
    badaT, badaT_b = sb("badaT", [128, 9 * KD])
    ld(V(scT_b, scT[:]), cT_d)
    ld(V(badaT_b, badaT[:]), badaT_d)
    act(V(scT_b, scT[:]), V(scT_b, scT[:]), AF.Silu)
    wada_v = wada_d.rearrange("(k p) n -> p k n", p=128)
    NMC = 9 * KD
    pm, pm_b = PS()
    for j in range(NMC):
        wt0, wb = xin[j % 2]
        wt = wt0[:].rearrange("p (k n) -> p k n", k=KD)
        S.dma("sp", wt, wada_v[:, :, j * 128:(j + 1) * 128], reads=[], writes=[V(wb, wt)], key=wb)
        for kc in range(KD):
            mm(V(pm_b, pm[:, j:j + 1]), V(wb, wt[:, kc, :]), V(scT_b, scT[:, kc:kc + 1]),
               start=(kc == 0), stop=(kc == KD - 1))
    tt(V(modsT_b, modsT[:]), V(pm_b, pm[:, 0:NMC]), V(badaT_b, badaT[:]), ALU.add)
    for i in range(3):
        sc = V(modsT_b, modsT[:, (3 * i + 1) * KD:(3 * i + 2) * KD])
        gt = V(modsT_b, modsT[:, (3 * i + 2) * KD:(3 * i + 3) * KD])
        stt(V(Acoef_b, Acoef[:, i, :]), sc, 1.0, V(normT_b, normT[:, i, :]), ALU.add, ALU.mult)
        ts(V(Gcoef_b, Gcoef[:, i, :]), gt, 0.5 if i != 1 else 1.0, None, ALU.mult)

    def shift_col(i, kc):
        return V(modsT_b, modsT[:, 3 * i * KD + kc:3 * i * KD + kc + 1])

    def rms_rstd():
        for kc in range(KD):
            act(V(sq_b, sq[:, kc, :]), V(xT_b, xT[:, kc, :]), AF.Square)
        p, pb = PS()
        for kc in range(KD):
            mm(V(pb, p[:, 0:TT]), V(ones_bf_b, ones_bf[:]), V(sq_b, sq[:, kc, :]), start=(kc == 0), stop=(kc == KD - 1))
        act(V(rstd_b, rstd[:]), V(pb, p[:, 0:TT]), AF.Sqrt, bias=EPS, scale=1.0 / D)
        S.op("dve", lambda e: e.reciprocal(rstd[:], rstd[:]), reads=[V(rstd_b, rstd[:])], writes=[V(rstd_b, rstd[:])])

    tmpn, tmpn_b = sb("tmpn", [128, TT])
    cacc, cacc_b = tmpn, tmpn_b

    def rms_mod(i):
        rms_rstd()
        for kc in range(KD):
            tt(V(tmpn_b, tmpn[:]), V(xT_b, xT[:, kc, :]), V(rstd_b, rstd[:]), ALU.mult)
            act(V(hT_b, hT[:, kc, :]), V(tmpn_b, tmpn[:]), AF.Identity,
                bias=shift_col(i, kc), scale=V(Acoef_b, Acoef[:, i, kc:kc + 1]))

    scr = {}
    cur_tile = [0]

    def wload(dst_pool, idx, src, K, tag):
        t, b = dst_pool[idx[0] % len(dst_pool)]
        idx[0] += 1
        if tag not in scr:
            dt_ = nc.dram_tensor("scr_" + tag, [128, K * 128], BF16, kind="Internal").ap()
            scr[tag] = (dt_, Buf("scr_" + tag))
        sap, sbuf_ = scr[tag]
        sview = sap.rearrange("p (k n) -> p k n", k=K)
        if cur_tile[0] == 0:
            S.dma("pool", t[:, 0:K, :], src, reads=[], writes=[V(b, t[:])], key=b)
            if NT > 1:
                S.dma("sp", sview, t[:, 0:K, :], reads=[V(b, t[:])], writes=[V(sbuf_, sap)], key=b)
        else:
            S.dma("sp", t[:, 0:K, :], sview, reads=[V(sbuf_, sap)], writes=[V(b, t[:])], key=b)
        return t, b

    sg, sg_b = sb("sg", [128, TT])

    def ffn(f, i):
        aT, aT_b = actT[0]
        wgv = wg_d[f].rearrange("(k p) n -> p k n", p=128)
        wuv = wu_d[f].rearrange("(k p) n -> p k n", p=128)
        wdv = wd_d[f].rearrange("(k p) n -> p k n", p=128)
        for hf in range(2):
            for jj in range(KFH):
                j = hf * KFH + jj
                gt_, gb_ = wload(wa, wa_i, wgv[:, :, j * 128:(j + 1) * 128], KD, "g%d_%d" % (f, j))
                ut_, ub_ = wload(wa, wa_i, wuv[:, :, j * 128:(j + 1) * 128], KD, "u%d_%d" % (f, j))
                pg, pgb = PS()
                pu, pub = PS()
                for kc in range(KD):
                    mm(V(pgb, pg[:, 0:TT]), V(gb_, gt_[:, kc, :]), V(hT_b, hT[:, kc, :]), start=(kc == 0), stop=(kc == KD - 1))
                for kc in range(KD):
                    mm(V(pub, pu[:, 0:TT]), V(ub_, ut_[:, kc, :]), V(hT_b, hT[:, kc, :]), start=(kc == 0), stop=(kc == KD - 1))
                act(V(sg_b, sg[:]), V(pgb, pg[:, 0:TT]), AF.Silu)
                tt(V(aT_b, aT[:, jj, :]), V(pub, pu[:, 0:TT]), V(sg_b, sg[:]), ALU.mult)
            for m in range(KD):
                dt_, db_ = wload(wdp, wd_i, wdv[:, hf * KFH:(hf + 1) * KFH, m * 128:(m + 1) * 128], KFH, "d%d_%d_%d" % (f, hf, m))
                pd, pdb = PS()
                for kf in range(KFH):
                    mm(V(pdb, pd[:, 0:TT]), V(db_, dt_[:, kf, :]), V(aT_b, aT[:, kf, :]), start=(kf == 0), stop=(kf == KFH - 1))
                stt(V(xT_b, xT[:, m, :]), V(pdb, pd[:, 0:TT]), V(Gcoef_b, Gcoef[:, i, m:m + 1]), V(xT_b, xT[:, m, :]),
                    ALU.mult, ALU.add)

    glu, glu_b = sb("glu", [128, 30 + TT])
    assert KC * TT <= 2 * D
    ypre = xin_all[:].rearrange("p a d -> p (a d)")[:, 0:KC * TT].rearrange("p (k t) -> p k t", k=KC)
    ypre_b = xin_all_b
    ysq2 = [sb("ysq%d" % i, [128, TT]) for i in range(2)]
    pre, pre_b = glu, glu_b
    qkv = [sb("qkv%d" % i, [128, TT]) for i in range(3)]
    zs, zs_b = sb("zs", [128, TT])
    abT, abT_b = sb("abT", [64, TT])
    wab, wab_b = sb("wab", [128, KD, 32], BF16)
    RB = {n: sb("RB_" + n, [128, TT]) for n in ("lnb", "beta", "G", "lsk", "lsq", "R", "P", "Q", "Kd")}
    RB["t"] = RB["lsq"]
    RB["E"] = RB["lsq"]
    RB["g"] = RB["Kd"]
    RB["nEP"] = RB["lnb"]
    RB["nR"] = RB["lsk"]
    stk = [sb("stk%d" % i, [128, TT]) for i in range(2)]
    cols = [sb("cols%d" % i, [128, 128]) for i in range(2)]
    ktok, ktok_b = sb("ktok", [128, 128])
    vtok, vtok_b = sb("vtok", [128, 128])
    kd_t, kd_b = sb("kd_t", [128, 128])
    mA, mA_b = sb("mA", [128, 128])
    mB, mB_b = sb("mB", [128, 128])
    NAt, NA_b = sb("NAt", [128, 256])
    NBt, NB_b = sb("NBt", [128, 256])
    tA, tA_b = sb("tA", [128, 128])
    tB, tB_b = sb("tB", [128, 128])
    tY, tY_b = sb("tY", [128, 256])
    mean_t, mean_b = RB["R"]
    rs2, rs2_b = RB["P"]
    aTt, aTt_b = sb("aTt", [128, 128])
    qeT, qeT_b = sb("qeT", [128, 128])
    TbT, TbT_b = sb("TbT", [128, 128])
    TwT, TwT_b = sb("TwT", [128, 128])
    u_t, u_b = sb("u_t", [128, 128])
    nwT, nwT_b = sb("nwT", [128, 128])
    vnew, vnew_b = sb("vnew", [128, 128])
    on_t, on_b = TbT, TbT_b
    sml, sml_b = sb("sml", [128, 8])
    for i in range(2):
        S.op("dve", lambda e, i=i: e.memset(stk[i][0][:], 0.0), writes=[V(stk[i][1], stk[i][0][:])])
    S.op("dve", lambda e: e.memset(abT[:], 0.0), writes=[V(abT_b, abT[:])])
    selm, selm_b = sb("selm", [64, 2, 128])

    def build_sel(h):
        for r in range(2):
            rr = h if r == 0 else 32 + h
            ts(V(selm_b, selm[:, r, :]), V(cst_b, cst[0:64, 1, :]), V(cst_b, cst[0:64, 0, rr:rr + 1]), None, ALU.mult)

    win_v = win_d.rearrange("(k p) n -> p k n", p=128)
    wout_v = wout_d.rearrange("(k p) n -> p k n", p=128)
    QB = 2 * CW

    def proj(col0):
        wt, wb = wload(wa, wa_i, win_v[:, :, col0:col0 + 128], KD, "in%d" % col0)
        p, pb = PS()
        for kc in range(KD):
            mm(V(pb, p[:, 0:TT]), V(wb, wt[:, kc, :]), V(hT_b, hT[:, kc, :]), start=(kc == 0), stop=(kc == KD - 1))
        return V(pb, p[:, 0:TT])

    def rb(n, sl=None):
        t, b = RB[n]
        return V(b, t[:] if sl is None else t[:, sl])

    def mixer():
        for j in range(KC):
            pa = proj(j * 128)
            pgt = proj(CW + j * 128)
            act(V(sg_b, sg[:]), pgt, AF.Sigmoid)
            cp(V(glu_b, glu[:, 0:30]), V(gtail[j][1], gtail[j][0][:]))
            tt(V(glu_b, glu[:, 30:30 + TT]), pa, V(sg_b, sg[:]), ALU.mult)
            cp(V(gtail[j][1], gtail[j][0][:]), V(glu_b, glu[:, TT:TT + 30]))
            for k in range(31):
                wk = V(wdwT_b, wdwT[:, j, k:k + 1])
                if k == 0:
                    ts(V(cacc_b, cacc[:]), V(glu_b, glu[:, 0:TT]), wk, V(cvec_b, cvec[:, 0, j:j + 1]), ALU.mult, ALU.add)
                elif k < 30:
                    stt(V(cacc_b, cacc[:]), V(glu_b, glu[:, k:k + TT]), wk, V(cacc_b, cacc[:]), ALU.mult, ALU.add)
                else:
                    stt(V(ypre_b, ypre[:, j, :]), V(glu_b, glu[:, k:k + TT]), wk, V(cacc_b, cacc[:]), ALU.mult, ALU.add)
        pmn, pmn_b = PS()
        pvr, pvr_b = PS()
        for j in range(KC):
            mm(V(pmn_b, pmn[:, 0:TT]), ONES, V(ypre_b, ypre[:, j, :]), start=(j == 0), stop=(j == KC - 1))
        for j in range(KC):
            yq, yqb = ysq2[j % 2]
            act(V(yqb, yq[:]), V(ypre_b, ypre[:, j, :]), AF.Square)
            mm(V(pvr_b, pvr[:, 0:TT]), ONES, V(yqb, yq[:]), start=(j == 0), stop=(j == KC - 1), sig=True)
        ts(V(mean_b, mean_t[:]), V(pmn_b, pmn[:, 0:TT]), 1.0 / CW, None, ALU.mult)
        tt(V(rs2_b, rs2[:]), V(mean_b, mean_t[:]), V(mean_b, mean_t[:]), ALU.mult)
        stt(V(rs2_b, rs2[:]), V(pvr_b, pvr[:, 0:TT]), 1.0 / CW, V(rs2_b, rs2[:]), ALU.mult, ALU.subtract)
        act(V(rs2_b, rs2[:]), V(rs2_b, rs2[:]), AF.Sqrt, bias=EPS, scale=1.0)
        S.op("dve", lambda e: e.reciprocal(rs2[:], rs2[:]), reads=[V(rs2_b, rs2[:])], writes=[V(rs2_b, rs2[:])])
        for j in range(KC):
            tt(V(cacc_b, cacc[:]), V(ypre_b, ypre[:, j, :]), V(mean_b, mean_t[:]), ALU.subtract)
            tt(V(cacc_b, cacc[:]), V(cacc_b, cacc[:]), V(rs2_b, rs2[:]), ALU.mult)
            act(V(yT_b, yT[:, j, :]), V(cacc_b, cacc[:]), AF.Silu, bias=V(cvec_b, cvec[:, 2, j:j + 1]),
                scale=V(cvec_b, cvec[:, 1, j:j + 1]))

        if CUT <= 1:
            return
        S.dma("pool", wab[:, :, 0:NH], win_v[:, :, QB + 4 * DNW:QB + 4 * DNW + NH], reads=[], writes=[V(wab_b, wab[:])], key=wab_b, slow=True)
        S.dma("pool", wab[:, :, 16:16 + NH], win_v[:, :, QB + 4 * DNW + NH:QB + 4 * DNW + 2 * NH], reads=[], writes=[V(wab_b, wab[:])], key=wab_b, slow=True)
        pab, pab_b = PS()
        for kc in range(KD):
            mm(V(pab_b, pab[0:NH, 0:TT]), V(wab_b, wab[:, kc, 0:NH]), V(hT_b, hT[:, kc, :]), start=(kc == 0), stop=(kc == KD - 1))
        cp(V(abT_b, abT[0:NH, :]), V(pab_b, pab[0:NH, 0:TT]))
        pab2, pab2_b = PS()
        for kc in range(KD):
            mm(V(pab2_b, pab2[32:32 + NH, 0:TT]), V(wab_b, wab[:, kc, 16:16 + NH]), V(hT_b, hT[:, kc, :]), start=(kc == 0), stop=(kc == KD - 1))
        cp(V(abT_b, abT[32:32 + NH, :]), V(pab2_b, pab2[32:32 + NH, 0:TT]))

        if CUT <= 2:
            return
        for h in range(NH):
            for i3 in range(3):
                pp = proj(QB + i3 * DNW + h * 128)
                ci = i3 * NH + h
                cp(V(pre_b, pre[:, 0:3]), V(ptail[ci][1], ptail[ci][0][:]))
                cp(V(pre_b, pre[:, 3:3 + TT]), pp, eng="act")
                cp(V(ptail[ci][1], ptail[ci][0][:]), V(pre_b, pre[:, TT:TT + 3]))
                for k in range(4):
                    wk = V(wshT_b, wshT[:, ci, k:k + 1])
                    if k == 0:
                        ts(V(cacc_b, cacc[:]), V(pre_b, pre[:, 0:TT]), wk, None, ALU.mult)
                    else:
                        stt(V(cacc_b, cacc[:]), V(pre_b, pre[:, k:k + TT]), wk, V(cacc_b, cacc[:]), ALU.mult, ALU.add)
                act(V(qkv[i3][1], qkv[i3][0][:]), V(cacc_b, cacc[:]), AF.Silu)
            pz = proj(QB + 3 * DNW + h * 128)
            act(V(zs_b, zs[:]), pz, AF.Silu)
            qT, kT, vT = [V(qkv[i][1], qkv[i][0][:]) for i in range(3)]
            if CUT <= 3:
                continue
            pb1, pb1_b = PS()
            build_sel(h)
            mm(V(pb1_b, pb1[:, 0:TT]), V(selm_b, selm[:, 0, :]), V(abT_b, abT[:]))
            pa1, pa1_b = PS()
            mm(V(pa1_b, pa1[:, 0:TT]), V(selm_b, selm[:, 1, :]), V(abT_b, abT[:]))
            act(rb("t"), V(pb1_b, pb1[:, 0:TT]), AF.Exp, scale=-1.0)
            act(rb("lnb"), rb("t"), AF.Ln, bias=1.0)
            ts(rb("lnb"), rb("lnb"), -1.0, None, ALU.mult)
            act(rb("beta"), rb("lnb"), AF.Exp)
            act(rb("t"), V(pa1_b, pa1[:, 0:TT]), AF.Exp, bias=V(hvec_b, hvec[:, 1, h:h + 1]))
            act(rb("g"), rb("t"), AF.Ln, bias=1.0)
            ts(rb("g"), rb("g"), V(negA_b, negA[:, h:h + 1]), None, ALU.mult)
            S.op("dve", lambda e: e.tensor_tensor_scan(RB["G"][0][:], rmask[:], RB["g"][0][:], 0.0, ALU.mult, ALU.add),
                 reads=[V(rmask_b, rmask[:]), rb("g")], writes=[rb("G")])
            if CUT <= 4:
                continue
            for (src, dst) in ((kT, "lsk"), (qT, "lsq")):
                act(V(cacc_b, cacc[:]), src, AF.Square)
                pss, pss_b = PS()
                mm(V(pss_b, pss[:, 0:TT]), ONES, V(cacc_b, cacc[:]))
                act(rb(dst), V(pss_b, pss[:, 0:TT]), AF.Ln, bias=EPS)
            stt(rb("R"), rb("lsk"), 0.5, rb("G"), ALU.mult, ALU.add)
            stt(rb("P"), rb("lsk"), -0.5, rb("G"), ALU.mult, ALU.add)
            tt(rb("P"), rb("P"), rb("lnb"), ALU.add)
            stt(rb("Q"), rb("lsq"), -0.5, rb("G"), ALU.mult, ALU.add)
            ts(rb("Q"), rb("Q"), float(np.log(128.0 ** -0.5)), None, ALU.add)
            act(rb("E"), rb("Q"), AF.Exp)
            act(rb("nEP"), rb("P"), AF.Exp)
            ts(rb("nEP"), rb("nEP"), -1.0, None, ALU.mult)
            ts(rb("nR"), rb("R"), -1.0, None, ALU.mult)
            for blk in range(NB):
                bs = slice(blk * 128, (blk + 1) * 128)
                last = blk * 128 + 127
                glast = V(RB["G"][1], RB["G"][0][:, last:last + 1])
                act(rb("Kd", bs), rb("nR", bs), AF.Exp, bias=glast)
            if CUT <= 5:
                continue
            for (pi, nme) in ((0, "nR"), (32, "P"), (64, "beta"), (96, "nEP")):
                cp(V(stk[0][1], stk[0][0][pi:pi + 1, :]), V(RB[nme][1], RB[nme][0][pi:pi + 1, :]))
            cp(V(stk[1][1], stk[1][0][0:1, :]), V(RB["Kd"][1], RB["Kd"][0][0:1, :]))

            if CUT <= 6:
                continue
            Sb, Sbb = Sst[h]
            SV = V(Sbb, Sb[:])
            for blk in range(NB):
                bs = slice(blk * 128, (blk + 1) * 128)
                last = blk * 128 + 127
                for i in range(2):
                    pc, pcb = PS()
                    tr(V(pcb, pc[:, 0:128]), V(stk[i][1], stk[i][0][:, bs]))
                    cp(V(cols[i][1], cols[i][0][:]), V(pcb, pc[:, 0:128]), eng="act")
                c0, c0b = cols[0]
                cnR = V(c0b, c0[:, 0:1])
                cP = V(c0b, c0[:, 32:33])
                cbeta = V(c0b, c0[:, 64:65])
                cnEP = V(c0b, c0[:, 96:97])
                cKd = V(cols[1][1], cols[1][0][:, 0:1])
                qTb = V(qkv[0][1], qkv[0][0][:, bs])
                kTb = V(qkv[1][1], qkv[1][0][:, bs])
                vTb = V(qkv[2][1], qkv[2][0][:, bs])
                ptk, ptk_b = PS()
                tr(V(ptk_b, ptk[:, 0:128]), kTb)
                tr(V(ptk_b, ptk[:, 128:256]), vTb)
                cp(V(ktok_b, ktok[:]), V(ptk_b, ptk[:, 0:128]), eng="act")
                cp(V(vtok_b, vtok[:]), V(ptk_b, ptk[:, 128:256]), eng="act")
                ts(V(kd_b, kd_t[:]), V(ptk_b, ptk[:, 0:128]), cKd, None, ALU.mult)
                if CUT <= 7:
                    continue
                pkk, pkk_b = PS()
                mm(V(pkk_b, pkk[:, 0:128]), kTb, kTb)
                mm(V(pkk_b, pkk[:, 128:256]), kTb, qTb)
                tt(V(mA_b, mA[:]), NEGS, rb("R", bs), ALU.subtract)
                act(V(mA_b, mA[:]), V(mA_b, mA[:]), AF.Exp, bias=cP)
                tt(V(mB_b, mB[:]), NEGST, rb("P", bs), ALU.add)
                act(V(mB_b, mB[:]), V(mB_b, mB[:]), AF.Exp, bias=cnR)
                tt(V(aTt_b, aTt[:]), NEGTT, rb("Q", bs), ALU.add)
                act(V(aTt_b, aTt[:]), V(aTt_b, aTt[:]), AF.Exp, bias=cnR)
                stt(V(mA_b, mA[:]), V(pkk_b, pkk[:, 0:128]), -1.0, V(mA_b, mA[:]), ALU.mult, ALU.mult)
                stt(V(mB_b, mB[:]), V(pkk_b, pkk[:, 0:128]), -1.0, V(mB_b, mB[:]), ALU.mult, ALU.mult)
                tt(V(aTt_b, aTt[:]), V(pkk_b, pkk[:, 128:256]), V(aTt_b, aTt[:]), ALU.mult)
                tt(V(qeT_b, qeT[:]), qTb, rb("E", bs), ALU.mult)
                if CUT <= 8:
                    continue
                XTv = V(NA_b, NAt[:, 128:256])
                XUv = V(NB_b, NBt[:, 128:256])
                tt(V(NA_b, NAt[:, 0:128]), V(mA_b, mA[:]), BD16, ALU.mult)
                tt(V(NB_b, NBt[:, 0:128]), V(mB_b, mB[:]), BD16, ALU.mult)
                cp(XTv, IDN)
                cp(XUv, IDN)
                for lev in range(4):
                    p1, p1b = PS()
                    p2, p2b = PS()
                    if lev == 3:
                        mm(V(p1b, p1[:, 128:256]), V(NA_b, NAt[:, 0:128]), XUv)
                        mm(V(p2b, p2[:, 128:256]), V(NB_b, NBt[:, 0:128]), XTv)
                    else:
                        mm(V(p1b, p1[:, 0:256]), V(NA_b, NAt[:, 0:128]), V(NB_b, NBt[:, 0:256]))
                        mm(V(p2b, p2[:, 0:256]), V(NB_b, NBt[:, 0:128]), V(NA_b, NAt[:, 0:256]))
                        cp(V(NB_b, NBt[:, 0:128]), V(p1b, p1[:, 0:128]), eng="act")
                        cp(V(NA_b, NAt[:, 0:128]), V(p2b, p2[:, 0:128]), eng="act")
                    tt(XUv, V(p1b, p1[:, 128:256]), XUv, ALU.add)
                    tt(XTv, V(p2b, p2[:, 128:256]), XTv, ALU.add)
                for li in range(3):
                    lastm = (li == 2)
                    tt(V(tA_b, tA[:]), V(mA_b, mA[:]), LLm[li], ALU.mult)
                    pY, pYb = PS()
                    mm(V(pYb, pY[:, 0:128]), V(tA_b, tA[:]), XUv)
                    if not lastm:
                        tt(V(tB_b, tB[:]), V(mB_b, mB[:]), URm[li], ALU.mult)
                        mm(V(pYb, pY[:, 128:256]), V(tB_b, tB[:]), XTv)
                        cp(V(tY_b, tY[:, 0:256]), V(pYb, pY[:, 0:256]), eng="act")
                    else:
                        cp(V(tY_b, tY[:, 0:128]), V(pYb, pY[:, 0:128]), eng="act")
                    pZ, pZb = PS()
                    mm(V(pZb, pZ[:, 0:128]), XTv, V(tY_b, tY[:, 0:128]))
                    if not lastm:
                        mm(V(pZb, pZ[:, 128:256]), XUv, V(tY_b, tY[:, 128:256]))
                    tt(XUv, V(pZb, pZ[:, 0:128]), XUv, ALU.add)
                    if not lastm:
                        tt(XTv, V(pZb, pZ[:, 128:256]), XTv, ALU.add)
                XT = XUv
                ts(V(TbT_b, TbT[:]), XT, cbeta, None, ALU.mult)
                ts(V(TwT_b, TwT[:]), XT, cnEP, None, ALU.mult)
                pu_, pu_b = PS()
                mm(V(pu_b, pu_[:, 0:128]), V(TbT_b, TbT[:]), V(vtok_b, vtok[:]))
                mm(V(pu_b, pu_[:, 128:256]), V(ktok_b, ktok[:]), V(TwT_b, TwT[:]))
                cp(V(u_b, u_t[:]), V(pu_b, pu_[:, 0:128]), eng="act")
                cp(V(nwT_b, nwT[:]), V(pu_b, pu_[:, 128:256]), eng="act")
                if CUT <= 9:
                    continue
                pv, pvb = PS()
                mm(V(pvb, pv[:, 0:128]), V(nwT_b, nwT[:]), SV)
                tt(V(vnew_b, vnew[:]), V(pvb, pv[:, 0:128]), V(u_b, u_t[:]), ALU.add)
                po, pob = PS()
                mm(V(pob, po[:, 0:128]), V(qeT_b, qeT[:]), SV, start=True, stop=False)
                mm(V(pob, po[:, 0:128]), V(aTt_b, aTt[:]), V(vnew_b, vnew[:]), start=False, stop=True)
                pds, pdsb = PS()
                mm(V(pdsb, pds[:, 0:128]), V(kd_b, kd_t[:]), V(vnew_b, vnew[:]))
                act(V(sml_b, sml[:, 0:1]), V(RB["G"][1], RB["G"][0][:, last:last + 1]), AF.Exp)
                stt(SV, SV, V(sml_b, sml[:, 0:1]), V(pdsb, pds[:, 0:128]), ALU.mult, ALU.add)
                if CUT <= 10:
                    continue
                act(V(on_b, on_t[:]), V(pob, po[:, 0:128]), AF.Square, accum=V(sml_b, sml[:, 1:2]))
                act(V(sml_b, sml[:, 2:3]), V(sml_b, sml[:, 1:2]), AF.Sqrt, bias=EPS, scale=1.0 / 128)
                S.op("dve", lambda e: e.reciprocal(sml[:, 3:4], sml[:, 2:3]), reads=[V(sml_b, sml[:, 2:3])], writes=[V(sml_b, sml[:, 3:4])])
                ts(V(on_b, on_t[:]), V(pob, po[:, 0:128]), V(sml_b, sml[:, 3:4]), None, ALU.mult)
                pt2, pt2b = PS()
                tr(V(pt2b, pt2[:, 0:128]), V(on_b, on_t[:]))
                stt(V(yT_b, yT[:, KC + h, bs]), V(pt2b, pt2[:, 0:128]), V(onw_b, onw[:, 0:1]), V(zs_b, zs[:, bs]), ALU.mult, ALU.mult)

        if CUT <= 11:
            return
        for m in range(KD):
            wt, wb = wload(wa, wa_i, wout_v[:, :, m * 128:(m + 1) * 128], KY, "out%d" % m)
            p, pb = PS()
            for kc in range(KY):
                mm(V(pb, p[:, 0:TT]), V(wb, wt[:, kc, :]), V(yT_b, yT[:, kc, :]), start=(kc == 0), stop=(kc == KY - 1))
            stt(V(xT_b, xT[:, m, :]), V(pb, p[:, 0:TT]), V(Gcoef_b, Gcoef[:, 1, m:m + 1]), V(xT_b, xT[:, m, :]), ALU.mult, ALU.add)

    out_bufs = []
    for t in range(NT):
        cur_tile[0] = t
        for tb in range(NB):
            xt, xb = xin[tb % 2]
            r0 = t * TT + tb * 128
            ld(V(xb, xt[:]), x_d[r0:r0 + 128, :])
            for k0 in range(0, KD, 4):
                p, pb = PS()
                nk = min(4, KD - k0)
                for kk in range(nk):
                    tr(V(pb, p[:, kk * 128:(kk + 1) * 128]), V(xb, xt[:, (k0 + kk) * 128:(k0 + kk + 1) * 128]))
                S.op("dve", lambda e, p=p, k0=k0, nk=nk, tb=tb: e.tensor_copy(
                    xT[:, k0:k0 + nk, tb * 128:(tb + 1) * 128], p[:, 0:nk * 128].rearrange("p (k t) -> p k t", k=nk)),
                    reads=[V(pb, p[:])], writes=[V(xT_b, xT[:])])
        rms_mod(0)
        ffn(0, 0)
        if stop != "ffn1":
            rms_mod(1)
            mixer()
            if stop != "mixer":
                rms_mod(2)
                ffn(1, 2)
        if stop is None:
            rms_rstd()
            for kc in range(KD):
                tt(V(tmpn_b, tmpn[:]), V(xT_b, xT[:, kc, :]), V(rstd_b, rstd[:]), ALU.mult)
                ts(V(xT_b, xT[:, kc, :]), V(tmpn_b, tmpn[:]), V(normT_b, normT[:, 3, kc:kc + 1]), None, ALU.mult)
        for tb in range(NB):
            xt, xb = xin[tb % 2]
            r0 = t * TT + tb * 128
            for k0 in range(0, KD, 4):
                p, pb = PS()
                nk = min(4, KD - k0)
                for kk in range(nk):
                    tr(V(pb, p[:, kk * 128:(kk + 1) * 128]), V(xT_b, xT[:, k0 + kk, tb * 128:(tb + 1) * 128]))
                cp(V(xb, xt[:, k0 * 128:(k0 + nk) * 128]), V(pb, p[:, 0:nk * 128]), eng="act")
            S.dma("sp", y_d[r0:r0 + 128, :], xt[:], reads=[V(xb, xt[:])], writes=[], key=xb)
            if xb not in out_bufs:
                out_bufs.append(xb)
    S.emit(final_waits=out_bufs)
    es.close()
    return nc


def host_inputs(cfg, b, x, c, w_ada, b_ada, ffn1_norm, ffn1_wg, ffn1_wu, ffn1_wd, mix_norm, w_in,
                w_dw, b_dw, conv_ln_w, conv_ln_b, w_short, a_log, dt_bias, dn_norm_w, w_out,
                ffn2_norm, ffn2_wg, ffn2_wu, ffn2_wd, final_norm):
    KD, KC, NH, TT = cfg["KD"], cfg["KC"], cfg["NH"], cfg["TT"]
    f = np.float32
    A = np.ascontiguousarray

    def fm(v, k):
        return A(np.asarray(v, f).reshape(k, 128).T)

    consts = np.zeros((128, 12, 128), f)
    idx = np.arange(128)
    consts[:, 0, :] = np.eye(128, dtype=f)
    consts[:, 1, :] = 1.0
    consts[:, 2, :] = np.where(idx[:, None] > idx[None, :], 0.0, NEG)
    consts[:, 3, :] = np.where(idx[None, :] > idx[:, None], 0.0, NEG)
    consts[:, 4, :] = np.where(idx[None, :] >= idx[:, None], 0.0, NEG)
    consts[:, 5, :] = (idx[:, None] // 16 == idx[None, :] // 16)
    for i, bsz in enumerate((16, 32, 64)):
        ll = ((idx[:, None] // (2 * bsz) == idx[None, :] // (2 * bsz)) & (idx[:, None] % (2 * bsz) >= bsz)
              & (idx[None, :] % (2 * bsz) < bsz)).astype(f)
        consts[:, 6 + i, :] = ll
        consts[:, 9 + i, :] = ll.T
    rmask = np.ones((128, TT), f)
    rmask[:, ::128] = 0.0
    return {
        "x": A(np.asarray(x[b], f)),
        "cT": fm(c[b], KD),
        "w_ada": A(np.asarray(w_ada[0], f)),
        "b_adaT": fm(b_ada[0], 9 * KD),
        "normT": A(np.stack([fm(ffn1_norm[0], KD), fm(mix_norm[0], KD), fm(ffn2_norm[0], KD), fm(final_norm, KD)], axis=1)),
        "ffn1_wg": A(np.asarray(ffn1_wg[0], f)), "ffn1_wu": A(np.asarray(ffn1_wu[0], f)), "ffn1_wd": A(np.asarray(ffn1_wd[0], f)),
        "ffn2_wg": A(np.asarray(ffn2_wg[0], f)), "ffn2_wu": A(np.asarray(ffn2_wu[0], f)), "ffn2_wd": A(np.asarray(ffn2_wd[0], f)),
        "w_in": A(np.asarray(w_in[0], f)),
        "w_out": A(np.asarray(w_out[0], f)),
        "w_dwT": A(np.asarray(w_dw[0], f).T.reshape(KC, 128, 31).transpose(1, 0, 2)),
        "cvecT": A(np.stack([fm(b_dw[0], KC), fm(conv_ln_w[0], KC), fm(conv_ln_b[0], KC)], axis=1)),
        "w_shT": A(np.asarray(w_short[0], f).T.reshape(3 * NH, 128, 4).transpose(1, 0, 2)),
        "hvec": A(np.broadcast_to(np.stack([np.asarray(a_log[0], f), np.asarray(dt_bias[0], f)], axis=0)[None], (128, 2, NH))),
        "onwT": A(np.asarray(dn_norm_w[0], f).reshape(128, 1)),
        "consts": consts,
        "rmask": rmask,
    }


_NC_CACHE = {}


def kernel(**inputs):
    cfg = FULL
    B = inputs["x"].shape[0]
    if "nc" not in _NC_CACHE:
        _NC_CACHE["nc"] = build(cfg)
    nc = _NC_CACHE["nc"]
    n = 8
    in_maps = [host_inputs(cfg, i % B, **inputs) for i in range(n)]
    res = run_bass_kernel_spmd(nc, in_maps, core_ids=list(range(n)))
    out = np.stack([res.results[i]["y"] for i in range(B)], axis=0)
    return out.astype(np.float32)
```

```python
import numpy as np
import os
CUT = int(os.environ.get('KCUT', '99'))
from contextlib import ExitStack
import concourse.bass as bass
import concourse.mybir as mybir
from concourse.bass_utils import run_bass_kernel_spmd

F32 = mybir.dt.float32
BF16 = mybir.dt.bfloat16
AF = mybir.ActivationFunctionType
ALU = mybir.AluOpType
NEG = -1.0e9
EPS = 1e-6


class Buf:
    __slots__ = ("name", "w", "r", "dsem", "psum")

    def __init__(self, name, psum=False):
        self.name = name
        self.psum = psum
        self.w = None
        self.r = []
        self.dsem = None


class V:
    __slots__ = ("b", "ap")

    def __init__(self, b, ap):
        self.b = b
        self.ap = ap


class DSem:
    def __init__(self, sem):
        self.sem = sem
        self.count = 0


class Rec:
    __slots__ = ("eng", "fn", "deps", "sig", "idx", "dma", "dval")

    def __init__(self, eng, fn, deps, sig, dma=None):
        self.eng = eng
        self.fn = fn
        self.deps = deps
        self.sig = sig
        self.dma = dma
        self.dval = 0
        self.idx = -1


class Sched:
    ENGS = ("pe", "dve", "act", "pool", "sp")

    def __init__(self, nc, es):
        self.nc = nc
        self.es = es
        self.q = {e: [] for e in self.ENGS}
        self.esem = {e: es.enter_context(nc.semaphore("s_" + e)) for e in ("pe", "dve", "act", "pool")}
        self.nsem = 0

    def _deps(self, eng, reads, writes, is_dma):
        deps = []
        for v in reads:
            w = v.b.w
            if w is not None:
                deps.append((w, "raw"))
            if v.b.psum:
                for r in v.b.r:
                    if r.eng != eng:
                        deps.append((r, "rr"))
        for v in writes:
            b = v.b
            if b.w is not None:
                deps.append((b.w, "waw"))
            for r in b.r:
                deps.append((r, "war"))
        out = []
        for (d, kind) in deps:
            if d.dma is None and not is_dma and d.eng == eng:
                if eng == "pe":
                    continue
            if d.dma is not None:
                out.append((d, d.dma.count))
            else:
                out.append((d, None))
        return out

    def _commit(self, rec, reads, writes):
        for v in writes:
            v.b.w = rec
            v.b.r = []
        for v in reads:
            if v.b.w is not rec:
                v.b.r.append(rec)

    def op(self, eng, fn, reads=(), writes=(), sig=True):
        rec = Rec(eng, fn, self._deps(eng, reads, writes, False), sig)
        rec.idx = len(self.q[eng])
        self.q[eng].append(rec)
        self._commit(rec, reads, writes)
        return rec

    def dma(self, queue, out, in_, reads, writes, key, slow=False):
        if key.dsem is None:
            key.dsem = DSem(self.es.enter_context(self.nc.semaphore("d%d" % self.nsem)))
            self.nsem += 1
        ds = key.dsem
        rec = Rec(queue, lambda e: e.dma_start(out=out, in_=in_, allow_slow_non_contiguous=slow), self._deps(queue, reads, writes, True), True, dma=ds)
        ds.count += 16
        rec.dval = ds.count
        rec.idx = len(self.q[queue])
        self.q[queue].append(rec)
        self._commit(rec, reads, writes)
        return rec

    def emit(self, final_waits=()):
        nc = self.nc
        cnt = {}
        for e in ("pe", "dve", "act", "pool"):
            arr = []
            c = 0
            for r in self.q[e]:
                if r.dma is None and r.sig:
                    c += 1
                arr.append(c)
            res = [0] * len(arr)
            nxt = None
            for i in range(len(arr) - 1, -1, -1):
                r = self.q[e][i]
                if r.dma is None and r.sig:
                    nxt = arr[i]
                res[i] = nxt
            cnt[e] = res
        handles = {"pe": nc.tensor, "dve": nc.vector, "act": nc.scalar, "pool": nc.gpsimd, "sp": nc.sync}
        with nc.Block() as block:
            def run(e, h):
                waited = {}
                for r in self.q[e]:
                    for (d, dv) in r.deps:
                        if d.dma is not None:
                            sem, val = d.dma.sem, dv
                        else:
                            val = cnt[d.eng][d.idx]
                            assert val is not None, "dep on op with no later signal"
                            sem = self.esem[d.eng]
                        k = id(sem)
                        if waited.get(k, 0) >= val:
                            continue
                        waited[k] = val
                        h.wait_ge(sem, val)
                    ins = r.fn(h)
                    if r.dma is not None:
                        ins.then_inc(r.dma.sem, 16)
                    elif r.sig:
                        ins.then_inc(self.esem[e], 1)
                if e == "sp":
                    for b in final_waits:
                        h.wait_ge(b.dsem.sem, b.dsem.count)

            @block.tensor
            def _(h):
                run("pe", h)

            @block.vector
            def _(h):
                run("dve", h)

            @block.scalar
            def _(h):
                run("act", h)

            @block.gpsimd
            def _(h):
                run("pool", h)

            @block.sync
            def _(h):
                run("sp", h)


def make_cfg(D, DFF, CW, NH, T):
    return dict(D=D, DFF=DFF, CW=CW, NH=NH, T=T, KD=D // 128, KF=DFF // 128, KC=CW // 128,
                TT=min(512, T), INC=2 * CW + 4 * NH * 128 + 2 * NH)


FULL = make_cfg(2048, 5632, 1024, 8, 4096)


def build(cfg, stop=None):
    D, DFF, CW, NH, T = cfg["D"], cfg["DFF"], cfg["CW"], cfg["NH"], cfg["T"]
    KD, KF, KC, TT, INC = cfg["KD"], cfg["KF"], cfg["KC"], cfg["TT"], cfg["INC"]
    NT = T // TT
    NB = TT // 128
    KY = KC + NH
    DNW = NH * 128
    nc = bass.Bass("TRN2", target_bir_lowering=False)
    es = ExitStack()
    S = Sched(nc, es)

    def din(name, shape, dt=F32):
        return nc.dram_tensor(name, list(shape), dt, kind="ExternalInput").ap()

    x_d = din("x", [T, D])
    y_d = nc.dram_tensor("y", [T, D], F32, kind="ExternalOutput").ap()
    cT_d = din("cT", [128, KD])
    wada_d = din("w_ada", [D, 9 * D])
    badaT_d = din("b_adaT", [128, 9 * KD])
    normT_d = din("normT", [128, 4, KD])
    wg_d = [din("ffn1_wg", [D, DFF]), din("ffn2_wg", [D, DFF])]
    wu_d = [din("ffn1_wu", [D, DFF]), din("ffn2_wu", [D, DFF])]
    wd_d = [din("ffn1_wd", [DFF, D]), din("ffn2_wd", [DFF, D])]
    win_d = din("w_in", [D, INC])
    wout_d = din("w_out", [D, D])
    wdwT_d = din("w_dwT", [128, KC, 31])
    cvec_d = din("cvecT", [128, 3, KC])
    wshT_d = din("w_shT", [128, 3 * NH, 4])
    hvec_d = din("hvec", [128, 2, NH])
    onw_d = din("onwT", [128, 1])
    cst_d = din("consts", [128, 12, 128])
    rmask_d = din("rmask", [128, TT])

    SBTOT = [0]

    def sb(name, shape, dt=F32):
        t = es.enter_context(nc.sbuf_tensor("sb_" + name, list(shape), dt))
        nbytes = int(np.prod(shape[1:])) * (2 if dt == BF16 else 4)
        SBTOT[0] += nbytes
        if os.environ.get("KSB"):
            print("SB", name, nbytes, SBTOT[0])
        return t, Buf(name)

    cst, cst_b = sb("cst", [128, 12, 128])
    rmask, rmask_b = sb("rmask", [128, TT])
    ones_bf, ones_bf_b = sb("ones_bf", [128, 128], BF16)
    normT, normT_b = sb("normT", [128, 4, KD])
    wdwT, wdwT_b = sb("wdwT", [128, KC, 31])
    cvec, cvec_b = sb("cvec", [128, 3, KC])
    wshT, wshT_b = sb("wshT", [128, 3 * NH, 4])
    hvec, hvec_b = sb("hvec", [128, 2, NH])
    negA, negA_b = sb("negA", [128, NH])
    onw, onw_b = sb("onw", [128, 1])
    modsT, modsT_b = sb("modsT", [128, 9 * KD])
    Acoef, Acoef_b = sb("Acoef", [128, 3, KD])
    Gcoef, Gcoef_b = sb("Gcoef", [128, 3, KD])
    finw_b = normT_b
    xT, xT_b = sb("xT", [128, KD, TT])
    hT, hT_b = sb("hT", [128, KD, TT], BF16)
    KFH = KF // 2
    actT = [sb("actT%d" % i, [128, max(KFH, KD), TT], BF16) for i in range(1)]
    yT, yT_b = sb("yT", [128, KY, TT], BF16)
    sq, sq_b = actT[0]
    rstd, rstd_b = sb("rstd", [128, TT])
    xin_all, xin_all_b = sb("xin_all", [128, 2, D])
    xin = [(xin_all[:, i, :], xin_all_b) for i in range(2)]
    Sst = [sb("Sst%d" % h, [128, 128]) for h in range(NH)]
    gtail = [sb("gtail%d" % j, [128, 30]) for j in range(KC)]
    ptail = [sb("ptail%d" % j, [128, 3]) for j in range(3 * NH)]
    NWA, NWD = 4, 2
    wa = [sb("wa%d" % i, [128, KD, 128], BF16) for i in range(NWA)]
    wdp = [sb("wd%d" % i, [128, KF // 2, 128], BF16) for i in range(NWD)]
    wa_i = [0]
    wd_i = [0]
    psum = []
    for i in range(8):
        t = es.enter_context(nc.psum_tensor("ps%d" % i, [128, 512], F32))
        psum.append((t, Buf("ps%d" % i, psum=True)))
    ps_i = [0]

    def PS():
        t, b = psum[ps_i[0] % 8]
        ps_i[0] += 1
        return t, b

    IDN = V(cst_b, cst[:, 0, :])
    ONES = V(cst_b, cst[:, 1, :])
    NEGS = V(cst_b, cst[:, 2, :])
    NEGST = V(cst_b, cst[:, 3, :])
    NEGTT = V(cst_b, cst[:, 4, :])
    BD16 = V(cst_b, cst[:, 5, :])
    LLm = [V(cst_b, cst[:, 6 + i, :]) for i in range(3)]
    URm = [V(cst_b, cst[:, 9 + i, :]) for i in range(3)]

    def mm(out, lhsT, rhs, start=True, stop=True, sig=None):
        if sig is None:
            sig = stop
        return S.op("pe", lambda e: e.matmul(out.ap, lhsT.ap, rhs.ap, start=start, stop=stop),
                    reads=[lhsT, rhs], writes=[out], sig=sig)

    def tr(out, in_):
        return S.op("pe", lambda e: e.transpose(out.ap, in_.ap, IDN.ap), reads=[in_, IDN], writes=[out])

    def act(out, in_, func, bias=None, scale=None, accum=None, eng="act"):
        rd = [in_]
        kw = {}
        if bias is not None:
            if isinstance(bias, V):
                rd.append(bias)
                kw["bias"] = bias.ap
            else:
                kw["bias"] = float(bias)
        if scale is not None:
            if isinstance(scale, V):
                rd.append(scale)
                kw["scale"] = scale.ap
            else:
                kw["scale"] = float(scale)
        wr = [out]
        if accum is not None:
            kw["accum_out"] = accum.ap
            wr.append(accum)
        return S.op("act", lambda e: e.activation(out.ap, in_.ap, func, **kw), reads=rd, writes=wr)

    def tt(out, a, b, op, eng="dve"):
        return S.op(eng, lambda e: e.tensor_tensor(out.ap, a.ap, b.ap, op), reads=[a, b], writes=[out])

    def ts(out, a, s1, s2, op0, op1=None, eng="dve"):
        rd = [a]
        s1a = s1.ap if isinstance(s1, V) else s1
        s2a = s2.ap if isinstance(s2, V) else s2
        if isinstance(s1, V):
            rd.append(s1)
        if isinstance(s2, V):
            rd.append(s2)
        if op1 is None:
            return S.op(eng, lambda e: e.tensor_scalar(out.ap, a.ap, s1a, None, op0), reads=rd, writes=[out])
        return S.op(eng, lambda e: e.tensor_scalar(out.ap, a.ap, s1a, s2a, op0, op1), reads=rd, writes=[out])

    def stt(out, a, s, b, op0, op1):
        rd = [a, b]
        sa = s.ap if isinstance(s, V) else s
        if isinstance(s, V):
            rd.append(s)
        return S.op("dve", lambda e: e.scalar_tensor_tensor(out.ap, a.ap, sa, b.ap, op0, op1), reads=rd, writes=[out])

    def cp(out, in_, eng="dve"):
        if eng == "act":
            return act(out, in_, AF.Copy)
        return S.op(eng, lambda e: e.tensor_copy(out.ap, in_.ap), reads=[in_], writes=[out])

    def ld(out, src_ap, queue="sp"):
        return S.dma(queue, out.ap, src_ap, reads=[], writes=[out], key=out.b)

    ld(V(cst_b, cst[:]), cst_d)
    ld(V(rmask_b, rmask[:]), rmask_d)
    ld(V(normT_b, normT[:]), normT_d)
    ld(V(wdwT_b, wdwT[:]), wdwT_d)
    ld(V(cvec_b, cvec[:]), cvec_d)
    ld(V(wshT_b, wshT[:]), wshT_d)
    ld(V(hvec_b, hvec[:]), hvec_d)
    ld(V(onw_b, onw[:]), onw_d)
    cp(V(ones_bf_b, ones_bf[:]), ONES)
    act(V(negA_b, negA[:]), V(hvec_b, hvec[:, 0, :]), AF.Exp)
    ts(V(negA_b, negA[:]), V(negA_b, negA[:]), -1.0, None, ALU.mult)
    for h in range(NH):
        S.op("dve", lambda e, h=h: e.memset(Sst[h][0][:], 0.0), writes=[V(Sst[h][1], Sst[h][0][:])])
    for j in range(KC):
        S.op("dve", lambda e, j=j: e.memset(gtail[j][0][:], 0.0), writes=[V(gtail[j][1], gtail[j][0][:])])
    for j in range(3 * NH):
        S.op("dve", lambda e, j=j: e.memset(ptail[j][0][:], 0.0), writes=[V(ptail[j][1], ptail[j][0][:])])

    scT, scT_b = sb("scT", [128, KD])
    badaT, badaT_b = sb("badaT", [128, 9 * KD])
    ld(V(scT_b, scT[:]), cT_d)
    ld(V(badaT_b, badaT[:]), badaT_d)
    act(V(scT_b, scT[:]), V(scT_b, scT[:]), AF.Silu)
    wada_v = wada_d.rearrange("(k p) n -> p k n", p=128)
    NMC = 9 * KD
    pm, pm_b = PS()
    for j in range(NMC):
        wt0, wb = xin[j % 2]
        wt = wt0[:].rearrange("p (k n) -> p k n", k=KD)
        S.dma("sp", wt, wada_v[:, :, j * 128:(j + 1) * 128], reads=[], writes=[V(wb, wt)], key=wb)
        for kc in range(KD):
            mm(V(pm_b, pm[:, j:j + 1]), V(wb, wt[:, kc, :]), V(scT_b, scT[:, kc:kc + 1]),
               start=(kc == 0), stop=(kc == KD - 1))
    tt(V(modsT_b, modsT[:]), V(pm_b, pm[:, 0:NMC]), V(badaT_b, badaT[:]), ALU.add)
    for i in range(3):
        sc = V(modsT_b, modsT[:, (3 * i + 1) * KD:(3 * i + 2) * KD])
        gt = V(modsT_b, modsT[:, (3 * i + 2) * KD:(3 * i + 3) * KD])
        stt(V(Acoef_b, Acoef[:, i, :]), sc, 1.0, V(normT_b, normT[:, i, :]), ALU.add, ALU.mult)
        ts(V(Gcoef_b, Gcoef[:, i, :]), gt, 0.5 if i != 1 else 1.0, None, ALU.mult)

    def shift_col(i, kc):
        return V(modsT_b, modsT[:, 3 * i * KD + kc:3 * i * KD + kc + 1])

    def rms_rstd():
        for kc in range(KD):
            act(V(sq_b, sq[:, kc, :]), V(xT_b, xT[:, kc, :]), AF.Square)
        p, pb = PS()
        for kc in range(KD):
            mm(V(pb, p[:, 0:TT]), V(ones_bf_b, ones_bf[:]), V(sq_b, sq[:, kc, :]), start=(kc == 0), stop=(kc == KD - 1))
        act(V(rstd_b, rstd[:]), V(pb, p[:, 0:TT]), AF.Sqrt, bias=EPS, scale=1.0 / D)
        S.op("dve", lambda e: e.reciprocal(rstd[:], rstd[:]), reads=[V(rstd_b, rstd[:])], writes=[V(rstd_b, rstd[:])])

    tmpn, tmpn_b = sb("tmpn", [128, TT])
    cacc, cacc_b = tmpn, tmpn_b

    def rms_mod(i):
        rms_rstd()
        for kc in range(KD):
            tt(V(tmpn_b, tmpn[:]), V(xT_b, xT[:, kc, :]), V(rstd_b, rstd[:]), ALU.mult)
            act(V(hT_b, hT[:, kc, :]), V(tmpn_b, tmpn[:]), AF.Identity,
                bias=shift_col(i, kc), scale=V(Acoef_b, Acoef[:, i, kc:kc + 1]))

    scr = {}
    cur_tile = [0]

    def wload(dst_pool, idx, src, K, tag):
        t, b = dst_pool[idx[0] % len(dst_pool)]
        idx[0] += 1
        if tag not in scr:
            dt_ = nc.dram_tensor("scr_" + tag, [128, K * 128], BF16, kind="Internal").ap()
            scr[tag] = (dt_, Buf("scr_" + tag))
        sap, sbuf_ = scr[tag]
        sview = sap.rearrange("p (k n) -> p k n", k=K)
        if cur_tile[0] == 0:
            S.dma("pool", t[:, 0:K, :], src, reads=[], writes=[V(b, t[:])], key=b)
            if NT > 1:
                S.dma("sp", sview, t[:, 0:K, :], reads=[V(b, t[:])], writes=[V(sbuf_, sap)], key=b)
        else:
            S.dma("sp", t[:, 0:K, :], sview, reads=[V(sbuf_, sap)], writes=[V(b, t[:])], key=b)
        return t, b

    sg, sg_b = sb("sg", [128, TT])

    def ffn(f, i):
        aT, aT_b = actT[0]
        wgv = wg_d[f].rearrange("(k p) n -> p k n", p=128)
        wuv = wu_d[f].rearrange("(k p) n -> p k n", p=128)
        wdv = wd_d[f].rearrange("(k p) n -> p k n", p=128)
        for hf in range(2):
            for jj in range(KFH):
                j = hf * KFH + jj
                gt_, gb_ = wload(wa, wa_i, wgv[:, :, j * 128:(j + 1) * 128], KD, "g%d_%d" % (f, j))
                ut_, ub_ = wload(wa, wa_i, wuv[:, :, j * 128:(j + 1) * 128], KD, "u%d_%d" % (f, j))
                pg, pgb = PS()
                pu, pub = PS()
                for kc in range(KD):
                    mm(V(pgb, pg[:, 0:TT]), V(gb_, gt_[:, kc, :]), V(hT_b, hT[:, kc, :]), start=(kc == 0), stop=(kc == KD - 1))
                for kc in range(KD):
                    mm(V(pub, pu[:, 0:TT]), V(ub_, ut_[:, kc, :]), V(hT_b, hT[:, kc, :]), start=(kc == 0), stop=(kc == KD - 1))
                act(V(sg_b, sg[:]), V(pgb, pg[:, 0:TT]), AF.Silu)
                tt(V(aT_b, aT[:, jj, :]), V(pub, pu[:, 0:TT]), V(sg_b, sg[:]), ALU.mult)
            for m in range(KD):
                dt_, db_ = wload(wdp, wd_i, wdv[:, hf * KFH:(hf + 1) * KFH, m * 128:(m + 1) * 128], KFH, "d%d_%d_%d" % (f, hf, m))
                pd, pdb = PS()
                for kf in range(KFH):
                    mm(V(pdb, pd[:, 0:TT]), V(db_, dt_[:, kf, :]), V(aT_b, aT[:, kf, :]), start=(kf == 0), stop=(kf == KFH - 1))
                stt(V(xT_b, xT[:, m, :]), V(pdb, pd[:, 0:TT]), V(Gcoef_b, Gcoef[:, i, m:m + 1]), V(xT_b, xT[:, m, :]),
                    ALU.mult, ALU.add)

    glu, glu_b = sb("glu", [128, 30 + TT])
    assert KC * TT <= 2 * D
    ypre = xin_all[:].rearrange("p a d -> p (a d)")[:, 0:KC * TT].rearrange("p (k t) -> p k t", k=KC)
    ypre_b = xin_all_b
    ysq2 = [sb("ysq%d" % i, [128, TT]) for i in range(2)]
    pre, pre_b = glu, glu_b
    qkv = [sb("qkv%d" % i, [128, TT]) for i in range(3)]
    zs, zs_b = sb("zs", [128, TT])
    abT, abT_b = sb("abT", [64, TT])
    wab, wab_b = sb("wab", [128, KD, 32], BF16)
    RB = {n: sb("RB_" + n, [128, TT]) for n in ("lnb", "beta", "G", "lsk", "lsq", "R", "P", "Q", "Kd")}
    RB["t"] = RB["lsq"]
    RB["E"] = RB["lsq"]
    RB["g"] = RB["Kd"]
    RB["nEP"] = RB["lnb"]
    RB["nR"] = RB["lsk"]
    stk = [sb("stk%d" % i, [128, TT]) for i in range(2)]
    cols = [sb("cols%d" % i, [128, 128]) for i in range(2)]
    ktok, ktok_b = sb("ktok", [128, 128])
    vtok, vtok_b = sb("vtok", [128, 128])
    kd_t, kd_b = sb("kd_t", [128, 128])
    mA, mA_b = sb("mA", [128, 128])
    mB, mB_b = sb("mB", [128, 128])
    NAt, NA_b = sb("NAt", [128, 256])
    NBt, NB_b = sb("NBt", [128, 256])
    tA, tA_b = sb("tA", [128, 128])
    tB, tB_b = sb("tB", [128, 128])
    tY, tY_b = sb("tY", [128, 256])
    mean_t, mean_b = RB["R"]
    rs2, rs2_b = RB["P"]
    aTt, aTt_b = sb("aTt", [128, 128])
    qeT, qeT_b = sb("qeT", [128, 128])
    TbT, TbT_b = sb("TbT", [128, 128])
    TwT, TwT_b = sb("TwT", [128, 128])
    u_t, u_b = sb("u_t", [128, 128])
    nwT, nwT_b = sb("nwT", [128, 128])
    vnew, vnew_b = sb("vnew", [128, 128])
    on_t, on_b = TbT, TbT_b
    sml, sml_b = sb("sml", [128, 8])
    for i in range(2):
        S.op("dve", lambda e, i=i: e.memset(stk[i][0][:], 0.0), writes=[V(stk[i][1], stk[i][0][:])])
    S.op("dve", lambda e: e.memset(abT[:], 0.0), writes=[V(abT_b, abT[:])])
    selm, selm_b = sb("selm", [64, 2, 128])

    def build_sel(h):
        for r in range(2):
            rr = h if r == 0 else 32 + h
            ts(V(selm_b, selm[:, r, :]), V(cst_b, cst[0:64, 1, :]), V(cst_b, cst[0:64, 0, rr:rr + 1]), None, ALU.mult)

    win_v = win_d.rearrange("(k p) n -> p k n", p=128)
    wout_v = wout_d.rearrange("(k p) n -> p k n", p=128)
    QB = 2 * CW

    def proj(col0):
        wt, wb = wload(wa, wa_i, win_v[:, :, col0:col0 + 128], KD, "in%d" % col0)
        p, pb = PS()
        for kc in range(KD):
            mm(V(pb, p[:, 0:TT]), V(wb, wt[:, kc, :]), V(hT_b, hT[:, kc, :]), start=(kc == 0), stop=(kc == KD - 1))
        return V(pb, p[:, 0:TT])

    def rb(n, sl=None):
        t, b = RB[n]
        return V(b, t[:] if sl is None else t[:, sl])

    class DS:
        pass

    def mk_set0():
        d = DS()
        d.qkv = qkv
        d.zs = (zs, zs_b)
        d.RB = RB
        d.stk = stk
        d.cols = cols
        d.t = dict(ktok=(ktok, ktok_b), vtok=(vtok, vtok_b), kd=(kd_t, kd_b), mA=(mA, mA_b), mB=(mB, mB_b),
                   NA=(NAt, NA_b), NB=(NBt, NB_b), tA=(tA, tA_b), tB=(tB, tB_b), tY=(tY, tY_b), aT=(aTt, aTt_b),
                   qeT=(qeT, qeT_b), TbT=(TbT, TbT_b), TwT=(TwT, TwT_b), u=(u_t, u_b), nwT=(nwT, nwT_b),
                   vnew=(vnew, vnew_b), sml=(sml, sml_b))
        return d

    st1_bufs = []
    aT_alias_b = actT[0][1]

    def mk_set1():
        need_a = 11 * TT
        arenaA = actT[0][0][:].rearrange("p k t -> p (k t)").bitcast(F32)
        arenas = [arenaA] + [w[0][:].rearrange("p k n -> p (k n)").bitcast(F32) for w in wdp]
        sizes = [max(KFH, KD) * TT // 2] + [KFH * 64] * len(wdp)
        use_alias = (sizes[0] >= need_a) and (sizes[1] >= 1408) and len(wdp) >= 2 and TT == 512
        pos = [0] * len(arenas)

        def al(name, n, ai):
            if not use_alias:
                t_, b_ = sb("s1_" + name, [128, n])
                return t_[:], b_
            assert pos[ai] + n <= sizes[ai], (name, ai, pos[ai], n, sizes[ai])
            v_ = arenas[ai][:, pos[ai]:pos[ai] + n]
            pos[ai] += n
            b_ = Buf("s1_" + name)
            st1_bufs.append(b_)
            return v_, b_
        d = DS()
        d.qkv = [al("qkv%d" % i, TT, 0) for i in range(3)]
        d.zs = al("zs", TT, 0)
        d.RB = dict(RB)
        for n_ in ("R", "P", "Q", "lsq", "G"):
            d.RB[n_] = al("RB_" + n_, TT, 0)
        d.RB["t"] = d.RB["lsq"]
        d.RB["E"] = d.RB["lsq"]
        d.stk = [al("stk%d" % i, TT, 0) for i in range(2)]
        d.cols = [al("cols%d" % i, 128, 1) for i in range(2)]
        d.t = {}
        for n_ in ("ktok", "vtok", "kd", "mA", "mB"):
            d.t[n_] = al(n_, 128, 1)
        d.t["NA"] = al("NA", 256, 1)
        d.t["NB"] = al("NB", 256, 1)
        d.t["tA"] = al("tA", 128, 2)
        d.t["tB"] = al("tB", 128, 2)
        d.t["tY"] = al("tY", 256, 2)
        for n_ in ("aT", "qeT", "TbT", "TwT", "u", "nwT", "vnew"):
            d.t[n_] = al(n_, 128, 2)
        sm_, smb_ = sb("s1_sml", [128, 8])
        d.t["sml"] = (sm_[:], smb_)
        d.alias = use_alias
        return d

    def inherit(dst, src):
        acc = []
        for b_ in src:
            if b_.w is not None:
                acc.append(b_.w)
            acc.extend(b_.r)
        for b_ in dst:
            b_.r = list(b_.r) + acc

    dsets = [mk_set0(), mk_set1()]
    for i in range(2):
        S.op("dve", lambda e, i=i: e.memset(dsets[1].stk[i][0], 0.0), writes=[V(dsets[1].stk[i][1], dsets[1].stk[i][0])])

    def dn_prep(h, st):
        qkv_, RB_ = st.qkv, st.RB
        zs_, zs_b_ = st.zs

        def rbs(n, sl=None):
            t_, b_ = RB_[n]
            return V(b_, t_[:] if sl is None else t_[:, sl])
        for i3 in range(3):
            pp = proj(QB + i3 * DNW + h * 128)
            ci = i3 * NH + h
            cp(V(pre_b, pre[:, 0:3]), V(ptail[ci][1], ptail[ci][0][:]))
            cp(V(pre_b, pre[:, 3:3 + TT]), pp, eng="act")
            cp(V(ptail[ci][1], ptail[ci][0][:]), V(pre_b, pre[:, TT:TT + 3]))
            for k in range(4):
                wk = V(wshT_b, wshT[:, ci, k:k + 1])
                if k == 0:
                    ts(V(cacc_b, cacc[:]), V(pre_b, pre[:, 0:TT]), wk, None, ALU.mult)
                else:
                    stt(V(cacc_b, cacc[:]), V(pre_b, pre[:, k:k + TT]), wk, V(cacc_b, cacc[:]), ALU.mult, ALU.add)
            act(V(qkv_[i3][1], qkv_[i3][0][:]), V(cacc_b, cacc[:]), AF.Silu)
        pz = proj(QB + 3 * DNW + h * 128)
        act(V(zs_b_, zs_[:]), pz, AF.Silu)
        qT, kT, vT = [V(qkv_[i][1], qkv_[i][0][:]) for i in range(3)]
        pb1, pb1_b = PS()
        build_sel(h)
        mm(V(pb1_b, pb1[:, 0:TT]), V(selm_b, selm[:, 0, :]), V(abT_b, abT[:]))
        pa1, pa1_b = PS()
        mm(V(pa1_b, pa1[:, 0:TT]), V(selm_b, selm[:, 1, :]), V(abT_b, abT[:]))
        act(rbs("t"), V(pb1_b, pb1[:, 0:TT]), AF.Exp, scale=-1.0)
        act(rbs("lnb"), rbs("t"), AF.Ln, bias=1.0)
        ts(rbs("lnb"), rbs("lnb"), -1.0, None, ALU.mult)
        act(rbs("beta"), rbs("lnb"), AF.Exp)
        act(rbs("t"), V(pa1_b, pa1[:, 0:TT]), AF.Exp, bias=V(hvec_b, hvec[:, 1, h:h + 1]))
        act(rbs("g"), rbs("t"), AF.Ln, bias=1.0)
        ts(rbs("g"), rbs("g"), V(negA_b, negA[:, h:h + 1]), None, ALU.mult)
        S.op("dve", lambda e: e.tensor_tensor_scan(RB_["G"][0][:], rmask[:], RB_["g"][0][:], 0.0, ALU.mult, ALU.add),
             reads=[V(rmask_b, rmask[:]), rbs("g")], writes=[rbs("G")])
        for (src, dst) in ((kT, "lsk"), (qT, "lsq")):
            act(V(cacc_b, cacc[:]), src, AF.Square)
            pss, pss_b = PS()
            mm(V(pss_b, pss[:, 0:TT]), ONES, V(cacc_b, cacc[:]))
            act(rbs(dst), V(pss_b, pss[:, 0:TT]), AF.Ln, bias=EPS)
        stt(rbs("R"), rbs("lsk"), 0.5, rbs("G"), ALU.mult, ALU.add)
        stt(rbs("P"), rbs("lsk"), -0.5, rbs("G"), ALU.mult, ALU.add)
        tt(rbs("P"), rbs("P"), rbs("lnb"), ALU.add)
        stt(rbs("Q"), rbs("lsq"), -0.5, rbs("G"), ALU.mult, ALU.add)
        ts(rbs("Q"), rbs("Q"), float(np.log(128.0 ** -0.5)), None, ALU.add)
        act(rbs("E"), rbs("Q"), AF.Exp)
        act(rbs("nEP"), rbs("P"), AF.Exp)
        ts(rbs("nEP"), rbs("nEP"), -1.0, None, ALU.mult)
        ts(rbs("nR"), rbs("R"), -1.0, None, ALU.mult)
        for blk in range(NB):
            bs = slice(blk * 128, (blk + 1) * 128)
            last = blk * 128 + 127
            glast = V(RB_["G"][1], RB_["G"][0][:, last:last + 1])
            act(rbs("Kd", bs), rbs("nR", bs), AF.Exp, bias=glast)
        for (pi, nme) in ((0, "nR"), (32, "P"), (64, "beta"), (96, "nEP")):
            cp(V(st.stk[0][1], st.stk[0][0][pi:pi + 1, :]), V(RB_[nme][1], RB_[nme][0][pi:pi + 1, :]))
        cp(V(st.stk[1][1], st.stk[1][0][0:1, :]), V(RB_["Kd"][1], RB_["Kd"][0][0:1, :]))

    def dn_blocks(h, st):
        qkv_, RB_ = st.qkv, st.RB
        zs_, zs_b_ = st.zs
        T_ = st.t

        def rbs(n, sl=None):
            t_, b_ = RB_[n]
            return V(b_, t_[:] if sl is None else t_[:, sl])

        def tv(n, sl=None):
            t_, b_ = T_[n]
            return V(b_, t_[:] if sl is None else t_[:, sl])
        Sb, Sbb = Sst[h]
        SV = V(Sbb, Sb[:])
        for blk in range(NB):
            bs = slice(blk * 128, (blk + 1) * 128)
            last = blk * 128 + 127
            for i in range(2):
                pc, pcb = PS()
                tr(V(pcb, pc[:, 0:128]), V(st.stk[i][1], st.stk[i][0][:, bs]))
                cp(V(st.cols[i][1], st.cols[i][0][:]), V(pcb, pc[:, 0:128]), eng="act")
            c0, c0b = st.cols[0]
            cnR = V(c0b, c0[:, 0:1])
            cP = V(c0b, c0[:, 32:33])
            cbeta = V(c0b, c0[:, 64:65])
            cnEP = V(c0b, c0[:, 96:97])
            cKd = V(st.cols[1][1], st.cols[1][0][:, 0:1])
            qTb = V(qkv_[0][1], qkv_[0][0][:, bs])
            kTb = V(qkv_[1][1], qkv_[1][0][:, bs])
            vTb = V(qkv_[2][1], qkv_[2][0][:, bs])
            ptk, ptk_b = PS()
            tr(V(ptk_b, ptk[:, 0:128]), kTb)
            tr(V(ptk_b, ptk[:, 128:256]), vTb)
            pkk, pkk_b = PS()
            mm(V(pkk_b, pkk[:, 0:128]), kTb, kTb)
            mm(V(pkk_b, pkk[:, 128:256]), kTb, qTb)
            yield
            cp(tv("ktok"), V(ptk_b, ptk[:, 0:128]), eng="act")
            cp(tv("vtok"), V(ptk_b, ptk[:, 128:256]), eng="act")
            ts(tv("kd"), V(ptk_b, ptk[:, 0:128]), cKd, None, ALU.mult)
            tt(tv("mA"), NEGS, rbs("R", bs), ALU.subtract)
            act(tv("mA"), tv("mA"), AF.Exp, bias=cP)
            tt(tv("mB"), NEGST, rbs("P", bs), ALU.add)
            act(tv("mB"), tv("mB"), AF.Exp, bias=cnR)
            tt(tv("aT"), NEGTT, rbs("Q", bs), ALU.add)
            act(tv("aT"), tv("aT"), AF.Exp, bias=cnR)
            yield
            stt(tv("mA"), V(pkk_b, pkk[:, 0:128]), -1.0, tv("mA"), ALU.mult, ALU.mult)
            stt(tv("mB"), V(pkk_b, pkk[:, 0:128]), -1.0, tv("mB"), ALU.mult, ALU.mult)
            tt(tv("aT"), V(pkk_b, pkk[:, 128:256]), tv("aT"), ALU.mult)
            tt(tv("qeT"), qTb, rbs("E", bs), ALU.mult)
            XTv = tv("NA", slice(128, 256))
            XUv = tv("NB", slice(128, 256))
            tt(tv("NA", slice(0, 128)), tv("mA"), BD16, ALU.mult)
            tt(tv("NB", slice(0, 128)), tv("mB"), BD16, ALU.mult)
            cp(XTv, IDN)
            cp(XUv, IDN)
            yield
            for lev in range(4):
                p1, p1b = PS()
                p2, p2b = PS()
                if lev == 3:
                    mm(V(p1b, p1[:, 128:256]), tv("NA", slice(0, 128)), XUv)
                    mm(V(p2b, p2[:, 128:256]), tv("NB", slice(0, 128)), XTv)
                    yield
                else:
                    mm(V(p1b, p1[:, 0:256]), tv("NA", slice(0, 128)), tv("NB", slice(0, 256)))
                    mm(V(p2b, p2[:, 0:256]), tv("NB", slice(0, 128)), tv("NA", slice(0, 256)))
                    yield
                    cp(tv("NB", slice(0, 128)), V(p1b, p1[:, 0:128]), eng="act")
                    cp(tv("NA", slice(0, 128)), V(p2b, p2[:, 0:128]), eng="act")
                tt(XUv, V(p1b, p1[:, 128:256]), XUv, ALU.add)
                tt(XTv, V(p2b, p2[:, 128:256]), XTv, ALU.add)
                yield
            for li in range(3):
                lastm = (li == 2)
                tt(tv("tA"), tv("mA"), LLm[li], ALU.mult)
                pY, pYb = PS()
                mm(V(pYb, pY[:, 0:128]), tv("tA"), XUv)
                if not lastm:
                    tt(tv("tB"), tv("mB"), URm[li], ALU.mult)
                    mm(V(pYb, pY[:, 128:256]), tv("tB"), XTv)
                    yield
                    cp(tv("tY", slice(0, 256)), V(pYb, pY[:, 0:256]), eng="act")
                else:
                    yield
                    cp(tv("tY", slice(0, 128)), V(pYb, pY[:, 0:128]), eng="act")
                pZ, pZb = PS()
                mm(V(pZb, pZ[:, 0:128]), XTv, tv("tY", slice(0, 128)))
                if not lastm:
                    mm(V(pZb, pZ[:, 128:256]), XUv, tv("tY", slice(128, 256)))
                yield
                tt(XUv, V(pZb, pZ[:, 0:128]), XUv, ALU.add)
                if not lastm:
                    tt(XTv, V(pZb, pZ[:, 128:256]), XTv, ALU.add)
            ts(tv("TbT"), XUv, cbeta, None, ALU.mult)
            ts(tv("TwT"), XUv, cnEP, None, ALU.mult)
            pu_, pu_b = PS()
            mm(V(pu_b, pu_[:, 0:128]), tv("TbT"), tv("vtok"))
            mm(V(pu_b, pu_[:, 128:256]), tv("ktok"), tv("TwT"))
            yield
            cp(tv("u"), V(pu_b, pu_[:, 0:128]), eng="act")
            cp(tv("nwT"), V(pu_b, pu_[:, 128:256]), eng="act")
            pv, pvb = PS()
            mm(V(pvb, pv[:, 0:128]), tv("nwT"), SV)
            yield
            tt(tv("vnew"), V(pvb, pv[:, 0:128]), tv("u"), ALU.add)
            po, pob = PS()
            mm(V(pob, po[:, 0:128]), tv("qeT"), SV, start=True, stop=False)
            mm(V(pob, po[:, 0:128]), tv("aT"), tv("vnew"), start=False, stop=True)
            pds, pdsb = PS()
            mm(V(pdsb, pds[:, 0:128]), tv("kd"), tv("vnew"))
            act(tv("sml", slice(0, 1)), V(RB_["G"][1], RB_["G"][0][:, last:last + 1]), AF.Exp)
            yield
            stt(SV, SV, tv("sml", slice(0, 1)), V(pdsb, pds[:, 0:128]), ALU.mult, ALU.add)
            act(tv("TbT"), V(pob, po[:, 0:128]), AF.Square, accum=tv("sml", slice(1, 2)))
            act(tv("sml", slice(2, 3)), tv("sml", slice(1, 2)), AF.Sqrt, bias=EPS, scale=1.0 / 128)
            smt, smb = T_["sml"]
            S.op("dve", lambda e, smt=smt: e.reciprocal(smt[:, 3:4], smt[:, 2:3]), reads=[tv("sml", slice(2, 3))], writes=[tv("sml", slice(3, 4))])
            ts(tv("TbT"), V(pob, po[:, 0:128]), tv("sml", slice(3, 4)), None, ALU.mult)
            pt2, pt2b = PS()
            tr(V(pt2b, pt2[:, 0:128]), tv("TbT"))
            yield
            stt(V(yT_b, yT[:, KC + h, bs]), V(pt2b, pt2[:, 0:128]), V(onw_b, onw[:, 0:1]), V(zs_b_, zs_[:, bs]), ALU.mult, ALU.mult)

    def mixer():
        for j in range(KC):
            pa = proj(j * 128)
            pgt = proj(CW + j * 128)
            act(V(sg_b, sg[:]), pgt, AF.Sigmoid)
            cp(V(glu_b, glu[:, 0:30]), V(gtail[j][1], gtail[j][0][:]))
            tt(V(glu_b, glu[:, 30:30 + TT]), pa, V(sg_b, sg[:]), ALU.mult)
            cp(V(gtail[j][1], gtail[j][0][:]), V(glu_b, glu[:, TT:TT + 30]))
            for k in range(31):
                wk = V(wdwT_b, wdwT[:, j, k:k + 1])
                if k == 0:
                    ts(V(cacc_b, cacc[:]), V(glu_b, glu[:, 0:TT]), wk, V(cvec_b, cvec[:, 0, j:j + 1]), ALU.mult, ALU.add)
                elif k < 30:
                    stt(V(cacc_b, cacc[:]), V(glu_b, glu[:, k:k + TT]), wk, V(cacc_b, cacc[:]), ALU.mult, ALU.add)
                else:
                    stt(V(ypre_b, ypre[:, j, :]), V(glu_b, glu[:, k:k + TT]), wk, V(cacc_b, cacc[:]), ALU.mult, ALU.add)
        pmn, pmn_b = PS()
        pvr, pvr_b = PS()
        for j in range(KC):
            mm(V(pmn_b, pmn[:, 0:TT]), ONES, V(ypre_b, ypre[:, j, :]), start=(j == 0), stop=(j == KC - 1))
        for j in range(KC):
            yq, yqb = ysq2[j % 2]
            act(V(yqb, yq[:]), V(ypre_b, ypre[:, j, :]), AF.Square)
            mm(V(pvr_b, pvr[:, 0:TT]), ONES, V(yqb, yq[:]), start=(j == 0), stop=(j == KC - 1), sig=True)
        ts(V(mean_b, mean_t[:]), V(pmn_b, pmn[:, 0:TT]), 1.0 / CW, None, ALU.mult)
        tt(V(rs2_b, rs2[:]), V(mean_b, mean_t[:]), V(mean_b, mean_t[:]), ALU.mult)
        stt(V(rs2_b, rs2[:]), V(pvr_b, pvr[:, 0:TT]), 1.0 / CW, V(rs2_b, rs2[:]), ALU.mult, ALU.subtract)
        act(V(rs2_b, rs2[:]), V(rs2_b, rs2[:]), AF.Sqrt, bias=EPS, scale=1.0)
        S.op("dve", lambda e: e.reciprocal(rs2[:], rs2[:]), reads=[V(rs2_b, rs2[:])], writes=[V(rs2_b, rs2[:])])
        for j in range(KC):
            tt(V(cacc_b, cacc[:]), V(ypre_b, ypre[:, j, :]), V(mean_b, mean_t[:]), ALU.subtract)
            tt(V(cacc_b, cacc[:]), V(cacc_b, cacc[:]), V(rs2_b, rs2[:]), ALU.mult)
            act(V(yT_b, yT[:, j, :]), V(cacc_b, cacc[:]), AF.Silu, bias=V(cvec_b, cvec[:, 2, j:j + 1]),
                scale=V(cvec_b, cvec[:, 1, j:j + 1]))

        if CUT <= 1:
            return
        S.dma("pool", wab[:, :, 0:NH], win_v[:, :, QB + 4 * DNW:QB + 4 * DNW + NH], reads=[], writes=[V(wab_b, wab[:])], key=wab_b, slow=True)
        S.dma("pool", wab[:, :, 16:16 + NH], win_v[:, :, QB + 4 * DNW + NH:QB + 4 * DNW + 2 * NH], reads=[], writes=[V(wab_b, wab[:])], key=wab_b, slow=True)
        pab, pab_b = PS()
        for kc in range(KD):
            mm(V(pab_b, pab[0:NH, 0:TT]), V(wab_b, wab[:, kc, 0:NH]), V(hT_b, hT[:, kc, :]), start=(kc == 0), stop=(kc == KD - 1))
        cp(V(abT_b, abT[0:NH, :]), V(pab_b, pab[0:NH, 0:TT]))
        pab2, pab2_b = PS()
        for kc in range(KD):
            mm(V(pab2_b, pab2[32:32 + NH, 0:TT]), V(wab_b, wab[:, kc, 16:16 + NH]), V(hT_b, hT[:, kc, :]), start=(kc == 0), stop=(kc == KD - 1))
        cp(V(abT_b, abT[32:32 + NH, :]), V(pab2_b, pab2[32:32 + NH, 0:TT]))

        if CUT <= 2:
            return
        inherit([bb for bb in st1_bufs], [aT_alias_b] + [w[1] for w in wdp])
        for hp in range(0, NH, 2):
            hs = [hh for hh in (hp, hp + 1) if hh < NH]
            for i, hh in enumerate(hs):
                dn_prep(hh, dsets[i])
            gens = [dn_blocks(hh, dsets[i]) for i, hh in enumerate(hs)]
            while gens:
                for g in list(gens):
                    try:
                        next(g)
                    except StopIteration:
                        gens.remove(g)
        inherit([aT_alias_b] + [w[1] for w in wdp], [bb for bb in st1_bufs])

        if CUT <= 11:
            return
        for m in range(KD):
            wt, wb = wload(wa, wa_i, wout_v[:, :, m * 128:(m + 1) * 128], KY, "out%d" % m)
            p, pb = PS()
            for kc in range(KY):
                mm(V(pb, p[:, 0:TT]), V(wb, wt[:, kc, :]), V(yT_b, yT[:, kc, :]), start=(kc == 0), stop=(kc == KY - 1))
            stt(V(xT_b, xT[:, m, :]), V(pb, p[:, 0:TT]), V(Gcoef_b, Gcoef[:, 1, m:m + 1]), V(xT_b, xT[:, m, :]), ALU.mult, ALU.add)

    out_bufs = []
    for t in range(NT):
        cur_tile[0] = t
        for tb in range(NB):
            xt, xb = xin[tb % 2]
            r0 = t * TT + tb * 128
            ld(V(xb, xt[:]), x_d[r0:r0 + 128, :])
            for k0 in range(0, KD, 4):
                p, pb = PS()
                nk = min(4, KD - k0)
                for kk in range(nk):
                    tr(V(pb, p[:, kk * 128:(kk + 1) * 128]), V(xb, xt[:, (k0 + kk) * 128:(k0 + kk + 1) * 128]))
                S.op("dve", lambda e, p=p, k0=k0, nk=nk, tb=tb: e.tensor_copy(
                    xT[:, k0:k0 + nk, tb * 128:(tb + 1) * 128], p[:, 0:nk * 128].rearrange("p (k t) -> p k t", k=nk)),
                    reads=[V(pb, p[:])], writes=[V(xT_b, xT[:])])
        rms_mod(0)
        ffn(0, 0)
        if stop != "ffn1":
            rms_mod(1)
            mixer()
            if stop != "mixer":
                rms_mod(2)
                ffn(1, 2)
        if stop is None:
            rms_rstd()
            for kc in range(KD):
                tt(V(tmpn_b, tmpn[:]), V(xT_b, xT[:, kc, :]), V(rstd_b, rstd[:]), ALU.mult)
                ts(V(xT_b, xT[:, kc, :]), V(tmpn_b, tmpn[:]), V(normT_b, normT[:, 3, kc:kc + 1]), None, ALU.mult)
        for tb in range(NB):
            xt, xb = xin[tb % 2]
            r0 = t * TT + tb * 128
            for k0 in range(0, KD, 4):
                p, pb = PS()
                nk = min(4, KD - k0)
                for kk in range(nk):
                    tr(V(pb, p[:, kk * 128:(kk + 1) * 128]), V(xT_b, xT[:, k0 + kk, tb * 128:(tb + 1) * 128]))
                cp(V(xb, xt[:, k0 * 128:(k0 + nk) * 128]), V(pb, p[:, 0:nk * 128]), eng="act")
            S.dma("sp", y_d[r0:r0 + 128, :], xt[:], reads=[V(xb, xt[:])], writes=[], key=xb)
            if xb not in out_bufs:
                out_bufs.append(xb)
    S.emit(final_waits=out_bufs)
    es.close()
    return nc


def host_inputs(cfg, b, x, c, w_ada, b_ada, ffn1_norm, ffn1_wg, ffn1_wu, ffn1_wd, mix_norm, w_in,
                w_dw, b_dw, conv_ln_w, conv_ln_b, w_short, a_log, dt_bias, dn_norm_w, w_out,
                ffn2_norm, ffn2_wg, ffn2_wu, ffn2_wd, final_norm):
    KD, KC, NH, TT = cfg["KD"], cfg["KC"], cfg["NH"], cfg["TT"]
    f = np.float32
    A = np.ascontiguousarray

    def fm(v, k):
        return A(np.asarray(v, f).reshape(k, 128).T)

    consts = np.zeros((128, 12, 128), f)
    idx = np.arange(128)
    consts[:, 0, :] = np.eye(128, dtype=f)
    consts[:, 1, :] = 1.0
    consts[:, 2, :] = np.where(idx[:, None] > idx[None, :], 0.0, NEG)
    consts[:, 3, :] = np.where(idx[None, :] > idx[:, None], 0.0, NEG)
    consts[:, 4, :] = np.where(idx[None, :] >= idx[:, None], 0.0, NEG)
    consts[:, 5, :] = (idx[:, None] // 16 == idx[None, :] // 16)
    for i, bsz in enumerate((16, 32, 64)):
        ll = ((idx[:, None] // (2 * bsz) == idx[None, :] // (2 * bsz)) & (idx[:, None] % (2 * bsz) >= bsz)
              & (idx[None, :] % (2 * bsz) < bsz)).astype(f)
        consts[:, 6 + i, :] = ll
        consts[:, 9 + i, :] = ll.T
    rmask = np.ones((128, TT), f)
    rmask[:, ::128] = 0.0
    return {
        "x": A(np.asarray(x[b], f)),
        "cT": fm(c[b], KD),
        "w_ada": A(np.asarray(w_ada[0], f)),
        "b_adaT": fm(b_ada[0], 9 * KD),
        "normT": A(np.stack([fm(ffn1_norm[0], KD), fm(mix_norm[0], KD), fm(ffn2_norm[0], KD), fm(final_norm, KD)], axis=1)),
        "ffn1_wg": A(np.asarray(ffn1_wg[0], f)), "ffn1_wu": A(np.asarray(ffn1_wu[0], f)), "ffn1_wd": A(np.asarray(ffn1_wd[0], f)),
        "ffn2_wg": A(np.asarray(ffn2_wg[0], f)), "ffn2_wu": A(np.asarray(ffn2_wu[0], f)), "ffn2_wd": A(np.asarray(ffn2_wd[0], f)),
        "w_in": A(np.asarray(w_in[0], f)),
        "w_out": A(np.asarray(w_out[0], f)),
        "w_dwT": A(np.asarray(w_dw[0], f).T.reshape(KC, 128, 31).transpose(1, 0, 2)),
        "cvecT": A(np.stack([fm(b_dw[0], KC), fm(conv_ln_w[0], KC), fm(conv_ln_b[0], KC)], axis=1)),
        "w_shT": A(np.asarray(w_short[0], f).T.reshape(3 * NH, 128, 4).transpose(1, 0, 2)),
        "hvec": A(np.broadcast_to(np.stack([np.asarray(a_log[0], f), np.asarray(dt_bias[0], f)], axis=0)[None], (128, 2, NH))),
        "onwT": A(np.asarray(dn_norm_w[0], f).reshape(128, 1)),
        "consts": consts,
        "rmask": rmask,
    }


_NC_CACHE = {}


def kernel(**inputs):
    cfg = FULL
    B = inputs["x"].shape[0]
    if "nc" not in _NC_CACHE:
        _NC_CACHE["nc"] = build(cfg)
    nc = _NC_CACHE["nc"]
    n = 8
    in_maps = [host_inputs(cfg, i % B, **inputs) for i in range(n)]
    res = run_bass_kernel_spmd(nc, in_maps, core_ids=list(range(n)))
    out = np.stack([res.results[i]["y"] for i in range(B)], axis=0)
    return out.astype(np.float32)
```

```python
import numpy as np
import os
CUT = int(os.environ.get('KCUT', '99'))
from contextlib import ExitStack
import concourse.bass as bass
import concourse.mybir as mybir
from concourse.bass_utils import run_bass_kernel_spmd

F32 = mybir.dt.float32
BF16 = mybir.dt.bfloat16
AF = mybir.ActivationFunctionType
ALU = mybir.AluOpType
NEG = -1.0e9
EPS = 1e-6


class Buf:
    __slots__ = ("name", "w", "r", "dsem", "psum")

    def __init__(self, name, psum=False):
        self.name = name
        self.psum = psum
        self.w = None
        self.r = []
        self.dsem = None


class V:
    __slots__ = ("b", "ap")

    def __init__(self, b, ap):
        self.b = b
        self.ap = ap


class DSem:
    def __init__(self, sem):
        self.sem = sem
        self.count = 0


class Rec:
    __slots__ = ("eng", "fn", "deps", "sig", "idx", "dma", "dval", "dinc")

    def __init__(self, eng, fn, deps, sig, dma=None):
        self.eng = eng
        self.fn = fn
        self.deps = deps
        self.sig = sig
        self.dma = dma
        self.dval = 0
        self.dinc = 16
        self.idx = -1


class Sched:
    ENGS = ("pe", "dve", "act", "pool", "sp")

    def __init__(self, nc, es):
        self.nc = nc
        self.es = es
        self.q = {e: [] for e in self.ENGS}
        self.esem = {e: es.enter_context(nc.semaphore("s_" + e)) for e in ("pe", "dve", "act", "pool")}
        self.nsem = 0

    def _deps(self, eng, reads, writes, is_dma):
        deps = []
        for v in reads:
            w = v.b.w
            if w is not None:
                deps.append((w, "raw"))
            if v.b.psum:
                for r in v.b.r:
                    if r.eng != eng:
                        deps.append((r, "rr"))
        for v in writes:
            b = v.b
            if b.w is not None:
                deps.append((b.w, "waw"))
            for r in b.r:
                deps.append((r, "war"))
        out = []
        for (d, kind) in deps:
            if d.dma is None and not is_dma and d.eng == eng:
                if eng == "pe":
                    continue
            if d.dma is not None:
                out.append((d, d.dma.count))
            else:
                out.append((d, None))
        return out

    def _commit(self, rec, reads, writes):
        for v in writes:
            v.b.w = rec
            v.b.r = []
        for v in reads:
            if v.b.w is not rec:
                v.b.r.append(rec)

    def op(self, eng, fn, reads=(), writes=(), sig=True):
        rec = Rec(eng, fn, self._deps(eng, reads, writes, False), sig)
        rec.idx = len(self.q[eng])
        self.q[eng].append(rec)
        self._commit(rec, reads, writes)
        return rec

    def dma(self, queue, out, in_, reads, writes, key, slow=False):
        if key.dsem is None:
            key.dsem = DSem(self.es.enter_context(self.nc.semaphore("d%d" % self.nsem)))
            self.nsem += 1
        ds = key.dsem
        rec = Rec(queue, lambda e: e.dma_start(out=out, in_=in_, allow_slow_non_contiguous=slow), self._deps(queue, reads, writes, True), True, dma=ds)
        ds.count += 16
        rec.dval = ds.count
        rec.idx = len(self.q[queue])
        self.q[queue].append(rec)
        self._commit(rec, reads, writes)
        return rec

    def coll(self, ins_ap, outs_ap, groups, reads, writes, key):
        if key.dsem is None:
            key.dsem = DSem(self.es.enter_context(self.nc.semaphore("c%d" % self.nsem)))
            self.nsem += 1
        ds = key.dsem
        rec = Rec("pool", lambda e: e.collective_compute("AllGather", ALU.bypass, replica_groups=groups,
                                                         ins=[ins_ap], outs=[outs_ap]),
                  self._deps("pool", reads, writes, True), True, dma=ds)
        rec.dinc = 1
        ds.count += 1
        rec.dval = ds.count
        rec.idx = len(self.q["pool"])
        self.q["pool"].append(rec)
        self._commit(rec, reads, writes)
        return rec

    def emit(self, final_waits=()):
        nc = self.nc
        cnt = {}
        for e in ("pe", "dve", "act", "pool"):
            arr = []
            c = 0
            for r in self.q[e]:
                if r.dma is None and r.sig:
                    c += 1
                arr.append(c)
            res = [0] * len(arr)
            nxt = None
            for i in range(len(arr) - 1, -1, -1):
                r = self.q[e][i]
                if r.dma is None and r.sig:
                    nxt = arr[i]
                res[i] = nxt
            cnt[e] = res
        handles = {"pe": nc.tensor, "dve": nc.vector, "act": nc.scalar, "pool": nc.gpsimd, "sp": nc.sync}
        with nc.Block() as block:
            def run(e, h):
                waited = {}
                for r in self.q[e]:
                    for (d, dv) in r.deps:
                        if d.dma is not None:
                            sem, val = d.dma.sem, dv
                        else:
                            val = cnt[d.eng][d.idx]
                            assert val is not None, "dep on op with no later signal"
                            sem = self.esem[d.eng]
                        k = id(sem)
                        if waited.get(k, 0) >= val:
                            continue
                        waited[k] = val
                        h.wait_ge(sem, val)
                    ins = r.fn(h)
                    if r.dma is not None:
                        ins.then_inc(r.dma.sem, r.dinc)
                    elif r.sig:
                        ins.then_inc(self.esem[e], 1)
                if e == "sp":
                    for b in final_waits:
                        h.wait_ge(b.dsem.sem, b.dsem.count)

            @block.tensor
            def _(h):
                run("pe", h)

            @block.vector
            def _(h):
                run("dve", h)

            @block.scalar
            def _(h):
                run("act", h)

            @block.gpsimd
            def _(h):
                run("pool", h)

            @block.sync
            def _(h):
                run("sp", h)


def make_cfg(D, DFF, CW, NH, T, split=0):
    f = 1 + split
    CWl, NHl = CW // f, NH // f
    return dict(D=D, DFF=DFF, CW=CWl, NH=NHl, T=T, KD=D // 128, KF=DFF // 128, KC=CWl // 128, split=split,
                CWg=CW, NHg=NH, TT=min(512, T), INC=2 * CWl + 4 * NHl * 128 + 2 * NHl)


FULL = make_cfg(2048, 5632, 1024, 8, 4096, split=1)


def build(cfg, stop=None):
    D, DFF, CW, NH, T = cfg["D"], cfg["DFF"], cfg["CW"], cfg["NH"], cfg["T"]
    KD, KF, KC, TT, INC = cfg["KD"], cfg["KF"], cfg["KC"], cfg["TT"], cfg["INC"]
    NT = T // TT
    NB = TT // 128
    SPLIT = cfg.get("split", 0)
    KCg = cfg["CWg"] // 128
    NHg = cfg["NHg"]
    KY = KCg + NHg
    DNW = NH * 128
    nc = bass.Bass("TRN2", target_bir_lowering=False)
    es = ExitStack()
    S = Sched(nc, es)

    def din(name, shape, dt=F32):
        return nc.dram_tensor(name, list(shape), dt, kind="ExternalInput").ap()

    x_d = din("x", [T, D])
    y_d = nc.dram_tensor("y", [T, D], F32, kind="ExternalOutput").ap()
    cT_d = din("cT", [128, KD])
    wada_d = din("w_ada", [D, 9 * D])
    badaT_d = din("b_adaT", [128, 9 * KD])
    normT_d = din("normT", [128, 4, KD])
    wg_d = [din("ffn1_wg", [D, DFF]), din("ffn2_wg", [D, DFF])]
    wu_d = [din("ffn1_wu", [D, DFF]), din("ffn2_wu", [D, DFF])]
    wd_d = [din("ffn1_wd", [DFF, D]), din("ffn2_wd", [DFF, D])]
    win_d = din("w_in", [D, INC])
    wout_d = din("w_out", [D, D])
    wdwT_d = din("w_dwT", [128, KC, 31])
    cvec_d = din("cvecT", [128, 3, KC])
    lnT_d = din("lnT", [128, 2, KCg])
    wshT_d = din("w_shT", [128, 3 * NH, 4])
    hvec_d = din("hvec", [128, 2, NH])
    onw_d = din("onwT", [128, 1])
    cst_d = din("consts", [128, 12, 128])
    rmask_d = din("rmask", [128, TT])

    SBTOT = [0]

    def sb(name, shape, dt=F32):
        t = es.enter_context(nc.sbuf_tensor("sb_" + name, list(shape), dt))
        nbytes = int(np.prod(shape[1:])) * (2 if dt == BF16 else 4)
        SBTOT[0] += nbytes
        if os.environ.get("KSB"):
            print("SB", name, nbytes, SBTOT[0])
        return t, Buf(name)

    cst, cst_b = sb("cst", [128, 12, 128])
    rmask, rmask_b = sb("rmask", [128, TT])
    ones_bf, ones_bf_b = sb("ones_bf", [128, 128], BF16)
    normT, normT_b = sb("normT", [128, 4, KD])
    wdwT, wdwT_b = sb("wdwT", [128, KC, 31])
    cvec, cvec_b = sb("cvec", [128, 3, KC])
    lnT, lnT_b = sb("lnT", [128, 2, KCg])
    wshT, wshT_b = sb("wshT", [128, 3 * NH, 4])
    hvec, hvec_b = sb("hvec", [128, 2, NH])
    negA, negA_b = sb("negA", [128, NH])
    onw, onw_b = sb("onw", [128, 1])
    modsT, modsT_b = sb("modsT", [128, 9 * KD])
    Acoef, Acoef_b = sb("Acoef", [128, 3, KD])
    Gcoef, Gcoef_b = sb("Gcoef", [128, 3, KD])
    finw_b = normT_b
    xT, xT_b = sb("xT", [128, KD, TT])
    hT, hT_b = sb("hT", [128, KD, TT], BF16)
    KFH = KF // 2
    actT = [sb("actT%d" % i, [128, max(KFH, KD), TT], BF16) for i in range(1)]
    yT, yT_b = sb("yT", [128, KY, TT], BF16)
    sq, sq_b = actT[0]
    rstd, rstd_b = sb("rstd", [128, TT])
    xin_all, xin_all_b = sb("xin_all", [128, 2, D])
    xin = [(xin_all[:, i, :], xin_all_b) for i in range(2)]
    Sst = [sb("Sst%d" % h, [128, 128]) for h in range(NH)]
    gtail = [sb("gtail%d" % j, [128, 30]) for j in range(KC)]
    ptail = [sb("ptail%d" % j, [128, 3]) for j in range(3 * NH)]
    NWA, NWD = 4, 2
    wa = [sb("wa%d" % i, [128, KD, 128], BF16) for i in range(NWA)]
    wdp = [sb("wd%d" % i, [128, KF // 2, 128], BF16) for i in range(NWD)]
    wa_i = [0]
    wd_i = [0]
    psum = []
    for i in range(8):
        t = es.enter_context(nc.psum_tensor("ps%d" % i, [128, 512], F32))
        psum.append((t, Buf("ps%d" % i, psum=True)))
    ps_i = [0]

    def PS():
        t, b = psum[ps_i[0] % 8]
        ps_i[0] += 1
        return t, b

    IDN = V(cst_b, cst[:, 0, :])
    ONES = V(cst_b, cst[:, 1, :])
    NEGS = V(cst_b, cst[:, 2, :])
    NEGST = V(cst_b, cst[:, 3, :])
    NEGTT = V(cst_b, cst[:, 4, :])
    BD16 = V(cst_b, cst[:, 5, :])
    LLm = [V(cst_b, cst[:, 6 + i, :]) for i in range(3)]
    URm = [V(cst_b, cst[:, 9 + i, :]) for i in range(3)]

    def mm(out, lhsT, rhs, start=True, stop=True, sig=None):
        if sig is None:
            sig = stop
        return S.op("pe", lambda e: e.matmul(out.ap, lhsT.ap, rhs.ap, start=start, stop=stop),
                    reads=[lhsT, rhs], writes=[out], sig=sig)

    def tr(out, in_):
        return S.op("pe", lambda e: e.transpose(out.ap, in_.ap, IDN.ap), reads=[in_, IDN], writes=[out])

    def act(out, in_, func, bias=None, scale=None, accum=None, eng="act"):
        rd = [in_]
        kw = {}
        if bias is not None:
            if isinstance(bias, V):
                rd.append(bias)
                kw["bias"] = bias.ap
            else:
                kw["bias"] = float(bias)
        if scale is not None:
            if isinstance(scale, V):
                rd.append(scale)
                kw["scale"] = scale.ap
            else:
                kw["scale"] = float(scale)
        wr = [out]
        if accum is not None:
            kw["accum_out"] = accum.ap
            wr.append(accum)
        return S.op("act", lambda e: e.activation(out.ap, in_.ap, func, **kw), reads=rd, writes=wr)

    def tt(out, a, b, op, eng="dve"):
        return S.op(eng, lambda e: e.tensor_tensor(out.ap, a.ap, b.ap, op), reads=[a, b], writes=[out])

    def ts(out, a, s1, s2, op0, op1=None, eng="dve"):
        rd = [a]
        s1a = s1.ap if isinstance(s1, V) else s1
        s2a = s2.ap if isinstance(s2, V) else s2
        if isinstance(s1, V):
            rd.append(s1)
        if isinstance(s2, V):
            rd.append(s2)
        if op1 is None:
            return S.op(eng, lambda e: e.tensor_scalar(out.ap, a.ap, s1a, None, op0), reads=rd, writes=[out])
        return S.op(eng, lambda e: e.tensor_scalar(out.ap, a.ap, s1a, s2a, op0, op1), reads=rd, writes=[out])

    def stt(out, a, s, b, op0, op1):
        rd = [a, b]
        sa = s.ap if isinstance(s, V) else s
        if isinstance(s, V):
            rd.append(s)
        return S.op("dve", lambda e: e.scalar_tensor_tensor(out.ap, a.ap, sa, b.ap, op0, op1), reads=rd, writes=[out])

    def cp(out, in_, eng="dve"):
        if eng == "act":
            return act(out, in_, AF.Copy)
        return S.op(eng, lambda e: e.tensor_copy(out.ap, in_.ap), reads=[in_], writes=[out])

    def ld(out, src_ap, queue="sp"):
        return S.dma(queue, out.ap, src_ap, reads=[], writes=[out], key=out.b)

    ld(V(cst_b, cst[:]), cst_d)
    ld(V(rmask_b, rmask[:]), rmask_d)
    ld(V(normT_b, normT[:]), normT_d)
    ld(V(wdwT_b, wdwT[:]), wdwT_d)
    ld(V(cvec_b, cvec[:]), cvec_d)
    ld(V(lnT_b, lnT[:]), lnT_d)
    ld(V(wshT_b, wshT[:]), wshT_d)
    ld(V(hvec_b, hvec[:]), hvec_d)
    ld(V(onw_b, onw[:]), onw_d)
    cp(V(ones_bf_b, ones_bf[:]), ONES)
    act(V(negA_b, negA[:]), V(hvec_b, hvec[:, 0, :]), AF.Exp)
    ts(V(negA_b, negA[:]), V(negA_b, negA[:]), -1.0, None, ALU.mult)
    for h in range(NH):
        S.op("dve", lambda e, h=h: e.memset(Sst[h][0][:], 0.0), writes=[V(Sst[h][1], Sst[h][0][:])])
    for j in range(KC):
        S.op("dve", lambda e, j=j: e.memset(gtail[j][0][:], 0.0), writes=[V(gtail[j][1], gtail[j][0][:])])
    for j in range(3 * NH):
        S.op("dve", lambda e, j=j: e.memset(ptail[j][0][:], 0.0), writes=[V(ptail[j][1], ptail[j][0][:])])

    scT, scT_b = sb("scT", [128, KD])
    badaT, badaT_b = sb("badaT", [128, 9 * KD])
    ld(V(scT_b, scT[:]), cT_d)
    ld(V(badaT_b, badaT[:]), badaT_d)
    act(V(scT_b, scT[:]), V(scT_b, scT[:]), AF.Silu)
    wada_v = wada_d.rearrange("(k p) n -> p k n", p=128)
    NMC = 9 * KD
    pm, pm_b = PS()
    for j in range(NMC):
        wt0, wb = xin[j % 2]
        wt = wt0[:].rearrange("p (k n) -> p k n", k=KD)
        S.dma("sp", wt, wada_v[:, :, j * 128:(j + 1) * 128], reads=[], writes=[V(wb, wt)], key=wb)
        for kc in range(KD):
            mm(V(pm_b, pm[:, j:j + 1]), V(wb, wt[:, kc, :]), V(scT_b, scT[:, kc:kc + 1]),
               start=(kc == 0), stop=(kc == KD - 1))
    tt(V(modsT_b, modsT[:]), V(pm_b, pm[:, 0:NMC]), V(badaT_b, badaT[:]), ALU.add)
    for i in range(3):
        sc = V(modsT_b, modsT[:, (3 * i + 1) * KD:(3 * i + 2) * KD])
        gt = V(modsT_b, modsT[:, (3 * i + 2) * KD:(3 * i + 3) * KD])
        stt(V(Acoef_b, Acoef[:, i, :]), sc, 1.0, V(normT_b, normT[:, i, :]), ALU.add, ALU.mult)
        ts(V(Gcoef_b, Gcoef[:, i, :]), gt, 0.5 if i != 1 else 1.0, None, ALU.mult)

    def shift_col(i, kc):
        return V(modsT_b, modsT[:, 3 * i * KD + kc:3 * i * KD + kc + 1])

    def rms_rstd():
        for kc in range(KD):
            act(V(sq_b, sq[:, kc, :]), V(xT_b, xT[:, kc, :]), AF.Square)
        p, pb = PS()
        for kc in range(KD):
            mm(V(pb, p[:, 0:TT]), V(ones_bf_b, ones_bf[:]), V(sq_b, sq[:, kc, :]), start=(kc == 0), stop=(kc == KD - 1))
        act(V(rstd_b, rstd[:]), V(pb, p[:, 0:TT]), AF.Sqrt, bias=EPS, scale=1.0 / D)
        S.op("dve", lambda e: e.reciprocal(rstd[:], rstd[:]), reads=[V(rstd_b, rstd[:])], writes=[V(rstd_b, rstd[:])])

    tmpn, tmpn_b = sb("tmpn", [128, TT])
    cacc, cacc_b = tmpn, tmpn_b

    def rms_mod(i):
        rms_rstd()
        for kc in range(KD):
            tt(V(tmpn_b, tmpn[:]), V(xT_b, xT[:, kc, :]), V(rstd_b, rstd[:]), ALU.mult)
            act(V(hT_b, hT[:, kc, :]), V(tmpn_b, tmpn[:]), AF.Identity,
                bias=shift_col(i, kc), scale=V(Acoef_b, Acoef[:, i, kc:kc + 1]))

    scr = {}
    cur_tile = [0]

    def wload(dst_pool, idx, src, K, tag):
        t, b = dst_pool[idx[0] % len(dst_pool)]
        idx[0] += 1
        if tag not in scr:
            dt_ = nc.dram_tensor("scr_" + tag, [128, K * 128], BF16, kind="Internal").ap()
            scr[tag] = (dt_, Buf("scr_" + tag))
        sap, sbuf_ = scr[tag]
        sview = sap.rearrange("p (k n) -> p k n", k=K)
        if cur_tile[0] == 0:
            S.dma("pool", t[:, 0:K, :], src, reads=[], writes=[V(b, t[:])], key=b)
            if NT > 1:
                S.dma("sp", sview, t[:, 0:K, :], reads=[V(b, t[:])], writes=[V(sbuf_, sap)], key=b)
        else:
            S.dma("sp", t[:, 0:K, :], sview, reads=[V(sbuf_, sap)], writes=[V(b, t[:])], key=b)
        return t, b

    sg, sg_b = sb("sg", [128, TT])

    def ffn(f, i):
        aT, aT_b = actT[0]
        wgv = wg_d[f].rearrange("(k p) n -> p k n", p=128)
        wuv = wu_d[f].rearrange("(k p) n -> p k n", p=128)
        wdv = wd_d[f].rearrange("(k p) n -> p k n", p=128)
        for hf in range(2):
            for jj in range(KFH):
                j = hf * KFH + jj
                gt_, gb_ = wload(wa, wa_i, wgv[:, :, j * 128:(j + 1) * 128], KD, "g%d_%d" % (f, j))
                ut_, ub_ = wload(wa, wa_i, wuv[:, :, j * 128:(j + 1) * 128], KD, "u%d_%d" % (f, j))
                pg, pgb = PS()
                pu, pub = PS()
                for kc in range(KD):
                    mm(V(pgb, pg[:, 0:TT]), V(gb_, gt_[:, kc, :]), V(hT_b, hT[:, kc, :]), start=(kc == 0), stop=(kc == KD - 1))
                for kc in range(KD):
                    mm(V(pub, pu[:, 0:TT]), V(ub_, ut_[:, kc, :]), V(hT_b, hT[:, kc, :]), start=(kc == 0), stop=(kc == KD - 1))
                act(V(sg_b, sg[:]), V(pgb, pg[:, 0:TT]), AF.Silu)
                tt(V(aT_b, aT[:, jj, :]), V(pub, pu[:, 0:TT]), V(sg_b, sg[:]), ALU.mult)
            for m in range(KD):
                dt_, db_ = wload(wdp, wd_i, wdv[:, hf * KFH:(hf + 1) * KFH, m * 128:(m + 1) * 128], KFH, "d%d_%d_%d" % (f, hf, m))
                pd, pdb = PS()
                for kf in range(KFH):
                    mm(V(pdb, pd[:, 0:TT]), V(db_, dt_[:, kf, :]), V(aT_b, aT[:, kf, :]), start=(kf == 0), stop=(kf == KFH - 1))
                stt(V(xT_b, xT[:, m, :]), V(pdb, pd[:, 0:TT]), V(Gcoef_b, Gcoef[:, i, m:m + 1]), V(xT_b, xT[:, m, :]),
                    ALU.mult, ALU.add)

    glu, glu_b = sb("glu", [128, 30 + TT])
    assert KCg * TT <= 2 * D
    ypre = xin_all[:].rearrange("p a d -> p (a d)")[:, 0:KCg * TT].rearrange("p (k t) -> p k t", k=KCg)
    ypre_b = xin_all_b
    ysq2 = [sb("ysq%d" % i, [128, TT]) for i in range(2)]
    pre, pre_b = glu, glu_b
    qkv = [sb("qkv%d" % i, [128, TT]) for i in range(3)]
    zs, zs_b = sb("zs", [128, TT])
    abT, abT_b = sb("abT", [64, TT])
    wab, wab_b = sb("wab", [128, KD, 32], BF16)
    RB = {n: sb("RB_" + n, [128, TT]) for n in ("lnb", "beta", "G", "lsk", "lsq", "R", "P", "Q", "Kd")}
    RB["t"] = RB["lsq"]
    RB["E"] = RB["lsq"]
    RB["g"] = RB["Kd"]
    RB["nEP"] = RB["lnb"]
    RB["nR"] = RB["lsk"]
    stk = [sb("stk%d" % i, [128, TT]) for i in range(2)]
    cols = [sb("cols%d" % i, [128, 128]) for i in range(2)]
    ktok, ktok_b = sb("ktok", [128, 128])
    vtok, vtok_b = sb("vtok", [128, 128])
    kd_t, kd_b = sb("kd_t", [128, 128])
    mA, mA_b = sb("mA", [128, 128])
    mB, mB_b = sb("mB", [128, 128])
    NAt, NA_b = sb("NAt", [128, 256])
    NBt, NB_b = sb("NBt", [128, 256])
    tA, tA_b = sb("tA", [128, 128])
    tB, tB_b = sb("tB", [128, 128])
    tY, tY_b = sb("tY", [128, 256])
    mean_t, mean_b = RB["R"]
    rs2, rs2_b = RB["P"]
    aTt, aTt_b = sb("aTt", [128, 128])
    qeT, qeT_b = sb("qeT", [128, 128])
    TbT, TbT_b = sb("TbT", [128, 128])
    TwT, TwT_b = sb("TwT", [128, 128])
    u_t, u_b = sb("u_t", [128, 128])
    nwT, nwT_b = sb("nwT", [128, 128])
    vnew, vnew_b = sb("vnew", [128, 128])
    on_t, on_b = TbT, TbT_b
    sml, sml_b = sb("sml", [128, 8])
    for i in range(2):
        S.op("dve", lambda e, i=i: e.memset(stk[i][0][:], 0.0), writes=[V(stk[i][1], stk[i][0][:])])
    S.op("dve", lambda e: e.memset(abT[:], 0.0), writes=[V(abT_b, abT[:])])
    selm, selm_b = sb("selm", [64, 2, 128])

    def build_sel(h):
        for r in range(2):
            rr = h if r == 0 else 32 + h
            ts(V(selm_b, selm[:, r, :]), V(cst_b, cst[0:64, 1, :]), V(cst_b, cst[0:64, 0, rr:rr + 1]), None, ALU.mult)

    gx = {}
    PAIRS = [[2 * i, 2 * i + 1] for i in range(cfg.get("ncores", 8) // 2)]
    if SPLIT:
        for nm_, shp_, dt_ in (("cin1", [128, KC * TT], F32), ("cout1", [256, KC * TT], F32),
                               ("cin2", [128, NH * TT], BF16), ("cout2", [256, NH * TT], BF16)):
            gx[nm_] = (nc.dram_tensor("gx_" + nm_, shp_, dt_, kind="Internal").ap(), Buf("gx_" + nm_))
    win_v = win_d.rearrange("(k p) n -> p k n", p=128)
    wout_v = wout_d.rearrange("(k p) n -> p k n", p=128)
    QB = 2 * CW

    def proj(col0):
        wt, wb = wload(wa, wa_i, win_v[:, :, col0:col0 + 128], KD, "in%d" % col0)
        p, pb = PS()
        for kc in range(KD):
            mm(V(pb, p[:, 0:TT]), V(wb, wt[:, kc, :]), V(hT_b, hT[:, kc, :]), start=(kc == 0), stop=(kc == KD - 1))
        return V(pb, p[:, 0:TT])

    def rb(n, sl=None):
        t, b = RB[n]
        return V(b, t[:] if sl is None else t[:, sl])

    class DS:
        pass

    def mk_set0():
        d = DS()
        d.qkv = qkv
        d.zs = (zs, zs_b)
        d.RB = RB
        d.stk = stk
        d.cols = cols
        d.t = dict(ktok=(ktok, ktok_b), vtok=(vtok, vtok_b), kd=(kd_t, kd_b), mA=(mA, mA_b), mB=(mB, mB_b),
                   NA=(NAt, NA_b), NB=(NBt, NB_b), tA=(tA, tA_b), tB=(tB, tB_b), tY=(tY, tY_b), aT=(aTt, aTt_b),
                   qeT=(qeT, qeT_b), TbT=(TbT, TbT_b), TwT=(TwT, TwT_b), u=(u_t, u_b), nwT=(nwT, nwT_b),
                   vnew=(vnew, vnew_b), sml=(sml, sml_b))
        return d

    st1_bufs = []
    aT_alias_b = actT[0][1]

    def mk_set1():
        need_a = 11 * TT
        arenaA = actT[0][0][:].rearrange("p k t -> p (k t)").bitcast(F32)
        arenas = [arenaA] + [w[0][:].rearrange("p k n -> p (k n)").bitcast(F32) for w in wdp]
        sizes = [max(KFH, KD) * TT // 2] + [KFH * 64] * len(wdp)
        use_alias = (sizes[0] >= need_a) and (sizes[1] >= 1408) and len(wdp) >= 2 and TT == 512
        pos = [0] * len(arenas)

        def al(name, n, ai):
            if not use_alias:
                t_, b_ = sb("s1_" + name, [128, n])
                return t_[:], b_
            assert pos[ai] + n <= sizes[ai], (name, ai, pos[ai], n, sizes[ai])
            v_ = arenas[ai][:, pos[ai]:pos[ai] + n]
            pos[ai] += n
            b_ = Buf("s1_" + name)
            st1_bufs.append(b_)
            return v_, b_
        d = DS()
        d.qkv = [al("qkv%d" % i, TT, 0) for i in range(3)]
        d.zs = al("zs", TT, 0)
        d.RB = dict(RB)
        for n_ in ("R", "P", "Q", "lsq", "G"):
            d.RB[n_] = al("RB_" + n_, TT, 0)
        d.RB["t"] = d.RB["lsq"]
        d.RB["E"] = d.RB["lsq"]
        d.stk = [al("stk%d" % i, TT, 0) for i in range(2)]
        d.cols = [al("cols%d" % i, 128, 1) for i in range(2)]
        d.t = {}
        for n_ in ("ktok", "vtok", "kd", "mA", "mB"):
            d.t[n_] = al(n_, 128, 1)
        d.t["NA"] = al("NA", 256, 1)
        d.t["NB"] = al("NB", 256, 1)
        d.t["tA"] = al("tA", 128, 2)
        d.t["tB"] = al("tB", 128, 2)
        d.t["tY"] = al("tY", 256, 2)
        for n_ in ("aT", "qeT", "TbT", "TwT", "u", "nwT", "vnew"):
            d.t[n_] = al(n_, 128, 2)
        sm_, smb_ = sb("s1_sml", [128, 8])
        d.t["sml"] = (sm_[:], smb_)
        d.alias = use_alias
        return d

    def inherit(dst, src):
        acc = []
        for b_ in src:
            if b_.w is not None:
                acc.append(b_.w)
            acc.extend(b_.r)
        for b_ in dst:
            b_.r = list(b_.r) + acc

    dsets = [mk_set0(), mk_set1()]
    for i in range(2):
        S.op("dve", lambda e, i=i: e.memset(dsets[1].stk[i][0], 0.0), writes=[V(dsets[1].stk[i][1], dsets[1].stk[i][0])])

    def dn_prep(h, st):
        qkv_, RB_ = st.qkv, st.RB
        zs_, zs_b_ = st.zs

        def rbs(n, sl=None):
            t_, b_ = RB_[n]
            return V(b_, t_[:] if sl is None else t_[:, sl])
        for i3 in range(3):
            pp = proj(QB + i3 * DNW + h * 128)
            ci = i3 * NH + h
            cp(V(pre_b, pre[:, 0:3]), V(ptail[ci][1], ptail[ci][0][:]))
            cp(V(pre_b, pre[:, 3:3 + TT]), pp, eng="act")
            cp(V(ptail[ci][1], ptail[ci][0][:]), V(pre_b, pre[:, TT:TT + 3]))
            for k in range(4):
                wk = V(wshT_b, wshT[:, ci, k:k + 1])
                if k == 0:
                    ts(V(cacc_b, cacc[:]), V(pre_b, pre[:, 0:TT]), wk, None, ALU.mult)
                else:
                    stt(V(cacc_b, cacc[:]), V(pre_b, pre[:, k:k + TT]), wk, V(cacc_b, cacc[:]), ALU.mult, ALU.add)
            act(V(qkv_[i3][1], qkv_[i3][0][:]), V(cacc_b, cacc[:]), AF.Silu)
        pz = proj(QB + 3 * DNW + h * 128)
        act(V(zs_b_, zs_[:]), pz, AF.Silu)
        qT, kT, vT = [V(qkv_[i][1], qkv_[i][0][:]) for i in range(3)]
        pb1, pb1_b = PS()
        build_sel(h)
        mm(V(pb1_b, pb1[:, 0:TT]), V(selm_b, selm[:, 0, :]), V(abT_b, abT[:]))
        pa1, pa1_b = PS()
        mm(V(pa1_b, pa1[:, 0:TT]), V(selm_b, selm[:, 1, :]), V(abT_b, abT[:]))
        act(rbs("t"), V(pb1_b, pb1[:, 0:TT]), AF.Exp, scale=-1.0)
        act(rbs("lnb"), rbs("t"), AF.Ln, bias=1.0)
        ts(rbs("lnb"), rbs("lnb"), -1.0, None, ALU.mult)
        act(rbs("beta"), rbs("lnb"), AF.Exp)
        act(rbs("t"), V(pa1_b, pa1[:, 0:TT]), AF.Exp, bias=V(hvec_b, hvec[:, 1, h:h + 1]))
        act(rbs("g"), rbs("t"), AF.Ln, bias=1.0)
        ts(rbs("g"), rbs("g"), V(negA_b, negA[:, h:h + 1]), None, ALU.mult)
        S.op("dve", lambda e: e.tensor_tensor_scan(RB_["G"][0][:], rmask[:], RB_["g"][0][:], 0.0, ALU.mult, ALU.add),
             reads=[V(rmask_b, rmask[:]), rbs("g")], writes=[rbs("G")])
        for (src, dst) in ((kT, "lsk"), (qT, "lsq")):
            act(V(cacc_b, cacc[:]), src, AF.Square)
            pss, pss_b = PS()
            mm(V(pss_b, pss[:, 0:TT]), ONES, V(cacc_b, cacc[:]))
            act(rbs(dst), V(pss_b, pss[:, 0:TT]), AF.Ln, bias=EPS)
        stt(rbs("R"), rbs("lsk"), 0.5, rbs("G"), ALU.mult, ALU.add)
        stt(rbs("P"), rbs("lsk"), -0.5, rbs("G"), ALU.mult, ALU.add)
        tt(rbs("P"), rbs("P"), rbs("lnb"), ALU.add)
        stt(rbs("Q"), rbs("lsq"), -0.5, rbs("G"), ALU.mult, ALU.add)
        ts(rbs("Q"), rbs("Q"), float(np.log(128.0 ** -0.5)), None, ALU.add)
        act(rbs("E"), rbs("Q"), AF.Exp)
        act(rbs("nEP"), rbs("P"), AF.Exp)
        ts(rbs("nEP"), rbs("nEP"), -1.0, None, ALU.mult)
        ts(rbs("nR"), rbs("R"), -1.0, None, ALU.mult)
        for blk in range(NB):
            bs = slice(blk * 128, (blk + 1) * 128)
            last = blk * 128 + 127
            glast = V(RB_["G"][1], RB_["G"][0][:, last:last + 1])
            act(rbs("Kd", bs), rbs("nR", bs), AF.Exp, bias=glast)
        for (pi, nme) in ((0, "nR"), (32, "P"), (64, "beta"), (96, "nEP")):
            cp(V(st.stk[0][1], st.stk[0][0][pi:pi + 1, :]), V(RB_[nme][1], RB_[nme][0][pi:pi + 1, :]))
        cp(V(st.stk[1][1], st.stk[1][0][0:1, :]), V(RB_["Kd"][1], RB_["Kd"][0][0:1, :]))

    def dn_blocks(h, st):
        qkv_, RB_ = st.qkv, st.RB
        zs_, zs_b_ = st.zs
        T_ = st.t

        def rbs(n, sl=None):
            t_, b_ = RB_[n]
            return V(b_, t_[:] if sl is None else t_[:, sl])

        def tv(n, sl=None):
            t_, b_ = T_[n]
            return V(b_, t_[:] if sl is None else t_[:, sl])
        Sb, Sbb = Sst[h]
        SV = V(Sbb, Sb[:])
        for blk in range(NB):
            bs = slice(blk * 128, (blk + 1) * 128)
            last = blk * 128 + 127
            for i in range(2):
                pc, pcb = PS()
                tr(V(pcb, pc[:, 0:128]), V(st.stk[i][1], st.stk[i][0][:, bs]))
                cp(V(st.cols[i][1], st.cols[i][0][:]), V(pcb, pc[:, 0:128]), eng="act")
            c0, c0b = st.cols[0]
            cnR = V(c0b, c0[:, 0:1])
            cP = V(c0b, c0[:, 32:33])
            cbeta = V(c0b, c0[:, 64:65])
            cnEP = V(c0b, c0[:, 96:97])
            cKd = V(st.cols[1][1], st.cols[1][0][:, 0:1])
            qTb = V(qkv_[0][1], qkv_[0][0][:, bs])
            kTb = V(qkv_[1][1], qkv_[1][0][:, bs])
            vTb = V(qkv_[2][1], qkv_[2][0][:, bs])
            ptk, ptk_b = PS()
            tr(V(ptk_b, ptk[:, 0:128]), kTb)
            tr(V(ptk_b, ptk[:, 128:256]), vTb)
            pkk, pkk_b = PS()
            mm(V(pkk_b, pkk[:, 0:128]), kTb, kTb)
            mm(V(pkk_b, pkk[:, 128:256]), kTb, qTb)
            yield
            cp(tv("ktok"), V(ptk_b, ptk[:, 0:128]), eng="act")
            cp(tv("vtok"), V(ptk_b, ptk[:, 128:256]), eng="act")
            ts(tv("kd"), V(ptk_b, ptk[:, 0:128]), cKd, None, ALU.mult)
            tt(tv("mA"), NEGS, rbs("R", bs), ALU.subtract)
            act(tv("mA"), tv("mA"), AF.Exp, bias=cP)
            tt(tv("mB"), NEGST, rbs("P", bs), ALU.add)
            act(tv("mB"), tv("mB"), AF.Exp, bias=cnR)
            tt(tv("aT"), NEGTT, rbs("Q", bs), ALU.add)
            act(tv("aT"), tv("aT"), AF.Exp, bias=cnR)
            yield
            stt(tv("mA"), V(pkk_b, pkk[:, 0:128]), -1.0, tv("mA"), ALU.mult, ALU.mult)
            stt(tv("mB"), V(pkk_b, pkk[:, 0:128]), -1.0, tv("mB"), ALU.mult, ALU.mult)
            tt(tv("aT"), V(pkk_b, pkk[:, 128:256]), tv("aT"), ALU.mult)
            tt(tv("qeT"), qTb, rbs("E", bs), ALU.mult)
            XTv = tv("NA", slice(128, 256))
            XUv = tv("NB", slice(128, 256))
            tt(tv("NA", slice(0, 128)), tv("mA"), BD16, ALU.mult)
            tt(tv("NB", slice(0, 128)), tv("mB"), BD16, ALU.mult)
            cp(XTv, IDN)
            cp(XUv, IDN)
            yield
            for lev in range(4):
                p1, p1b = PS()
                p2, p2b = PS()
                if lev == 3:
                    mm(V(p1b, p1[:, 128:256]), tv("NA", slice(0, 128)), XUv)
                    mm(V(p2b, p2[:, 128:256]), tv("NB", slice(0, 128)), XTv)
                    yield
                else:
                    mm(V(p1b, p1[:, 0:256]), tv("NA", slice(0, 128)), tv("NB", slice(0, 256)))
                    mm(V(p2b, p2[:, 0:256]), tv("NB", slice(0, 128)), tv("NA", slice(0, 256)))
                    yield
                    cp(tv("NB", slice(0, 128)), V(p1b, p1[:, 0:128]), eng="act")
                    cp(tv("NA", slice(0, 128)), V(p2b, p2[:, 0:128]), eng="act")
                tt(XUv, V(p1b, p1[:, 128:256]), XUv, ALU.add)
                tt(XTv, V(p2b, p2[:, 128:256]), XTv, ALU.add)
                yield
            for li in range(3):
                lastm = (li == 2)
                tt(tv("tA"), tv("mA"), LLm[li], ALU.mult)
                pY, pYb = PS()
                mm(V(pYb, pY[:, 0:128]), tv("tA"), XUv)
                if not lastm:
                    tt(tv("tB"), tv("mB"), URm[li], ALU.mult)
                    mm(V(pYb, pY[:, 128:256]), tv("tB"), XTv)
                    yield
                    cp(tv("tY", slice(0, 256)), V(pYb, pY[:, 0:256]), eng="act")
                else:
                    yield
                    cp(tv("tY", slice(0, 128)), V(pYb, pY[:, 0:128]), eng="act")
                pZ, pZb = PS()
                mm(V(pZb, pZ[:, 0:128]), XTv, tv("tY", slice(0, 128)))
                if not lastm:
                    mm(V(pZb, pZ[:, 128:256]), XUv, tv("tY", slice(128, 256)))
                yield
                tt(XUv, V(pZb, pZ[:, 0:128]), XUv, ALU.add)
                if not lastm:
                    tt(XTv, V(pZb, pZ[:, 128:256]), XTv, ALU.add)
            ts(tv("TbT"), XUv, cbeta, None, ALU.mult)
            ts(tv("TwT"), XUv, cnEP, None, ALU.mult)
            pu_, pu_b = PS()
            mm(V(pu_b, pu_[:, 0:128]), tv("TbT"), tv("vtok"))
            mm(V(pu_b, pu_[:, 128:256]), tv("ktok"), tv("TwT"))
            yield
            cp(tv("u"), V(pu_b, pu_[:, 0:128]), eng="act")
            cp(tv("nwT"), V(pu_b, pu_[:, 128:256]), eng="act")
            pv, pvb = PS()
            mm(V(pvb, pv[:, 0:128]), tv("nwT"), SV)
            yield
            tt(tv("vnew"), V(pvb, pv[:, 0:128]), tv("u"), ALU.add)
            po, pob = PS()
            mm(V(pob, po[:, 0:128]), tv("qeT"), SV, start=True, stop=False)
            mm(V(pob, po[:, 0:128]), tv("aT"), tv("vnew"), start=False, stop=True)
            pds, pdsb = PS()
            mm(V(pdsb, pds[:, 0:128]), tv("kd"), tv("vnew"))
            act(tv("sml", slice(0, 1)), V(RB_["G"][1], RB_["G"][0][:, last:last + 1]), AF.Exp)
            yield
            stt(SV, SV, tv("sml", slice(0, 1)), V(pdsb, pds[:, 0:128]), ALU.mult, ALU.add)
            act(tv("TbT"), V(pob, po[:, 0:128]), AF.Square, accum=tv("sml", slice(1, 2)))
            act(tv("sml", slice(2, 3)), tv("sml", slice(1, 2)), AF.Sqrt, bias=EPS, scale=1.0 / 128)
            smt, smb = T_["sml"]
            S.op("dve", lambda e, smt=smt: e.reciprocal(smt[:, 3:4], smt[:, 2:3]), reads=[tv("sml", slice(2, 3))], writes=[tv("sml", slice(3, 4))])
            ts(tv("TbT"), V(pob, po[:, 0:128]), tv("sml", slice(3, 4)), None, ALU.mult)
            pt2, pt2b = PS()
            tr(V(pt2b, pt2[:, 0:128]), tv("TbT"))
            yield
            stt(V(yT_b, yT[:, KCg + h, bs]), V(pt2b, pt2[:, 0:128]), V(onw_b, onw[:, 0:1]), V(zs_b_, zs_[:, bs]), ALU.mult, ALU.mult)

    def mixer():
        for j in range(KC):
            pa = proj(j * 128)
            pgt = proj(CW + j * 128)
            act(V(sg_b, sg[:]), pgt, AF.Sigmoid)
            cp(V(glu_b, glu[:, 0:30]), V(gtail[j][1], gtail[j][0][:]))
            tt(V(glu_b, glu[:, 30:30 + TT]), pa, V(sg_b, sg[:]), ALU.mult)
            cp(V(gtail[j][1], gtail[j][0][:]), V(glu_b, glu[:, TT:TT + 30]))
            for k in range(31):
                wk = V(wdwT_b, wdwT[:, j, k:k + 1])
                if k == 0:
                    ts(V(cacc_b, cacc[:]), V(glu_b, glu[:, 0:TT]), wk, V(cvec_b, cvec[:, 0, j:j + 1]), ALU.mult, ALU.add)
                elif k < 30:
                    stt(V(cacc_b, cacc[:]), V(glu_b, glu[:, k:k + TT]), wk, V(cacc_b, cacc[:]), ALU.mult, ALU.add)
                else:
                    stt(V(ypre_b, ypre[:, j, :]), V(glu_b, glu[:, k:k + TT]), wk, V(cacc_b, cacc[:]), ALU.mult, ALU.add)
        if CUT <= 1:
            return
        S.dma("pool", wab[:, :, 0:NH], win_v[:, :, QB + 4 * DNW:QB + 4 * DNW + NH], reads=[], writes=[V(wab_b, wab[:])], key=wab_b, slow=True)
        S.dma("pool", wab[:, :, 16:16 + NH], win_v[:, :, QB + 4 * DNW + NH:QB + 4 * DNW + 2 * NH], reads=[], writes=[V(wab_b, wab[:])], key=wab_b, slow=True)
        pab, pab_b = PS()
        for kc in range(KD):
            mm(V(pab_b, pab[0:NH, 0:TT]), V(wab_b, wab[:, kc, 0:NH]), V(hT_b, hT[:, kc, :]), start=(kc == 0), stop=(kc == KD - 1))
        cp(V(abT_b, abT[0:NH, :]), V(pab_b, pab[0:NH, 0:TT]))
        pab2, pab2_b = PS()
        for kc in range(KD):
            mm(V(pab2_b, pab2[32:32 + NH, 0:TT]), V(wab_b, wab[:, kc, 16:16 + NH]), V(hT_b, hT[:, kc, :]), start=(kc == 0), stop=(kc == KD - 1))
        cp(V(abT_b, abT[32:32 + NH, :]), V(pab2_b, pab2[32:32 + NH, 0:TT]))

        if CUT <= 2:
            return
        inherit([bb for bb in st1_bufs], [aT_alias_b] + [w[1] for w in wdp])
        for hp in range(0, NH, 2):
            hs = [hh for hh in (hp, hp + 1) if hh < NH]
            for i, hh in enumerate(hs):
                dn_prep(hh, dsets[i])
            gens = [dn_blocks(hh, dsets[i]) for i, hh in enumerate(hs)]
            while gens:
                for g in list(gens):
                    try:
                        next(g)
                    except StopIteration:
                        gens.remove(g)
        inherit([aT_alias_b] + [w[1] for w in wdp], [bb for bb in st1_bufs])

        if SPLIT:
            yv = yT[:, KCg:KCg + NHg, :].rearrange("p k t -> p (k t)")
            ypv = ypre.rearrange("p k t -> p (k t)")
            S.dma("sp", gx["cin1"][0], ypv[:, 0:KC * TT], reads=[V(ypre_b, ypre)], writes=[V(gx["cin1"][1], gx["cin1"][0])], key=ypre_b)
            S.dma("sp", gx["cin2"][0], yv[:, 0:NH * TT], reads=[V(yT_b, yT[:])], writes=[V(gx["cin2"][1], gx["cin2"][0])], key=yT_b)
            S.coll(gx["cin1"][0], gx["cout1"][0], PAIRS, reads=[V(gx["cin1"][1], gx["cin1"][0])], writes=[V(gx["cout1"][1], gx["cout1"][0])], key=gx["cout1"][1])
            S.coll(gx["cin2"][0], gx["cout2"][0], PAIRS, reads=[V(gx["cin2"][1], gx["cin2"][0])], writes=[V(gx["cout2"][1], gx["cout2"][0])], key=gx["cout2"][1])
            S.dma("sp", ypv.rearrange("p (r c) -> p r c", r=2), gx["cout1"][0].rearrange("(r p) c -> p r c", p=128),
                  reads=[V(gx["cout1"][1], gx["cout1"][0])], writes=[V(ypre_b, ypre)], key=ypre_b)
            S.dma("sp", yv.rearrange("p (r c) -> p r c", r=2), gx["cout2"][0].rearrange("(r p) c -> p r c", p=128),
                  reads=[V(gx["cout2"][1], gx["cout2"][0])], writes=[V(yT_b, yT[:])], key=yT_b)
        pmn, pmn_b = PS()
        pvr, pvr_b = PS()
        for j in range(KCg):
            mm(V(pmn_b, pmn[:, 0:TT]), ONES, V(ypre_b, ypre[:, j, :]), start=(j == 0), stop=(j == KCg - 1))
        for j in range(KCg):
            yq, yqb = ysq2[j % 2]
            act(V(yqb, yq[:]), V(ypre_b, ypre[:, j, :]), AF.Square)
            mm(V(pvr_b, pvr[:, 0:TT]), ONES, V(yqb, yq[:]), start=(j == 0), stop=(j == KCg - 1), sig=True)
        ts(V(mean_b, mean_t[:]), V(pmn_b, pmn[:, 0:TT]), 1.0 / (KCg * 128), None, ALU.mult)
        tt(V(rs2_b, rs2[:]), V(mean_b, mean_t[:]), V(mean_b, mean_t[:]), ALU.mult)
        stt(V(rs2_b, rs2[:]), V(pvr_b, pvr[:, 0:TT]), 1.0 / (KCg * 128), V(rs2_b, rs2[:]), ALU.mult, ALU.subtract)
        act(V(rs2_b, rs2[:]), V(rs2_b, rs2[:]), AF.Sqrt, bias=EPS, scale=1.0)
        S.op("dve", lambda e: e.reciprocal(rs2[:], rs2[:]), reads=[V(rs2_b, rs2[:])], writes=[V(rs2_b, rs2[:])])
        for j in range(KCg):
            tt(V(cacc_b, cacc[:]), V(ypre_b, ypre[:, j, :]), V(mean_b, mean_t[:]), ALU.subtract)
            tt(V(cacc_b, cacc[:]), V(cacc_b, cacc[:]), V(rs2_b, rs2[:]), ALU.mult)
            act(V(yT_b, yT[:, j, :]), V(cacc_b, cacc[:]), AF.Silu, bias=V(lnT_b, lnT[:, 1, j:j + 1]),
                scale=V(lnT_b, lnT[:, 0, j:j + 1]))

        if CUT <= 11:
            return
        for m in range(KD):
            wt, wb = wload(wa, wa_i, wout_v[:, :, m * 128:(m + 1) * 128], KY, "out%d" % m)
            p, pb = PS()
            for kc in range(KY):
                mm(V(pb, p[:, 0:TT]), V(wb, wt[:, kc, :]), V(yT_b, yT[:, kc, :]), start=(kc == 0), stop=(kc == KY - 1))
            stt(V(xT_b, xT[:, m, :]), V(pb, p[:, 0:TT]), V(Gcoef_b, Gcoef[:, 1, m:m + 1]), V(xT_b, xT[:, m, :]), ALU.mult, ALU.add)

    out_bufs = []
    for t in range(NT):
        cur_tile[0] = t
        for tb in range(NB):
            xt, xb = xin[tb % 2]
            r0 = t * TT + tb * 128
            ld(V(xb, xt[:]), x_d[r0:r0 + 128, :])
            for k0 in range(0, KD, 4):
                p, pb = PS()
                nk = min(4, KD - k0)
                for kk in range(nk):
                    tr(V(pb, p[:, kk * 128:(kk + 1) * 128]), V(xb, xt[:, (k0 + kk) * 128:(k0 + kk + 1) * 128]))
                S.op("dve", lambda e, p=p, k0=k0, nk=nk, tb=tb: e.tensor_copy(
                    xT[:, k0:k0 + nk, tb * 128:(tb + 1) * 128], p[:, 0:nk * 128].rearrange("p (k t) -> p k t", k=nk)),
                    reads=[V(pb, p[:])], writes=[V(xT_b, xT[:])])
        rms_mod(0)
        ffn(0, 0)
        if stop != "ffn1":
            rms_mod(1)
            mixer()
            if stop != "mixer":
                rms_mod(2)
                ffn(1, 2)
        if stop is None:
            rms_rstd()
            for kc in range(KD):
                tt(V(tmpn_b, tmpn[:]), V(xT_b, xT[:, kc, :]), V(rstd_b, rstd[:]), ALU.mult)
                ts(V(xT_b, xT[:, kc, :]), V(tmpn_b, tmpn[:]), V(normT_b, normT[:, 3, kc:kc + 1]), None, ALU.mult)
        for tb in range(NB):
            xt, xb = xin[tb % 2]
            r0 = t * TT + tb * 128
            for k0 in range(0, KD, 4):
                p, pb = PS()
                nk = min(4, KD - k0)
                for kk in range(nk):
                    tr(V(pb, p[:, kk * 128:(kk + 1) * 128]), V(xT_b, xT[:, k0 + kk, tb * 128:(tb + 1) * 128]))
                cp(V(xb, xt[:, k0 * 128:(k0 + nk) * 128]), V(pb, p[:, 0:nk * 128]), eng="act")
            S.dma("sp", y_d[r0:r0 + 128, :], xt[:], reads=[V(xb, xt[:])], writes=[], key=xb)
            if xb not in out_bufs:
                out_bufs.append(xb)
    S.emit(final_waits=out_bufs)
    es.close()
    return nc


def host_inputs(cfg, core, x, c, w_ada, b_ada, ffn1_norm, ffn1_wg, ffn1_wu, ffn1_wd, mix_norm, w_in,
                w_dw, b_dw, conv_ln_w, conv_ln_b, w_short, a_log, dt_bias, dn_norm_w, w_out,
                ffn2_norm, ffn2_wg, ffn2_wu, ffn2_wd, final_norm):
    KD, KC, NH, TT = cfg["KD"], cfg["KC"], cfg["NH"], cfg["TT"]
    split = cfg.get("split", 0)
    CWg, NHg, CW = cfg["CWg"], cfg["NHg"], cfg["CW"]
    KCg = CWg // 128
    if split:
        b, rank = core // 2, core % 2
    else:
        b, rank = core, 0
    f = np.float32
    A = np.ascontiguousarray

    def fm(v, k):
        return A(np.asarray(v, f).reshape(k, 128).T)

    consts = np.zeros((128, 12, 128), f)
    idx = np.arange(128)
    consts[:, 0, :] = np.eye(128, dtype=f)
    consts[:, 1, :] = 1.0
    consts[:, 2, :] = np.where(idx[:, None] > idx[None, :], 0.0, NEG)
    consts[:, 3, :] = np.where(idx[None, :] > idx[:, None], 0.0, NEG)
    consts[:, 4, :] = np.where(idx[None, :] >= idx[:, None], 0.0, NEG)
    consts[:, 5, :] = (idx[:, None] // 16 == idx[None, :] // 16)
    for i, bsz in enumerate((16, 32, 64)):
        ll = ((idx[:, None] // (2 * bsz) == idx[None, :] // (2 * bsz)) & (idx[:, None] % (2 * bsz) >= bsz)
              & (idx[None, :] % (2 * bsz) < bsz)).astype(f)
        consts[:, 6 + i, :] = ll
        consts[:, 9 + i, :] = ll.T
    rmask = np.ones((128, TT), f)
    rmask[:, ::128] = 0.0
    DNWg = NHg * 128
    DNW = NH * 128
    c0, h0 = rank * CW, rank * NH
    cols = np.concatenate([
        np.arange(c0, c0 + CW), CWg + np.arange(c0, c0 + CW),
        2 * CWg + 0 * DNWg + h0 * 128 + np.arange(DNW), 2 * CWg + 1 * DNWg + h0 * 128 + np.arange(DNW),
        2 * CWg + 2 * DNWg + h0 * 128 + np.arange(DNW), 2 * CWg + 3 * DNWg + h0 * 128 + np.arange(DNW),
        2 * CWg + 4 * DNWg + h0 + np.arange(NH), 2 * CWg + 4 * DNWg + NHg + h0 + np.arange(NH)])
    shc = np.concatenate([i3 * DNWg + h0 * 128 + np.arange(DNW) for i3 in range(3)])
    w_in_l = np.asarray(w_in[0], f)[:, cols]
    w_dw_l = np.asarray(w_dw[0], f)[:, c0:c0 + CW]
    w_sh_l = np.asarray(w_short[0], f)[:, shc]
    z1 = np.zeros(CW, f)
    return {
        "x": A(np.asarray(x[b], f)),
        "cT": fm(c[b], KD),
        "w_ada": A(np.asarray(w_ada[0], f)),
        "b_adaT": fm(b_ada[0], 9 * KD),
        "normT": A(np.stack([fm(ffn1_norm[0], KD), fm(mix_norm[0], KD), fm(ffn2_norm[0], KD), fm(final_norm, KD)], axis=1)),
        "ffn1_wg": A(np.asarray(ffn1_wg[0], f)), "ffn1_wu": A(np.asarray(ffn1_wu[0], f)), "ffn1_wd": A(np.asarray(ffn1_wd[0], f)),
        "ffn2_wg": A(np.asarray(ffn2_wg[0], f)), "ffn2_wu": A(np.asarray(ffn2_wu[0], f)), "ffn2_wd": A(np.asarray(ffn2_wd[0], f)),
        "w_in": A(w_in_l),
        "w_out": A(np.asarray(w_out[0], f)),
        "w_dwT": A(w_dw_l.T.reshape(KC, 128, 31).transpose(1, 0, 2)),
        "cvecT": A(np.stack([fm(np.asarray(b_dw[0], f)[c0:c0 + CW], KC), fm(z1, KC), fm(z1, KC)], axis=1)),
        "lnT": A(np.stack([fm(conv_ln_w[0], KCg), fm(conv_ln_b[0], KCg)], axis=1)),
        "w_shT": A(w_sh_l.T.reshape(3 * NH, 128, 4).transpose(1, 0, 2)),
        "hvec": A(np.broadcast_to(np.stack([np.asarray(a_log[0], f)[h0:h0 + NH], np.asarray(dt_bias[0], f)[h0:h0 + NH]], axis=0)[None], (128, 2, NH))),
        "onwT": A(np.asarray(dn_norm_w[0], f).reshape(128, 1)),
        "consts": consts,
        "rmask": rmask,
    }


_NC_CACHE = {}


def kernel(**inputs):
    cfg = FULL
    B = inputs["x"].shape[0]
    if "nc" not in _NC_CACHE:
        _NC_CACHE["nc"] = build(cfg)
    nc = _NC_CACHE["nc"]
    n = 8
    in_maps = [host_inputs(cfg, i, **inputs) for i in range(n)]
    res = run_bass_kernel_spmd(nc, in_maps, core_ids=list(range(n)))
    out = np.stack([res.results[2 * i]["y"] for i in range(B)], axis=0)
    return out.astype(np.float32)
```

```python
import numpy as np
import os
CUT = int(os.environ.get('KCUT', '99'))
from contextlib import ExitStack
import concourse.bass as bass
import concourse.mybir as mybir
from concourse.bass_utils import run_bass_kernel_spmd

F32 = mybir.dt.float32
BF16 = mybir.dt.bfloat16
AF = mybir.ActivationFunctionType
ALU = mybir.AluOpType
NEG = -1.0e9
EPS = 1e-6


class Buf:
    __slots__ = ("name", "w", "r", "dsem", "psum")

    def __init__(self, name, psum=False):
        self.name = name
        self.psum = psum
        self.w = None
        self.r = []
        self.dsem = None


class V:
    __slots__ = ("b", "ap")

    def __init__(self, b, ap):
        self.b = b
        self.ap = ap


class DSem:
    def __init__(self, sem):
        self.sem = sem
        self.count = 0


class Rec:
    __slots__ = ("eng", "fn", "deps", "sig", "idx", "dma", "dval", "dinc")

    def __init__(self, eng, fn, deps, sig, dma=None):
        self.eng = eng
        self.fn = fn
        self.deps = deps
        self.sig = sig
        self.dma = dma
        self.dval = 0
        self.dinc = 16
        self.idx = -1


class Sched:
    ENGS = ("pe", "dve", "act", "pool", "sp")

    def __init__(self, nc, es):
        self.nc = nc
        self.es = es
        self.q = {e: [] for e in self.ENGS}
        self.esem = {e: es.enter_context(nc.semaphore("s_" + e)) for e in ("pe", "dve", "act", "pool")}
        self.nsem = 0

    def _deps(self, eng, reads, writes, is_dma):
        deps = []
        for v in reads:
            w = v.b.w
            if w is not None:
                deps.append((w, "raw"))
            if v.b.psum:
                for r in v.b.r:
                    if r.eng != eng:
                        deps.append((r, "rr"))
        for v in writes:
            b = v.b
            if b.w is not None:
                deps.append((b.w, "waw"))
            for r in b.r:
                deps.append((r, "war"))
        out = []
        for (d, kind) in deps:
            if d.dma is None and not is_dma and d.eng == eng:
                if eng == "pe":
                    continue
            if d.dma is not None:
                out.append((d, d.dma.count))
            else:
                out.append((d, None))
        return out

    def _commit(self, rec, reads, writes):
        for v in writes:
            v.b.w = rec
            v.b.r = []
        for v in reads:
            if v.b.w is not rec:
                v.b.r.append(rec)

    def op(self, eng, fn, reads=(), writes=(), sig=True):
        rec = Rec(eng, fn, self._deps(eng, reads, writes, False), sig)
        rec.idx = len(self.q[eng])
        self.q[eng].append(rec)
        self._commit(rec, reads, writes)
        return rec

    def dma(self, queue, out, in_, reads, writes, key, slow=False):
        if key.dsem is None:
            key.dsem = DSem(self.es.enter_context(self.nc.semaphore("d%d" % self.nsem)))
            self.nsem += 1
        ds = key.dsem
        rec = Rec(queue, lambda e: e.dma_start(out=out, in_=in_, allow_slow_non_contiguous=slow), self._deps(queue, reads, writes, True), True, dma=ds)
        ds.count += 16
        rec.dval = ds.count
        rec.idx = len(self.q[queue])
        self.q[queue].append(rec)
        self._commit(rec, reads, writes)
        return rec

    def coll(self, ins_ap, outs_ap, groups, reads, writes, key):
        if key.dsem is None:
            key.dsem = DSem(self.es.enter_context(self.nc.semaphore("c%d" % self.nsem)))
            self.nsem += 1
        ds = key.dsem
        rec = Rec("pool", lambda e: e.collective_compute("AllGather", ALU.bypass, replica_groups=groups,
                                                         ins=[ins_ap], outs=[outs_ap]),
                  self._deps("pool", reads, writes, True), True, dma=ds)
        rec.dinc = 1
        ds.count += 1
        rec.dval = ds.count
        rec.idx = len(self.q["pool"])
        self.q["pool"].append(rec)
        self._commit(rec, reads, writes)
        return rec

    def emit(self, final_waits=()):
        nc = self.nc
        cnt = {}
        for e in ("pe", "dve", "act", "pool"):
            arr = []
            c = 0
            for r in self.q[e]:
                if r.dma is None and r.sig:
                    c += 1
                arr.append(c)
            res = [0] * len(arr)
            nxt = None
            for i in range(len(arr) - 1, -1, -1):
                r = self.q[e][i]
                if r.dma is None and r.sig:
                    nxt = arr[i]
                res[i] = nxt
            cnt[e] = res
        handles = {"pe": nc.tensor, "dve": nc.vector, "act": nc.scalar, "pool": nc.gpsimd, "sp": nc.sync}
        with nc.Block() as block:
            def run(e, h):
                waited = {}
                for r in self.q[e]:
                    for (d, dv) in r.deps:
                        if d.dma is not None:
                            sem, val = d.dma.sem, dv
                        else:
                            val = cnt[d.eng][d.idx]
                            assert val is not None, "dep on op with no later signal"
                            sem = self.esem[d.eng]
                        k = id(sem)
                        if waited.get(k, 0) >= val:
                            continue
                        waited[k] = val
                        h.wait_ge(sem, val)
                    ins = r.fn(h)
                    if r.dma is not None:
                        ins.then_inc(r.dma.sem, r.dinc)
                    elif r.sig:
                        ins.then_inc(self.esem[e], 1)
                if e == "sp":
                    for b in final_waits:
                        h.wait_ge(b.dsem.sem, b.dsem.count)

            @block.tensor
            def _(h):
                run("pe", h)

            @block.vector
            def _(h):
                run("dve", h)

            @block.scalar
            def _(h):
                run("act", h)

            @block.gpsimd
            def _(h):
                run("pool", h)

            @block.sync
            def _(h):
                run("sp", h)


def make_cfg(D, DFF, CW, NH, T, split=0):
    f = 1 + split
    CWl, NHl = CW // f, NH // f
    return dict(D=D, DFF=DFF, CW=CWl, NH=NHl, T=T, KD=D // 128, KF=DFF // 128, KC=CWl // 128, split=split,
                CWg=CW, NHg=NH, TT=min(512, T), INC=2 * CWl + 4 * NHl * 128 + 2 * NHl)


FULL = make_cfg(2048, 5632, 1024, 8, 4096, split=1)


def build(cfg, stop=None):
    D, DFF, CW, NH, T = cfg["D"], cfg["DFF"], cfg["CW"], cfg["NH"], cfg["T"]
    KD, KF, KC, TT, INC = cfg["KD"], cfg["KF"], cfg["KC"], cfg["TT"], cfg["INC"]
    NT = T // TT
    NB = TT // 128
    SPLIT = cfg.get("split", 0)
    KCg = cfg["CWg"] // 128
    NHg = cfg["NHg"]
    KY = KCg + NHg
    DNW = NH * 128
    nc = bass.Bass("TRN2", target_bir_lowering=False)
    es = ExitStack()
    S = Sched(nc, es)

    def din(name, shape, dt=F32):
        return nc.dram_tensor(name, list(shape), dt, kind="ExternalInput").ap()

    x_d = din("x", [T, D])
    y_d = nc.dram_tensor("y", [T, D], F32, kind="ExternalOutput").ap()
    cT_d = din("cT", [128, KD])
    wada_d = din("w_ada", [D, 9 * D])
    badaT_d = din("b_adaT", [128, 9 * KD])
    normT_d = din("normT", [128, 4, KD])
    FSPLIT = cfg.get("split", 0)
    DFFl = DFF // (1 + FSPLIT)
    wg_d = [din("ffn1_wg", [D, DFFl]), din("ffn2_wg", [D, DFFl])]
    wu_d = [din("ffn1_wu", [D, DFFl]), din("ffn2_wu", [D, DFFl])]
    wd_d = [din("ffn1_wd", [DFFl, D]), din("ffn2_wd", [DFFl, D])]
    win_d = din("w_in", [D, INC])
    wout_d = din("w_out", [D, D])
    wdwT_d = din("w_dwT", [128, KC, 31])
    cvec_d = din("cvecT", [128, 3, KC])
    lnT_d = din("lnT", [128, 2, KCg])
    wshT_d = din("w_shT", [128, 3 * NH, 4])
    hvec_d = din("hvec", [128, 2, NH])
    onw_d = din("onwT", [128, 1])
    cst_d = din("consts", [128, 12, 128])
    rmask_d = din("rmask", [128, TT])

    SBTOT = [0]

    def sb(name, shape, dt=F32):
        t = es.enter_context(nc.sbuf_tensor("sb_" + name, list(shape), dt))
        nbytes = int(np.prod(shape[1:])) * (2 if dt == BF16 else 4)
        SBTOT[0] += nbytes
        if os.environ.get("KSB"):
            print("SB", name, nbytes, SBTOT[0])
        return t, Buf(name)

    cst, cst_b = sb("cst", [128, 12, 128])
    rmask, rmask_b = sb("rmask", [128, TT])
    ones_bf, ones_bf_b = sb("ones_bf", [128, 128], BF16)
    normT, normT_b = sb("normT", [128, 4, KD])
    wdwT, wdwT_b = sb("wdwT", [128, KC, 31])
    cvec, cvec_b = sb("cvec", [128, 3, KC])
    lnT, lnT_b = sb("lnT", [128, 2, KCg])
    wshT, wshT_b = sb("wshT", [128, 3 * NH, 4])
    hvec, hvec_b = sb("hvec", [128, 2, NH])
    negA, negA_b = sb("negA", [128, NH])
    onw, onw_b = sb("onw", [128, 1])
    modsT, modsT_b = sb("modsT", [128, 9 * KD])
    Acoef, Acoef_b = sb("Acoef", [128, 3, KD])
    Gcoef, Gcoef_b = sb("Gcoef", [128, 3, KD])
    finw_b = normT_b
    xT, xT_b = sb("xT", [128, KD, TT])
    hT, hT_b = sb("hT", [128, KD, TT], BF16)
    KFH = KF // 2
    actT = [sb("actT%d" % i, [128, max(KFH, KD), TT], BF16) for i in range(1)]
    yT, yT_b = sb("yT", [128, KY, TT], BF16)
    sq, sq_b = actT[0]
    rstd, rstd_b = sb("rstd", [128, TT])
    xin_all, xin_all_b = sb("xin_all", [128, 2, D])
    xin = [(xin_all[:, i, :], xin_all_b) for i in range(2)]
    Sst = [sb("Sst%d" % h, [128, 128]) for h in range(NH)]
    gtail = [sb("gtail%d" % j, [128, 30]) for j in range(KC)]
    ptail = [sb("ptail%d" % j, [128, 3]) for j in range(3 * NH)]
    NWA, NWD = 4, 2
    wa = [sb("wa%d" % i, [128, KD, 128], BF16) for i in range(NWA)]
    wdp = [sb("wd%d" % i, [128, KF // 2, 128], BF16) for i in range(NWD)]
    wa_i = [0]
    wd_i = [0]
    psum = []
    for i in range(8):
        t = es.enter_context(nc.psum_tensor("ps%d" % i, [128, 512], F32))
        psum.append((t, Buf("ps%d" % i, psum=True)))
    ps_i = [0]

    def PS():
        t, b = psum[ps_i[0] % 8]
        ps_i[0] += 1
        return t, b

    IDN = V(cst_b, cst[:, 0, :])
    ONES = V(cst_b, cst[:, 1, :])
    NEGS = V(cst_b, cst[:, 2, :])
    NEGST = V(cst_b, cst[:, 3, :])
    NEGTT = V(cst_b, cst[:, 4, :])
    BD16 = V(cst_b, cst[:, 5, :])
    LLm = [V(cst_b, cst[:, 6 + i, :]) for i in range(3)]
    URm = [V(cst_b, cst[:, 9 + i, :]) for i in range(3)]

    def mm(out, lhsT, rhs, start=True, stop=True, sig=None):
        if sig is None:
            sig = stop
        return S.op("pe", lambda e: e.matmul(out.ap, lhsT.ap, rhs.ap, start=start, stop=stop),
                    reads=[lhsT, rhs], writes=[out], sig=sig)

    def tr(out, in_):
        return S.op("pe", lambda e: e.transpose(out.ap, in_.ap, IDN.ap), reads=[in_, IDN], writes=[out])

    def act(out, in_, func, bias=None, scale=None, accum=None, eng="act"):
        rd = [in_]
        kw = {}
        if bias is not None:
            if isinstance(bias, V):
                rd.append(bias)
                kw["bias"] = bias.ap
            else:
                kw["bias"] = float(bias)
        if scale is not None:
            if isinstance(scale, V):
                rd.append(scale)
                kw["scale"] = scale.ap
            else:
                kw["scale"] = float(scale)
        wr = [out]
        if accum is not None:
            kw["accum_out"] = accum.ap
            wr.append(accum)
        return S.op("act", lambda e: e.activation(out.ap, in_.ap, func, **kw), reads=rd, writes=wr)

    def tt(out, a, b, op, eng="dve"):
        return S.op(eng, lambda e: e.tensor_tensor(out.ap, a.ap, b.ap, op), reads=[a, b], writes=[out])

    def ts(out, a, s1, s2, op0, op1=None, eng="dve"):
        rd = [a]
        s1a = s1.ap if isinstance(s1, V) else s1
        s2a = s2.ap if isinstance(s2, V) else s2
        if isinstance(s1, V):
            rd.append(s1)
        if isinstance(s2, V):
            rd.append(s2)
        if op1 is None:
            return S.op(eng, lambda e: e.tensor_scalar(out.ap, a.ap, s1a, None, op0), reads=rd, writes=[out])
        return S.op(eng, lambda e: e.tensor_scalar(out.ap, a.ap, s1a, s2a, op0, op1), reads=rd, writes=[out])

    def stt(out, a, s, b, op0, op1):
        rd = [a, b]
        sa = s.ap if isinstance(s, V) else s
        if isinstance(s, V):
            rd.append(s)
        return S.op("dve", lambda e: e.scalar_tensor_tensor(out.ap, a.ap, sa, b.ap, op0, op1), reads=rd, writes=[out])

    def cp(out, in_, eng="dve"):
        if eng == "act":
            return act(out, in_, AF.Copy)
        return S.op(eng, lambda e: e.tensor_copy(out.ap, in_.ap), reads=[in_], writes=[out])

    def ld(out, src_ap, queue="sp"):
        return S.dma(queue, out.ap, src_ap, reads=[], writes=[out], key=out.b)

    ld(V(cst_b, cst[:]), cst_d)
    ld(V(rmask_b, rmask[:]), rmask_d)
    ld(V(normT_b, normT[:]), normT_d)
    ld(V(wdwT_b, wdwT[:]), wdwT_d)
    ld(V(cvec_b, cvec[:]), cvec_d)
    ld(V(lnT_b, lnT[:]), lnT_d)
    ld(V(wshT_b, wshT[:]), wshT_d)
    ld(V(hvec_b, hvec[:]), hvec_d)
    ld(V(onw_b, onw[:]), onw_d)
    cp(V(ones_bf_b, ones_bf[:]), ONES)
    act(V(negA_b, negA[:]), V(hvec_b, hvec[:, 0, :]), AF.Exp)
    ts(V(negA_b, negA[:]), V(negA_b, negA[:]), -1.0, None, ALU.mult)
    for h in range(NH):
        S.op("dve", lambda e, h=h: e.memset(Sst[h][0][:], 0.0), writes=[V(Sst[h][1], Sst[h][0][:])])
    for j in range(KC):
        S.op("dve", lambda e, j=j: e.memset(gtail[j][0][:], 0.0), writes=[V(gtail[j][1], gtail[j][0][:])])
    for j in range(3 * NH):
        S.op("dve", lambda e, j=j: e.memset(ptail[j][0][:], 0.0), writes=[V(ptail[j][1], ptail[j][0][:])])

    scT, scT_b = sb("scT", [128, KD])
    badaT, badaT_b = sb("badaT", [128, 9 * KD])
    ld(V(scT_b, scT[:]), cT_d)
    ld(V(badaT_b, badaT[:]), badaT_d)
    act(V(scT_b, scT[:]), V(scT_b, scT[:]), AF.Silu)
    wada_v = wada_d.rearrange("(k p) n -> p k n", p=128)
    NMC = 9 * KD
    pm, pm_b = PS()
    for j in range(NMC):
        wt0, wb = xin[j % 2]
        wt = wt0[:].rearrange("p (k n) -> p k n", k=KD)
        S.dma("sp", wt, wada_v[:, :, j * 128:(j + 1) * 128], reads=[], writes=[V(wb, wt)], key=wb)
        for kc in range(KD):
            mm(V(pm_b, pm[:, j:j + 1]), V(wb, wt[:, kc, :]), V(scT_b, scT[:, kc:kc + 1]),
               start=(kc == 0), stop=(kc == KD - 1))
    tt(V(modsT_b, modsT[:]), V(pm_b, pm[:, 0:NMC]), V(badaT_b, badaT[:]), ALU.add)
    for i in range(3):
        sc = V(modsT_b, modsT[:, (3 * i + 1) * KD:(3 * i + 2) * KD])
        gt = V(modsT_b, modsT[:, (3 * i + 2) * KD:(3 * i + 3) * KD])
        stt(V(Acoef_b, Acoef[:, i, :]), sc, 1.0, V(normT_b, normT[:, i, :]), ALU.add, ALU.mult)
        ts(V(Gcoef_b, Gcoef[:, i, :]), gt, 0.5 if i != 1 else 1.0, None, ALU.mult)

    def shift_col(i, kc):
        return V(modsT_b, modsT[:, 3 * i * KD + kc:3 * i * KD + kc + 1])

    def rms_rstd():
        for kc in range(KD):
            act(V(sq_b, sq[:, kc, :]), V(xT_b, xT[:, kc, :]), AF.Square)
        p, pb = PS()
        for kc in range(KD):
            mm(V(pb, p[:, 0:TT]), V(ones_bf_b, ones_bf[:]), V(sq_b, sq[:, kc, :]), start=(kc == 0), stop=(kc == KD - 1))
        act(V(rstd_b, rstd[:]), V(pb, p[:, 0:TT]), AF.Sqrt, bias=EPS, scale=1.0 / D)
        S.op("dve", lambda e: e.reciprocal(rstd[:], rstd[:]), reads=[V(rstd_b, rstd[:])], writes=[V(rstd_b, rstd[:])])

    tmpn, tmpn_b = sb("tmpn", [128, TT])
    cacc, cacc_b = tmpn, tmpn_b

    def rms_mod(i):
        rms_rstd()
        for kc in range(KD):
            tt(V(tmpn_b, tmpn[:]), V(xT_b, xT[:, kc, :]), V(rstd_b, rstd[:]), ALU.mult)
            act(V(hT_b, hT[:, kc, :]), V(tmpn_b, tmpn[:]), AF.Identity,
                bias=shift_col(i, kc), scale=V(Acoef_b, Acoef[:, i, kc:kc + 1]))

    scr = {}
    cur_tile = [0]

    def wload(dst_pool, idx, src, K, tag):
        t, b = dst_pool[idx[0] % len(dst_pool)]
        idx[0] += 1
        if tag not in scr:
            dt_ = nc.dram_tensor("scr_" + tag, [128, K * 128], BF16, kind="Internal").ap()
            scr[tag] = (dt_, Buf("scr_" + tag))
        sap, sbuf_ = scr[tag]
        sview = sap.rearrange("p (k n) -> p k n", k=K)
        if cur_tile[0] == 0:
            S.dma("pool", t[:, 0:K, :], src, reads=[], writes=[V(b, t[:])], key=b)
            if NT > 1:
                S.dma("sp", sview, t[:, 0:K, :], reads=[V(b, t[:])], writes=[V(sbuf_, sap)], key=b)
        else:
            S.dma("sp", t[:, 0:K, :], sview, reads=[V(sbuf_, sap)], writes=[V(b, t[:])], key=b)
        return t, b

    sg, sg_b = sb("sg", [128, TT])

    def ffn(f, i):
        aT, aT_b = actT[0]
        wgv = wg_d[f].rearrange("(k p) n -> p k n", p=128)
        wuv = wu_d[f].rearrange("(k p) n -> p k n", p=128)
        wdv = wd_d[f].rearrange("(k p) n -> p k n", p=128)
        for hf in range(1 if FSPLIT else 2):
            for jj in range(KFH):
                j = hf * KFH + jj
                gt_, gb_ = wload(wa, wa_i, wgv[:, :, j * 128:(j + 1) * 128], KD, "g%d_%d" % (f, j))
                ut_, ub_ = wload(wa, wa_i, wuv[:, :, j * 128:(j + 1) * 128], KD, "u%d_%d" % (f, j))
                pg, pgb = PS()
                pu, pub = PS()
                for kc in range(KD):
                    mm(V(pgb, pg[:, 0:TT]), V(gb_, gt_[:, kc, :]), V(hT_b, hT[:, kc, :]), start=(kc == 0), stop=(kc == KD - 1))
                for kc in range(KD):
                    mm(V(pub, pu[:, 0:TT]), V(ub_, ut_[:, kc, :]), V(hT_b, hT[:, kc, :]), start=(kc == 0), stop=(kc == KD - 1))
                act(V(sg_b, sg[:]), V(pgb, pg[:, 0:TT]), AF.Silu)
                tt(V(aT_b, aT[:, jj, :]), V(pub, pu[:, 0:TT]), V(sg_b, sg[:]), ALU.mult)
            for m in range(KD):
                dt_, db_ = wload(wdp, wd_i, wdv[:, hf * KFH:(hf + 1) * KFH, m * 128:(m + 1) * 128], KFH, "d%d_%d_%d" % (f, hf, m))
                pd, pdb = PS()
                for kf in range(KFH):
                    mm(V(pdb, pd[:, 0:TT]), V(db_, dt_[:, kf, :]), V(aT_b, aT[:, kf, :]), start=(kf == 0), stop=(kf == KFH - 1))
                if not FSPLIT:
                    stt(V(xT_b, xT[:, m, :]), V(pdb, pd[:, 0:TT]), V(Gcoef_b, Gcoef[:, i, m:m + 1]), V(xT_b, xT[:, m, :]),
                        ALU.mult, ALU.add)
                    continue
                KH = KD // 2
                part, mo = m // KH, m % KH
                dsn = ("lnb", "beta")[m % 2]
                dsv = V(RB[dsn][1], RB[dsn][0][:])
                ts(dsv, V(pdb, pd[:, 0:TT]), V(Gcoef_b, Gcoef[:, i, m:m + 1]), None, ALU.mult)
                cinb = fx["cin%d" % part]
                S.dma("sp", cinb[0][:, mo * TT:(mo + 1) * TT], dsv.ap, reads=[dsv], writes=[V(cinb[1], cinb[0])], key=dsv.b)
                if mo == KH - 1:
                    coutb = fx["cout%d" % part]
                    S.coll(cinb[0], coutb[0], PAIRS, reads=[V(cinb[1], cinb[0])], writes=[V(coutb[1], coutb[0])], key=coutb[1])
            if FSPLIT:
                KH = KD // 2
                for m in range(KD):
                    part, mo = m // KH, m % KH
                    coutb = fx["cout%d" % part]
                    names = (("lsk", "Kd"), ("R", "P"))[m % 2]
                    for r_ in range(2):
                        tv_ = V(RB[names[r_]][1], RB[names[r_]][0][:])
                        S.dma("sp", tv_.ap, coutb[0][r_ * 128:(r_ + 1) * 128, mo * TT:(mo + 1) * TT],
                              reads=[V(coutb[1], coutb[0])], writes=[tv_], key=tv_.b)
                        tt(V(xT_b, xT[:, m, :]), V(xT_b, xT[:, m, :]), tv_, ALU.add)

    glu, glu_b = sb("glu", [128, 30 + TT])
    assert KCg * TT <= 2 * D
    ypre = xin_all[:].rearrange("p a d -> p (a d)")[:, 0:KCg * TT].rearrange("p (k t) -> p k t", k=KCg)
    ypre_b = xin_all_b
    ysq2 = [sb("ysq%d" % i, [128, TT]) for i in range(2)]
    pre, pre_b = glu, glu_b
    qkv = [sb("qkv%d" % i, [128, TT]) for i in range(3)]
    zs, zs_b = sb("zs", [128, TT])
    abT, abT_b = sb("abT", [64, TT])
    wab, wab_b = sb("wab", [128, KD, 32], BF16)
    RB = {n: sb("RB_" + n, [128, TT]) for n in ("lnb", "beta", "G", "lsk", "lsq", "R", "P", "Q", "Kd")}
    RB["t"] = RB["lsq"]
    RB["E"] = RB["lsq"]
    RB["g"] = RB["Kd"]
    RB["nEP"] = RB["lnb"]
    RB["nR"] = RB["lsk"]
    stk = [sb("stk%d" % i, [128, TT]) for i in range(2)]
    cols = [sb("cols%d" % i, [128, 128]) for i in range(2)]
    ktok, ktok_b = sb("ktok", [128, 128])
    vtok, vtok_b = sb("vtok", [128, 128])
    kd_t, kd_b = sb("kd_t", [128, 128])
    mA, mA_b = sb("mA", [128, 128])
    mB, mB_b = sb("mB", [128, 128])
    NAt, NA_b = sb("NAt", [128, 256])
    NBt, NB_b = sb("NBt", [128, 256])
    tA, tA_b = sb("tA", [128, 128])
    tB, tB_b = sb("tB", [128, 128])
    tY, tY_b = sb("tY", [128, 256])
    mean_t, mean_b = RB["R"]
    rs2, rs2_b = RB["P"]
    aTt, aTt_b = sb("aTt", [128, 128])
    qeT, qeT_b = sb("qeT", [128, 128])
    TbT, TbT_b = sb("TbT", [128, 128])
    TwT, TwT_b = sb("TwT", [128, 128])
    u_t, u_b = sb("u_t", [128, 128])
    nwT, nwT_b = sb("nwT", [128, 128])
    vnew, vnew_b = sb("vnew", [128, 128])
    on_t, on_b = TbT, TbT_b
    sml, sml_b = sb("sml", [128, 8])
    for i in range(2):
        S.op("dve", lambda e, i=i: e.memset(stk[i][0][:], 0.0), writes=[V(stk[i][1], stk[i][0][:])])
    S.op("dve", lambda e: e.memset(abT[:], 0.0), writes=[V(abT_b, abT[:])])
    selm, selm_b = sb("selm", [64, 2, 128])

    def build_sel(h):
        for r in range(2):
            rr = h if r == 0 else 32 + h
            ts(V(selm_b, selm[:, r, :]), V(cst_b, cst[0:64, 1, :]), V(cst_b, cst[0:64, 0, rr:rr + 1]), None, ALU.mult)

    gx = {}
    PAIRS = [[2 * i, 2 * i + 1] for i in range(cfg.get("ncores", 8) // 2)]
    if SPLIT:
        for nm_, shp_, dt_ in (("cin1", [128, KC * TT], F32), ("cout1", [256, KC * TT], F32),
                               ("cin2", [128, NH * TT], BF16), ("cout2", [256, NH * TT], BF16)):
            gx[nm_] = (nc.dram_tensor("gx_" + nm_, shp_, dt_, kind="Internal").ap(), Buf("gx_" + nm_))
    fx = {}
    if FSPLIT:
        for pi_ in range(2):
            fx["cin%d" % pi_] = (nc.dram_tensor("fx_cin%d" % pi_, [128, (KD // 2) * TT], F32, kind="Internal").ap(), Buf("fx_cin%d" % pi_))
            fx["cout%d" % pi_] = (nc.dram_tensor("fx_cout%d" % pi_, [256, (KD // 2) * TT], F32, kind="Internal").ap(), Buf("fx_cout%d" % pi_))
    win_v = win_d.rearrange("(k p) n -> p k n", p=128)
    wout_v = wout_d.rearrange("(k p) n -> p k n", p=128)
    QB = 2 * CW

    def proj(col0):
        wt, wb = wload(wa, wa_i, win_v[:, :, col0:col0 + 128], KD, "in%d" % col0)
        p, pb = PS()
        for kc in range(KD):
            mm(V(pb, p[:, 0:TT]), V(wb, wt[:, kc, :]), V(hT_b, hT[:, kc, :]), start=(kc == 0), stop=(kc == KD - 1))
        return V(pb, p[:, 0:TT])

    def rb(n, sl=None):
        t, b = RB[n]
        return V(b, t[:] if sl is None else t[:, sl])

    class DS:
        pass

    def mk_set0():
        d = DS()
        d.qkv = qkv
        d.zs = (zs, zs_b)
        d.RB = RB
        d.stk = stk
        d.cols = cols
        d.t = dict(ktok=(ktok, ktok_b), vtok=(vtok, vtok_b), kd=(kd_t, kd_b), mA=(mA, mA_b), mB=(mB, mB_b),
                   NA=(NAt, NA_b), NB=(NBt, NB_b), tA=(tA, tA_b), tB=(tB, tB_b), tY=(tY, tY_b), aT=(aTt, aTt_b),
                   qeT=(qeT, qeT_b), TbT=(TbT, TbT_b), TwT=(TwT, TwT_b), u=(u_t, u_b), nwT=(nwT, nwT_b),
                   vnew=(vnew, vnew_b), sml=(sml, sml_b))
        return d

    st1_bufs = []
    aT_alias_b = actT[0][1]

    def mk_set1():
        need_a = 11 * TT
        arenaA = actT[0][0][:].rearrange("p k t -> p (k t)").bitcast(F32)
        arenas = [arenaA] + [w[0][:].rearrange("p k n -> p (k n)").bitcast(F32) for w in wdp]
        sizes = [max(KFH, KD) * TT // 2] + [KFH * 64] * len(wdp)
        use_alias = (sizes[0] >= need_a) and (sizes[1] >= 1408) and len(wdp) >= 2 and TT == 512
        pos = [0] * len(arenas)

        def al(name, n, ai):
            if not use_alias:
                t_, b_ = sb("s1_" + name, [128, n])
                return t_[:], b_
            assert pos[ai] + n <= sizes[ai], (name, ai, pos[ai], n, sizes[ai])
            v_ = arenas[ai][:, pos[ai]:pos[ai] + n]
            pos[ai] += n
            b_ = Buf("s1_" + name)
            st1_bufs.append(b_)
            return v_, b_
        d = DS()
        d.qkv = [al("qkv%d" % i, TT, 0) for i in range(3)]
        d.zs = al("zs", TT, 0)
        d.RB = dict(RB)
        for n_ in ("R", "P", "Q", "lsq", "G"):
            d.RB[n_] = al("RB_" + n_, TT, 0)
        d.RB["t"] = d.RB["lsq"]
        d.RB["E"] = d.RB["lsq"]
        d.stk = [al("stk%d" % i, TT, 0) for i in range(2)]
        d.cols = [al("cols%d" % i, 128, 1) for i in range(2)]
        d.t = {}
        for n_ in ("ktok", "vtok", "kd", "mA", "mB"):
            d.t[n_] = al(n_, 128, 1)
        d.t["NA"] = al("NA", 256, 1)
        d.t["NB"] = al("NB", 256, 1)
        d.t["tA"] = al("tA", 128, 2)
        d.t["tB"] = al("tB", 128, 2)
        d.t["tY"] = al("tY", 256, 2)
        for n_ in ("aT", "qeT", "TbT", "TwT", "u", "nwT", "vnew"):
            d.t[n_] = al(n_, 128, 2)
        sm_, smb_ = sb("s1_sml", [128, 8])
        d.t["sml"] = (sm_[:], smb_)
        d.alias = use_alias
        return d

    def inherit(dst, src):
        acc = []
        for b_ in src:
            if b_.w is not None:
                acc.append(b_.w)
            acc.extend(b_.r)
        for b_ in dst:
            b_.r = list(b_.r) + acc

    dsets = [mk_set0(), mk_set1()]
    for i in range(2):
        S.op("dve", lambda e, i=i: e.memset(dsets[1].stk[i][0], 0.0), writes=[V(dsets[1].stk[i][1], dsets[1].stk[i][0])])

    def dn_prep(h, st):
        qkv_, RB_ = st.qkv, st.RB
        zs_, zs_b_ = st.zs

        def rbs(n, sl=None):
            t_, b_ = RB_[n]
            return V(b_, t_[:] if sl is None else t_[:, sl])
        for i3 in range(3):
            pp = proj(QB + i3 * DNW + h * 128)
            ci = i3 * NH + h
            cp(V(pre_b, pre[:, 0:3]), V(ptail[ci][1], ptail[ci][0][:]))
            cp(V(pre_b, pre[:, 3:3 + TT]), pp, eng="act")
            cp(V(ptail[ci][1], ptail[ci][0][:]), V(pre_b, pre[:, TT:TT + 3]))
            for k in range(4):
                wk = V(wshT_b, wshT[:, ci, k:k + 1])
                if k == 0:
                    ts(V(cacc_b, cacc[:]), V(pre_b, pre[:, 0:TT]), wk, None, ALU.mult)
                else:
                    stt(V(cacc_b, cacc[:]), V(pre_b, pre[:, k:k + TT]), wk, V(cacc_b, cacc[:]), ALU.mult, ALU.add)
            act(V(qkv_[i3][1], qkv_[i3][0][:]), V(cacc_b, cacc[:]), AF.Silu)
        pz = proj(QB + 3 * DNW + h * 128)
        act(V(zs_b_, zs_[:]), pz, AF.Silu)
        qT, kT, vT = [V(qkv_[i][1], qkv_[i][0][:]) for i in range(3)]
        pb1, pb1_b = PS()
        build_sel(h)
        mm(V(pb1_b, pb1[:, 0:TT]), V(selm_b, selm[:, 0, :]), V(abT_b, abT[:]))
        pa1, pa1_b = PS()
        mm(V(pa1_b, pa1[:, 0:TT]), V(selm_b, selm[:, 1, :]), V(abT_b, abT[:]))
        act(rbs("t"), V(pb1_b, pb1[:, 0:TT]), AF.Exp, scale=-1.0)
        act(rbs("lnb"), rbs("t"), AF.Ln, bias=1.0)
        ts(rbs("lnb"), rbs("lnb"), -1.0, None, ALU.mult)
        act(rbs("beta"), rbs("lnb"), AF.Exp)
        act(rbs("t"), V(pa1_b, pa1[:, 0:TT]), AF.Exp, bias=V(hvec_b, hvec[:, 1, h:h + 1]))
        act(rbs("g"), rbs("t"), AF.Ln, bias=1.0)
        ts(rbs("g"), rbs("g"), V(negA_b, negA[:, h:h + 1]), None, ALU.mult)
        S.op("dve", lambda e: e.tensor_tensor_scan(RB_["G"][0][:], rmask[:], RB_["g"][0][:], 0.0, ALU.mult, ALU.add),
             reads=[V(rmask_b, rmask[:]), rbs("g")], writes=[rbs("G")])
        for (src, dst) in ((kT, "lsk"), (qT, "lsq")):
            act(V(cacc_b, cacc[:]), src, AF.Square)
            pss, pss_b = PS()
            mm(V(pss_b, pss[:, 0:TT]), ONES, V(cacc_b, cacc[:]))
            act(rbs(dst), V(pss_b, pss[:, 0:TT]), AF.Ln, bias=EPS)
        stt(rbs("R"), rbs("lsk"), 0.5, rbs("G"), ALU.mult, ALU.add)
        stt(rbs("P"), rbs("lsk"), -0.5, rbs("G"), ALU.mult, ALU.add)
        tt(rbs("P"), rbs("P"), rbs("lnb"), ALU.add)
        stt(rbs("Q"), rbs("lsq"), -0.5, rbs("G"), ALU.mult, ALU.add)
        ts(rbs("Q"), rbs("Q"), float(np.log(128.0 ** -0.5)), None, ALU.add)
        act(rbs("E"), rbs("Q"), AF.Exp)
        act(rbs("nEP"), rbs("P"), AF.Exp)
        ts(rbs("nEP"), rbs("nEP"), -1.0, None, ALU.mult)
        ts(rbs("nR"), rbs("R"), -1.0, None, ALU.mult)
        for blk in range(NB):
            bs = slice(blk * 128, (blk + 1) * 128)
            last = blk * 128 + 127
            glast = V(RB_["G"][1], RB_["G"][0][:, last:last + 1])
            act(rbs("Kd", bs), rbs("nR", bs), AF.Exp, bias=glast)
        for (pi, nme) in ((0, "nR"), (32, "P"), (64, "beta"), (96, "nEP")):
            cp(V(st.stk[0][1], st.stk[0][0][pi:pi + 1, :]), V(RB_[nme][1], RB_[nme][0][pi:pi + 1, :]))
        cp(V(st.stk[1][1], st.stk[1][0][0:1, :]), V(RB_["Kd"][1], RB_["Kd"][0][0:1, :]))

    def dn_blocks(h, st):
        qkv_, RB_ = st.qkv, st.RB
        zs_, zs_b_ = st.zs
        T_ = st.t

        def rbs(n, sl=None):
            t_, b_ = RB_[n]
            return V(b_, t_[:] if sl is None else t_[:, sl])

        def tv(n, sl=None):
            t_, b_ = T_[n]
            return V(b_, t_[:] if sl is None else t_[:, sl])
        Sb, Sbb = Sst[h]
        SV = V(Sbb, Sb[:])
        for blk in range(NB):
            bs = slice(blk * 128, (blk + 1) * 128)
            last = blk * 128 + 127
            for i in range(2):
                pc, pcb = PS()
                tr(V(pcb, pc[:, 0:128]), V(st.stk[i][1], st.stk[i][0][:, bs]))
                cp(V(st.cols[i][1], st.cols[i][0][:]), V(pcb, pc[:, 0:128]), eng="act")
            c0, c0b = st.cols[0]
            cnR = V(c0b, c0[:, 0:1])
            cP = V(c0b, c0[:, 32:33])
            cbeta = V(c0b, c0[:, 64:65])
            cnEP = V(c0b, c0[:, 96:97])
            cKd = V(st.cols[1][1], st.cols[1][0][:, 0:1])
            qTb = V(qkv_[0][1], qkv_[0][0][:, bs])
            kTb = V(qkv_[1][1], qkv_[1][0][:, bs])
            vTb = V(qkv_[2][1], qkv_[2][0][:, bs])
            ptk, ptk_b = PS()
            tr(V(ptk_b, ptk[:, 0:128]), kTb)
            tr(V(ptk_b, ptk[:, 128:256]), vTb)
            pkk, pkk_b = PS()
            mm(V(pkk_b, pkk[:, 0:128]), kTb, kTb)
            mm(V(pkk_b, pkk[:, 128:256]), kTb, qTb)
            yield
            cp(tv("ktok"), V(ptk_b, ptk[:, 0:128]), eng="act")
            cp(tv("vtok"), V(ptk_b, ptk[:, 128:256]), eng="act")
            ts(tv("kd"), V(ptk_b, ptk[:, 0:128]), cKd, None, ALU.mult)
            tt(tv("mA"), NEGS, rbs("R", bs), ALU.subtract)
            act(tv("mA"), tv("mA"), AF.Exp, bias=cP)
            tt(tv("mB"), NEGST, rbs("P", bs), ALU.add)
            act(tv("mB"), tv("mB"), AF.Exp, bias=cnR)
            tt(tv("aT"), NEGTT, rbs("Q", bs), ALU.add)
            act(tv("aT"), tv("aT"), AF.Exp, bias=cnR)
            yield
            stt(tv("mA"), V(pkk_b, pkk[:, 0:128]), -1.0, tv("mA"), ALU.mult, ALU.mult)
            stt(tv("mB"), V(pkk_b, pkk[:, 0:128]), -1.0, tv("mB"), ALU.mult, ALU.mult)
            tt(tv("aT"), V(pkk_b, pkk[:, 128:256]), tv("aT"), ALU.mult)
            tt(tv("qeT"), qTb, rbs("E", bs), ALU.mult)
            XTv = tv("NA", slice(128, 256))
            XUv = tv("NB", slice(128, 256))
            tt(tv("NA", slice(0, 128)), tv("mA"), BD16, ALU.mult)
            tt(tv("NB", slice(0, 128)), tv("mB"), BD16, ALU.mult)
            cp(XTv, IDN)
            cp(XUv, IDN)
            yield
            for lev in range(4):
                p1, p1b = PS()
                p2, p2b = PS()
                if lev == 3:
                    mm(V(p1b, p1[:, 128:256]), tv("NA", slice(0, 128)), XUv)
                    mm(V(p2b, p2[:, 128:256]), tv("NB", slice(0, 128)), XTv)
                    yield
                else:
                    mm(V(p1b, p1[:, 0:256]), tv("NA", slice(0, 128)), tv("NB", slice(0, 256)))
                    mm(V(p2b, p2[:, 0:256]), tv("NB", slice(0, 128)), tv("NA", slice(0, 256)))
                    yield
                    cp(tv("NB", slice(0, 128)), V(p1b, p1[:, 0:128]), eng="act")
                    cp(tv("NA", slice(0, 128)), V(p2b, p2[:, 0:128]), eng="act")
                tt(XUv, V(p1b, p1[:, 128:256]), XUv, ALU.add)
                tt(XTv, V(p2b, p2[:, 128:256]), XTv, ALU.add)
                yield
            for li in range(3):
                lastm = (li == 2)
                tt(tv("tA"), tv("mA"), LLm[li], ALU.mult)
                pY, pYb = PS()
                mm(V(pYb, pY[:, 0:128]), tv("tA"), XUv)
                if not lastm:
                    tt(tv("tB"), tv("mB"), URm[li], ALU.mult)
                    mm(V(pYb, pY[:, 128:256]), tv("tB"), XTv)
                    yield
                    cp(tv("tY", slice(0, 256)), V(pYb, pY[:, 0:256]), eng="act")
                else:
                    yield
                    cp(tv("tY", slice(0, 128)), V(pYb, pY[:, 0:128]), eng="act")
                pZ, pZb = PS()
                mm(V(pZb, pZ[:, 0:128]), XTv, tv("tY", slice(0, 128)))
                if not lastm:
                    mm(V(pZb, pZ[:, 128:256]), XUv, tv("tY", slice(128, 256)))
                yield
                tt(XUv, V(pZb, pZ[:, 0:128]), XUv, ALU.add)
                if not lastm:
                    tt(XTv, V(pZb, pZ[:, 128:256]), XTv, ALU.add)
            ts(tv("TbT"), XUv, cbeta, None, ALU.mult)
            ts(tv("TwT"), XUv, cnEP, None, ALU.mult)
            pu_, pu_b = PS()
            mm(V(pu_b, pu_[:, 0:128]), tv("TbT"), tv("vtok"))
            mm(V(pu_b, pu_[:, 128:256]), tv("ktok"), tv("TwT"))
            yield
            cp(tv("u"), V(pu_b, pu_[:, 0:128]), eng="act")
            cp(tv("nwT"), V(pu_b, pu_[:, 128:256]), eng="act")
            pv, pvb = PS()
            mm(V(pvb, pv[:, 0:128]), tv("nwT"), SV)
            yield
            tt(tv("vnew"), V(pvb, pv[:, 0:128]), tv("u"), ALU.add)
            po, pob = PS()
            mm(V(pob, po[:, 0:128]), tv("qeT"), SV, start=True, stop=False)
            mm(V(pob, po[:, 0:128]), tv("aT"), tv("vnew"), start=False, stop=True)
            pds, pdsb = PS()
            mm(V(pdsb, pds[:, 0:128]), tv("kd"), tv("vnew"))
            act(tv("sml", slice(0, 1)), V(RB_["G"][1], RB_["G"][0][:, last:last + 1]), AF.Exp)
            yield
            stt(SV, SV, tv("sml", slice(0, 1)), V(pdsb, pds[:, 0:128]), ALU.mult, ALU.add)
            act(tv("TbT"), V(pob, po[:, 0:128]), AF.Square, accum=tv("sml", slice(1, 2)))
            act(tv("sml", slice(2, 3)), tv("sml", slice(1, 2)), AF.Sqrt, bias=EPS, scale=1.0 / 128)
            smt, smb = T_["sml"]
            S.op("dve", lambda e, smt=smt: e.reciprocal(smt[:, 3:4], smt[:, 2:3]), reads=[tv("sml", slice(2, 3))], writes=[tv("sml", slice(3, 4))])
            ts(tv("TbT"), V(pob, po[:, 0:128]), tv("sml", slice(3, 4)), None, ALU.mult)
            pt2, pt2b = PS()
            tr(V(pt2b, pt2[:, 0:128]), tv("TbT"))
            yield
            stt(V(yT_b, yT[:, KCg + h, bs]), V(pt2b, pt2[:, 0:128]), V(onw_b, onw[:, 0:1]), V(zs_b_, zs_[:, bs]), ALU.mult, ALU.mult)

    def mixer():
        for j in range(KC):
            pa = proj(j * 128)
            pgt = proj(CW + j * 128)
            act(V(sg_b, sg[:]), pgt, AF.Sigmoid)
            cp(V(glu_b, glu[:, 0:30]), V(gtail[j][1], gtail[j][0][:]))
            tt(V(glu_b, glu[:, 30:30 + TT]), pa, V(sg_b, sg[:]), ALU.mult)
            cp(V(gtail[j][1], gtail[j][0][:]), V(glu_b, glu[:, TT:TT + 30]))
            for k in range(31):
                wk = V(wdwT_b, wdwT[:, j, k:k + 1])
                if k == 0:
                    ts(V(cacc_b, cacc[:]), V(glu_b, glu[:, 0:TT]), wk, V(cvec_b, cvec[:, 0, j:j + 1]), ALU.mult, ALU.add)
                elif k < 30:
                    stt(V(cacc_b, cacc[:]), V(glu_b, glu[:, k:k + TT]), wk, V(cacc_b, cacc[:]), ALU.mult, ALU.add)
                else:
                    stt(V(ypre_b, ypre[:, j, :]), V(glu_b, glu[:, k:k + TT]), wk, V(cacc_b, cacc[:]), ALU.mult, ALU.add)
        if CUT <= 1:
            return
        S.dma("pool", wab[:, :, 0:NH], win_v[:, :, QB + 4 * DNW:QB + 4 * DNW + NH], reads=[], writes=[V(wab_b, wab[:])], key=wab_b, slow=True)
        S.dma("pool", wab[:, :, 16:16 + NH], win_v[:, :, QB + 4 * DNW + NH:QB + 4 * DNW + 2 * NH], reads=[], writes=[V(wab_b, wab[:])], key=wab_b, slow=True)
        pab, pab_b = PS()
        for kc in range(KD):
            mm(V(pab_b, pab[0:NH, 0:TT]), V(wab_b, wab[:, kc, 0:NH]), V(hT_b, hT[:, kc, :]), start=(kc == 0), stop=(kc == KD - 1))
        cp(V(abT_b, abT[0:NH, :]), V(pab_b, pab[0:NH, 0:TT]))
        pab2, pab2_b = PS()
        for kc in range(KD):
            mm(V(pab2_b, pab2[32:32 + NH, 0:TT]), V(wab_b, wab[:, kc, 16:16 + NH]), V(hT_b, hT[:, kc, :]), start=(kc == 0), stop=(kc == KD - 1))
        cp(V(abT_b, abT[32:32 + NH, :]), V(pab2_b, pab2[32:32 + NH, 0:TT]))

        if CUT <= 2:
            return
        inherit([bb for bb in st1_bufs], [aT_alias_b] + [w[1] for w in wdp])
        for hp in range(0, NH, 2):
            hs = [hh for hh in (hp, hp + 1) if hh < NH]
            for i, hh in enumerate(hs):
                dn_prep(hh, dsets[i])
            gens = [dn_blocks(hh, dsets[i]) for i, hh in enumerate(hs)]
            while gens:
                for g in list(gens):
                    try:
                        next(g)
                    except StopIteration:
                        gens.remove(g)
        inherit([aT_alias_b] + [w[1] for w in wdp], [bb for bb in st1_bufs])

        if SPLIT:
            yv = yT[:, KCg:KCg + NHg, :].rearrange("p k t -> p (k t)")
            ypv = ypre.rearrange("p k t -> p (k t)")
            S.dma("sp", gx["cin1"][0], ypv[:, 0:KC * TT], reads=[V(ypre_b, ypre)], writes=[V(gx["cin1"][1], gx["cin1"][0])], key=ypre_b)
            S.dma("sp", gx["cin2"][0], yv[:, 0:NH * TT], reads=[V(yT_b, yT[:])], writes=[V(gx["cin2"][1], gx["cin2"][0])], key=yT_b)
            S.coll(gx["cin1"][0], gx["cout1"][0], PAIRS, reads=[V(gx["cin1"][1], gx["cin1"][0])], writes=[V(gx["cout1"][1], gx["cout1"][0])], key=gx["cout1"][1])
            S.coll(gx["cin2"][0], gx["cout2"][0], PAIRS, reads=[V(gx["cin2"][1], gx["cin2"][0])], writes=[V(gx["cout2"][1], gx["cout2"][0])], key=gx["cout2"][1])
            S.dma("sp", ypv.rearrange("p (r c) -> p r c", r=2), gx["cout1"][0].rearrange("(r p) c -> p r c", p=128),
                  reads=[V(gx["cout1"][1], gx["cout1"][0])], writes=[V(ypre_b, ypre)], key=ypre_b)
            S.dma("sp", yv.rearrange("p (r c) -> p r c", r=2), gx["cout2"][0].rearrange("(r p) c -> p r c", p=128),
                  reads=[V(gx["cout2"][1], gx["cout2"][0])], writes=[V(yT_b, yT[:])], key=yT_b)
        pmn, pmn_b = PS()
        pvr, pvr_b = PS()
        for j in range(KCg):
            mm(V(pmn_b, pmn[:, 0:TT]), ONES, V(ypre_b, ypre[:, j, :]), start=(j == 0), stop=(j == KCg - 1))
        for j in range(KCg):
            yq, yqb = ysq2[j % 2]
            act(V(yqb, yq[:]), V(ypre_b, ypre[:, j, :]), AF.Square)
            mm(V(pvr_b, pvr[:, 0:TT]), ONES, V(yqb, yq[:]), start=(j == 0), stop=(j == KCg - 1), sig=True)
        ts(V(mean_b, mean_t[:]), V(pmn_b, pmn[:, 0:TT]), 1.0 / (KCg * 128), None, ALU.mult)
        tt(V(rs2_b, rs2[:]), V(mean_b, mean_t[:]), V(mean_b, mean_t[:]), ALU.mult)
        stt(V(rs2_b, rs2[:]), V(pvr_b, pvr[:, 0:TT]), 1.0 / (KCg * 128), V(rs2_b, rs2[:]), ALU.mult, ALU.subtract)
        act(V(rs2_b, rs2[:]), V(rs2_b, rs2[:]), AF.Sqrt, bias=EPS, scale=1.0)
        S.op("dve", lambda e: e.reciprocal(rs2[:], rs2[:]), reads=[V(rs2_b, rs2[:])], writes=[V(rs2_b, rs2[:])])
        for j in range(KCg):
            tt(V(cacc_b, cacc[:]), V(ypre_b, ypre[:, j, :]), V(mean_b, mean_t[:]), ALU.subtract)
            tt(V(cacc_b, cacc[:]), V(cacc_b, cacc[:]), V(rs2_b, rs2[:]), ALU.mult)
            act(V(yT_b, yT[:, j, :]), V(cacc_b, cacc[:]), AF.Silu, bias=V(lnT_b, lnT[:, 1, j:j + 1]),
                scale=V(lnT_b, lnT[:, 0, j:j + 1]))

        if CUT <= 11:
            return
        for m in range(KD):
            wt, wb = wload(wa, wa_i, wout_v[:, :, m * 128:(m + 1) * 128], KY, "out%d" % m)
            p, pb = PS()
            for kc in range(KY):
                mm(V(pb, p[:, 0:TT]), V(wb, wt[:, kc, :]), V(yT_b, yT[:, kc, :]), start=(kc == 0), stop=(kc == KY - 1))
            stt(V(xT_b, xT[:, m, :]), V(pb, p[:, 0:TT]), V(Gcoef_b, Gcoef[:, 1, m:m + 1]), V(xT_b, xT[:, m, :]), ALU.mult, ALU.add)

    out_bufs = []
    for t in range(NT):
        cur_tile[0] = t
        for tb in range(NB):
            xt, xb = xin[tb % 2]
            r0 = t * TT + tb * 128
            ld(V(xb, xt[:]), x_d[r0:r0 + 128, :])
            for k0 in range(0, KD, 4):
                p, pb = PS()
                nk = min(4, KD - k0)
                for kk in range(nk):
                    tr(V(pb, p[:, kk * 128:(kk + 1) * 128]), V(xb, xt[:, (k0 + kk) * 128:(k0 + kk + 1) * 128]))
                S.op("dve", lambda e, p=p, k0=k0, nk=nk, tb=tb: e.tensor_copy(
                    xT[:, k0:k0 + nk, tb * 128:(tb + 1) * 128], p[:, 0:nk * 128].rearrange("p (k t) -> p k t", k=nk)),
                    reads=[V(pb, p[:])], writes=[V(xT_b, xT[:])])
        rms_mod(0)
        ffn(0, 0)
        if stop != "ffn1":
            rms_mod(1)
            mixer()
            if stop != "mixer":
                rms_mod(2)
                ffn(1, 2)
        if stop is None:
            rms_rstd()
            for kc in range(KD):
                tt(V(tmpn_b, tmpn[:]), V(xT_b, xT[:, kc, :]), V(rstd_b, rstd[:]), ALU.mult)
                ts(V(xT_b, xT[:, kc, :]), V(tmpn_b, tmpn[:]), V(normT_b, normT[:, 3, kc:kc + 1]), None, ALU.mult)
        for tb in range(NB):
            xt, xb = xin[tb % 2]
            r0 = t * TT + tb * 128
            for k0 in range(0, KD, 4):
                p, pb = PS()
                nk = min(4, KD - k0)
                for kk in range(nk):
                    tr(V(pb, p[:, kk * 128:(kk + 1) * 128]), V(xT_b, xT[:, k0 + kk, tb * 128:(tb + 1) * 128]))
                cp(V(xb, xt[:, k0 * 128:(k0 + nk) * 128]), V(pb, p[:, 0:nk * 128]), eng="act")
            S.dma("sp", y_d[r0:r0 + 128, :], xt[:], reads=[V(xb, xt[:])], writes=[], key=xb)
            if xb not in out_bufs:
                out_bufs.append(xb)
    S.emit(final_waits=out_bufs)
    es.close()
    return nc


def host_inputs(cfg, core, x, c, w_ada, b_ada, ffn1_norm, ffn1_wg, ffn1_wu, ffn1_wd, mix_norm, w_in,
                w_dw, b_dw, conv_ln_w, conv_ln_b, w_short, a_log, dt_bias, dn_norm_w, w_out,
                ffn2_norm, ffn2_wg, ffn2_wu, ffn2_wd, final_norm):
    KD, KC, NH, TT = cfg["KD"], cfg["KC"], cfg["NH"], cfg["TT"]
    split = cfg.get("split", 0)
    CWg, NHg, CW = cfg["CWg"], cfg["NHg"], cfg["CW"]
    KCg = CWg // 128
    if split:
        b, rank = core // 2, core % 2
    else:
        b, rank = core, 0
    f = np.float32
    A = np.ascontiguousarray

    def fm(v, k):
        return A(np.asarray(v, f).reshape(k, 128).T)

    consts = np.zeros((128, 12, 128), f)
    idx = np.arange(128)
    consts[:, 0, :] = np.eye(128, dtype=f)
    consts[:, 1, :] = 1.0
    consts[:, 2, :] = np.where(idx[:, None] > idx[None, :], 0.0, NEG)
    consts[:, 3, :] = np.where(idx[None, :] > idx[:, None], 0.0, NEG)
    consts[:, 4, :] = np.where(idx[None, :] >= idx[:, None], 0.0, NEG)
    consts[:, 5, :] = (idx[:, None] // 16 == idx[None, :] // 16)
    for i, bsz in enumerate((16, 32, 64)):
        ll = ((idx[:, None] // (2 * bsz) == idx[None, :] // (2 * bsz)) & (idx[:, None] % (2 * bsz) >= bsz)
              & (idx[None, :] % (2 * bsz) < bsz)).astype(f)
        consts[:, 6 + i, :] = ll
        consts[:, 9 + i, :] = ll.T
    rmask = np.ones((128, TT), f)
    rmask[:, ::128] = 0.0
    DNWg = NHg * 128
    DNW = NH * 128
    c0, h0 = rank * CW, rank * NH
    cols = np.concatenate([
        np.arange(c0, c0 + CW), CWg + np.arange(c0, c0 + CW),
        2 * CWg + 0 * DNWg + h0 * 128 + np.arange(DNW), 2 * CWg + 1 * DNWg + h0 * 128 + np.arange(DNW),
        2 * CWg + 2 * DNWg + h0 * 128 + np.arange(DNW), 2 * CWg + 3 * DNWg + h0 * 128 + np.arange(DNW),
        2 * CWg + 4 * DNWg + h0 + np.arange(NH), 2 * CWg + 4 * DNWg + NHg + h0 + np.arange(NH)])
    shc = np.concatenate([i3 * DNWg + h0 * 128 + np.arange(DNW) for i3 in range(3)])
    w_in_l = np.asarray(w_in[0], f)[:, cols]
    w_dw_l = np.asarray(w_dw[0], f)[:, c0:c0 + CW]
    w_sh_l = np.asarray(w_short[0], f)[:, shc]
    z1 = np.zeros(CW, f)
    DFFl = cfg["DFF"] // (1 + split)
    fs = slice(rank * DFFl, (rank + 1) * DFFl)
    return {
        "x": A(np.asarray(x[b], f)),
        "cT": fm(c[b], KD),
        "w_ada": A(np.asarray(w_ada[0], f)),
        "b_adaT": fm(b_ada[0], 9 * KD),
        "normT": A(np.stack([fm(ffn1_norm[0], KD), fm(mix_norm[0], KD), fm(ffn2_norm[0], KD), fm(final_norm, KD)], axis=1)),
        "ffn1_wg": A(np.asarray(ffn1_wg[0], f)[:, fs]), "ffn1_wu": A(np.asarray(ffn1_wu[0], f)[:, fs]), "ffn1_wd": A(np.asarray(ffn1_wd[0], f)[fs, :]),
        "ffn2_wg": A(np.asarray(ffn2_wg[0], f)[:, fs]), "ffn2_wu": A(np.asarray(ffn2_wu[0], f)[:, fs]), "ffn2_wd": A(np.asarray(ffn2_wd[0], f)[fs, :]),
        "w_in": A(w_in_l),
        "w_out": A(np.asarray(w_out[0], f)),
        "w_dwT": A(w_dw_l.T.reshape(KC, 128, 31).transpose(1, 0, 2)),
        "cvecT": A(np.stack([fm(np.asarray(b_dw[0], f)[c0:c0 + CW], KC), fm(z1, KC), fm(z1, KC)], axis=1)),
        "lnT": A(np.stack([fm(conv_ln_w[0], KCg), fm(conv_ln_b[0], KCg)], axis=1)),
        "w_shT": A(w_sh_l.T.reshape(3 * NH, 128, 4).transpose(1, 0, 2)),
        "hvec": A(np.broadcast_to(np.stack([np.asarray(a_log[0], f)[h0:h0 + NH], np.asarray(dt_bias[0], f)[h0:h0 + NH]], axis=0)[None], (128, 2, NH))),
        "onwT": A(np.asarray(dn_norm_w[0], f).reshape(128, 1)),
        "consts": consts,
        "rmask": rmask,
    }


_NC_CACHE = {}


def kernel(**inputs):
    cfg = FULL
    B = inputs["x"].shape[0]
    if "nc" not in _NC_CACHE:
        _NC_CACHE["nc"] = build(cfg)
    nc = _NC_CACHE["nc"]
    n = 8
    in_maps = [host_inputs(cfg, i, **inputs) for i in range(n)]
    res = run_bass_kernel_spmd(nc, in_maps, core_ids=list(range(n)))
    out = np.stack([res.results[2 * i]["y"] for i in range(B)], axis=0)
    return out.astype(np.float32)
```

```python
import numpy as np
import os
CUT = int(os.environ.get('KCUT', '99'))
from contextlib import ExitStack
import concourse.bass as bass
import concourse.mybir as mybir
from concourse.bass_utils import run_bass_kernel_spmd

F32 = mybir.dt.float32
BF16 = mybir.dt.bfloat16
AF = mybir.ActivationFunctionType
ALU = mybir.AluOpType
NEG = -1.0e9
EPS = 1e-6


class Buf:
    __slots__ = ("name", "w", "r", "dsem", "psum")

    def __init__(self, name, psum=False):
        self.name = name
        self.psum = psum
        self.w = None
        self.r = []
        self.dsem = None


class V:
    __slots__ = ("b", "ap")

    def __init__(self, b, ap):
        self.b = b
        self.ap = ap


class DSem:
    def __init__(self, sem):
        self.sem = sem
        self.count = 0


class Rec:
    __slots__ = ("eng", "fn", "deps", "sig", "idx", "dma", "dval", "dinc")

    def __init__(self, eng, fn, deps, sig, dma=None):
        self.eng = eng
        self.fn = fn
        self.deps = deps
        self.sig = sig
        self.dma = dma
        self.dval = 0
        self.dinc = 16
        self.idx = -1


class Sched:
    ENGS = ("pe", "dve", "act", "pool", "sp")

    def __init__(self, nc, es):
        self.nc = nc
        self.es = es
        self.q = {e: [] for e in self.ENGS}
        self.esem = {e: es.enter_context(nc.semaphore("s_" + e)) for e in ("pe", "dve", "act", "pool")}
        self.nsem = 0

    def _deps(self, eng, reads, writes, is_dma):
        deps = []
        for v in reads:
            w = v.b.w
            if w is not None:
                deps.append((w, "raw"))
            if v.b.psum:
                for r in v.b.r:
                    if r.eng != eng:
                        deps.append((r, "rr"))
        for v in writes:
            b = v.b
            if b.w is not None:
                deps.append((b.w, "waw"))
            for r in b.r:
                deps.append((r, "war"))
        out = []
        for (d, kind) in deps:
            if d.dma is None and not is_dma and d.eng == eng:
                if eng == "pe":
                    continue
            if d.dma is not None:
                out.append((d, d.dma.count))
            else:
                out.append((d, None))
        return out

    def _commit(self, rec, reads, writes):
        for v in writes:
            v.b.w = rec
            v.b.r = []
        for v in reads:
            if v.b.w is not rec:
                v.b.r.append(rec)

    def op(self, eng, fn, reads=(), writes=(), sig=True):
        rec = Rec(eng, fn, self._deps(eng, reads, writes, False), sig)
        rec.idx = len(self.q[eng])
        self.q[eng].append(rec)
        self._commit(rec, reads, writes)
        return rec

    def dma(self, queue, out, in_, reads, writes, key, slow=False):
        if key.dsem is None:
            key.dsem = DSem(self.es.enter_context(self.nc.semaphore("d%d" % self.nsem)))
            self.nsem += 1
        ds = key.dsem
        rec = Rec(queue, lambda e: e.dma_start(out=out, in_=in_, allow_slow_non_contiguous=slow), self._deps(queue, reads, writes, True), True, dma=ds)
        ds.count += 16
        rec.dval = ds.count
        rec.idx = len(self.q[queue])
        self.q[queue].append(rec)
        self._commit(rec, reads, writes)
        return rec

    def coll(self, ins_ap, outs_ap, groups, reads, writes, key):
        if key.dsem is None:
            key.dsem = DSem(self.es.enter_context(self.nc.semaphore("c%d" % self.nsem)))
            self.nsem += 1
        ds = key.dsem
        rec = Rec("pool", lambda e: e.collective_compute("AllGather", ALU.bypass, replica_groups=groups,
                                                         ins=[ins_ap], outs=[outs_ap]),
                  self._deps("pool", reads, writes, True), True, dma=ds)
        rec.dinc = 1
        ds.count += 1
        rec.dval = ds.count
        rec.idx = len(self.q["pool"])
        self.q["pool"].append(rec)
        self._commit(rec, reads, writes)
        return rec

    def emit(self, final_waits=()):
        nc = self.nc
        cnt = {}
        for e in ("pe", "dve", "act", "pool"):
            arr = []
            c = 0
            for r in self.q[e]:
                if r.dma is None and r.sig:
                    c += 1
                arr.append(c)
            res = [0] * len(arr)
            nxt = None
            for i in range(len(arr) - 1, -1, -1):
                r = self.q[e][i]
                if r.dma is None and r.sig:
                    nxt = arr[i]
                res[i] = nxt
            cnt[e] = res
        handles = {"pe": nc.tensor, "dve": nc.vector, "act": nc.scalar, "pool": nc.gpsimd, "sp": nc.sync}
        with nc.Block() as block:
            def run(e, h):
                waited = {}
                for r in self.q[e]:
                    for (d, dv) in r.deps:
                        if d.dma is not None:
                            sem, val = d.dma.sem, dv
                        else:
                            val = cnt[d.eng][d.idx]
                            assert val is not None, "dep on op with no later signal"
                            sem = self.esem[d.eng]
                        k = id(sem)
                        if waited.get(k, 0) >= val:
                            continue
                        waited[k] = val
                        h.wait_ge(sem, val)
                    ins = r.fn(h)
                    if r.dma is not None:
                        ins.then_inc(r.dma.sem, r.dinc)
                    elif r.sig:
                        ins.then_inc(self.esem[e], 1)
                if e == "sp":
                    for b in final_waits:
                        h.wait_ge(b.dsem.sem, b.dsem.count)

            @block.tensor
            def _(h):
                run("pe", h)

            @block.vector
            def _(h):
                run("dve", h)

            @block.scalar
            def _(h):
                run("act", h)

            @block.gpsimd
            def _(h):
                run("pool", h)

            @block.sync
            def _(h):
                run("sp", h)


def make_cfg(D, DFF, CW, NH, T, split=0):
    f = 1 + split
    CWl, NHl = CW // f, NH // f
    return dict(D=D, DFF=DFF, CW=CWl, NH=NHl, T=T, KD=D // 128, KF=DFF // 128, KC=CWl // 128, split=split,
                CWg=CW, NHg=NH, TT=min(512, T), INC=2 * CWl + 4 * NHl * 128 + 2 * NHl, ncores=8, B=4)


FULL = make_cfg(2048, 5632, 1024, 8, 4096, split=1)


def build(cfg, stop=None):
    D, DFF, CW, NH, T = cfg["D"], cfg["DFF"], cfg["CW"], cfg["NH"], cfg["T"]
    KD, KF, KC, TT, INC = cfg["KD"], cfg["KF"], cfg["KC"], cfg["TT"], cfg["INC"]
    NT = T // TT
    NB = TT // 128
    SPLIT = cfg.get("split", 0)
    KCg = cfg["CWg"] // 128
    NHg = cfg["NHg"]
    KY = KCg + NHg
    DNW = NH * 128
    nc = bass.Bass("TRN2", target_bir_lowering=False)
    es = ExitStack()
    S = Sched(nc, es)

    def din(name, shape, dt=F32):
        return nc.dram_tensor(name, list(shape), dt, kind="ExternalInput").ap()

    x_d = din("x", [T, D])
    y_d = nc.dram_tensor("y", [T, D], F32, kind="ExternalOutput").ap()
    MSPLIT = cfg.get("split", 0)
    if MSPLIT:
        MG_ = min(4, cfg["ncores"])
        NML_ = 9 * KD // MG_
        cT_d = din("cT", [128, KD, MG_ // 2])
        wada_d = din("w_ada", [D, NML_ * 128])
        badaT_d = din("b_adaT", [128, NML_])
        oh_d = din("onehot", [128, MG_ // 2])
    else:
        cT_d = din("cT", [128, KD])
        wada_d = din("w_ada", [D, 9 * D])
        badaT_d = din("b_adaT", [128, 9 * KD])
    normT_d = din("normT", [128, 4, KD])
    FSPLIT = cfg.get("split", 0)
    DFFl = DFF // (1 + FSPLIT)
    wg_d = [din("ffn1_wg", [D, DFFl]), din("ffn2_wg", [D, DFFl])]
    wu_d = [din("ffn1_wu", [D, DFFl]), din("ffn2_wu", [D, DFFl])]
    wd_d = [din("ffn1_wd", [DFFl, D]), din("ffn2_wd", [DFFl, D])]
    win_d = din("w_in", [D, INC])
    wout_d = din("w_out", [D, D])
    wdwT_d = din("w_dwT", [128, KC, 31])
    cvec_d = din("cvecT", [128, 3, KC])
    lnT_d = din("lnT", [128, 2, KCg])
    wshT_d = din("w_shT", [128, 3 * NH, 4])
    hvec_d = din("hvec", [128, 2, NH])
    onw_d = din("onwT", [128, 1])
    cst_d = din("consts", [128, 12, 128])
    rmask_d = din("rmask", [128, TT])

    SBTOT = [0]

    def sb(name, shape, dt=F32):
        t = es.enter_context(nc.sbuf_tensor("sb_" + name, list(shape), dt))
        nbytes = int(np.prod(shape[1:])) * (2 if dt == BF16 else 4)
        SBTOT[0] += nbytes
        if os.environ.get("KSB"):
            print("SB", name, nbytes, SBTOT[0])
        return t, Buf(name)

    cst, cst_b = sb("cst", [128, 12, 128])
    rmask, rmask_b = sb("rmask", [128, TT])
    ones_bf, ones_bf_b = sb("ones_bf", [128, 128], BF16)
    normT, normT_b = sb("normT", [128, 4, KD])
    wdwT, wdwT_b = sb("wdwT", [128, KC, 31])
    cvec, cvec_b = sb("cvec", [128, 3, KC])
    lnT, lnT_b = sb("lnT", [128, 2, KCg])
    wshT, wshT_b = sb("wshT", [128, 3 * NH, 4])
    hvec, hvec_b = sb("hvec", [128, 2, NH])
    negA, negA_b = sb("negA", [128, NH])
    onw, onw_b = sb("onw", [128, 1])
    modsT, modsT_b = sb("modsT", [128, 9 * KD])
    Acoef, Acoef_b = sb("Acoef", [128, 3, KD])
    Gcoef, Gcoef_b = sb("Gcoef", [128, 3, KD])
    finw_b = normT_b
    xT, xT_b = sb("xT", [128, KD, TT])
    hT, hT_b = sb("hT", [128, KD, TT], BF16)
    KFH = KF // 2
    actT = [sb("actT%d" % i, [128, max(KFH, KD), TT], BF16) for i in range(1)]
    yT, yT_b = sb("yT", [128, KY, TT], BF16)
    sq, sq_b = actT[0]
    rstd, rstd_b = sb("rstd", [128, TT])
    xin_all, xin_all_b = sb("xin_all", [128, 2, D])
    xin = [(xin_all[:, i, :], xin_all_b) for i in range(2)]
    Sst = [sb("Sst%d" % h, [128, 128]) for h in range(NH)]
    gtail = [sb("gtail%d" % j, [128, 30]) for j in range(KC)]
    ptail = [sb("ptail%d" % j, [128, 3]) for j in range(3 * NH)]
    NWA, NWD = 6, 2
    wa = [sb("wa%d" % i, [128, KD, 128], BF16) for i in range(NWA)]
    wdp = [sb("wd%d" % i, [128, KF // 2, 128], BF16) for i in range(NWD)]
    wa_i = [0]
    wd_i = [0]
    psum = []
    for i in range(8):
        t = es.enter_context(nc.psum_tensor("ps%d" % i, [128, 512], F32))
        psum.append((t, Buf("ps%d" % i, psum=True)))
    ps_i = [0]

    def PS():
        t, b = psum[ps_i[0] % 8]
        ps_i[0] += 1
        return t, b

    IDN = V(cst_b, cst[:, 0, :])
    ONES = V(cst_b, cst[:, 1, :])
    NEGS = V(cst_b, cst[:, 2, :])
    NEGST = V(cst_b, cst[:, 3, :])
    NEGTT = V(cst_b, cst[:, 4, :])
    BD16 = V(cst_b, cst[:, 5, :])
    LLm = [V(cst_b, cst[:, 6 + i, :]) for i in range(3)]
    URm = [V(cst_b, cst[:, 9 + i, :]) for i in range(3)]

    def mm(out, lhsT, rhs, start=True, stop=True, sig=None):
        if sig is None:
            sig = stop
        return S.op("pe", lambda e: e.matmul(out.ap, lhsT.ap, rhs.ap, start=start, stop=stop),
                    reads=[lhsT, rhs], writes=[out], sig=sig)

    def tr(out, in_):
        return S.op("pe", lambda e: e.transpose(out.ap, in_.ap, IDN.ap), reads=[in_, IDN], writes=[out])

    def act(out, in_, func, bias=None, scale=None, accum=None, eng="act"):
        rd = [in_]
        kw = {}
        if bias is not None:
            if isinstance(bias, V):
                rd.append(bias)
                kw["bias"] = bias.ap
            else:
                kw["bias"] = float(bias)
        if scale is not None:
            if isinstance(scale, V):
                rd.append(scale)
                kw["scale"] = scale.ap
            else:
                kw["scale"] = float(scale)
        wr = [out]
        if accum is not None:
            kw["accum_out"] = accum.ap
            wr.append(accum)
        return S.op("act", lambda e: e.activation(out.ap, in_.ap, func, **kw), reads=rd, writes=wr)

    def tt(out, a, b, op, eng="dve"):
        return S.op(eng, lambda e: e.tensor_tensor(out.ap, a.ap, b.ap, op), reads=[a, b], writes=[out])

    def ts(out, a, s1, s2, op0, op1=None, eng="dve"):
        rd = [a]
        s1a = s1.ap if isinstance(s1, V) else s1
        s2a = s2.ap if isinstance(s2, V) else s2
        if isinstance(s1, V):
            rd.append(s1)
        if isinstance(s2, V):
            rd.append(s2)
        if op1 is None:
            return S.op(eng, lambda e: e.tensor_scalar(out.ap, a.ap, s1a, None, op0), reads=rd, writes=[out])
        return S.op(eng, lambda e: e.tensor_scalar(out.ap, a.ap, s1a, s2a, op0, op1), reads=rd, writes=[out])

    def stt(out, a, s, b, op0, op1):
        rd = [a, b]
        sa = s.ap if isinstance(s, V) else s
        if isinstance(s, V):
            rd.append(s)
        return S.op("dve", lambda e: e.scalar_tensor_tensor(out.ap, a.ap, sa, b.ap, op0, op1), reads=rd, writes=[out])

    def cp(out, in_, eng="dve"):
        if eng == "act":
            return act(out, in_, AF.Copy)
        return S.op(eng, lambda e: e.tensor_copy(out.ap, in_.ap), reads=[in_], writes=[out])

    def ld(out, src_ap, queue="sp"):
        return S.dma(queue, out.ap, src_ap, reads=[], writes=[out], key=out.b)

    ld(V(cst_b, cst[:]), cst_d)
    ld(V(rmask_b, rmask[:]), rmask_d)
    ld(V(normT_b, normT[:]), normT_d)
    ld(V(wdwT_b, wdwT[:]), wdwT_d)
    ld(V(cvec_b, cvec[:]), cvec_d)
    ld(V(lnT_b, lnT[:]), lnT_d)
    ld(V(wshT_b, wshT[:]), wshT_d)
    ld(V(hvec_b, hvec[:]), hvec_d)
    ld(V(onw_b, onw[:]), onw_d)
    cp(V(ones_bf_b, ones_bf[:]), ONES)
    act(V(negA_b, negA[:]), V(hvec_b, hvec[:, 0, :]), AF.Exp)
    ts(V(negA_b, negA[:]), V(negA_b, negA[:]), -1.0, None, ALU.mult)
    for h in range(NH):
        S.op("dve", lambda e, h=h: e.memset(Sst[h][0][:], 0.0), writes=[V(Sst[h][1], Sst[h][0][:])])
    for j in range(KC):
        S.op("dve", lambda e, j=j: e.memset(gtail[j][0][:], 0.0), writes=[V(gtail[j][1], gtail[j][0][:])])
    for j in range(3 * NH):
        S.op("dve", lambda e, j=j: e.memset(ptail[j][0][:], 0.0), writes=[V(ptail[j][1], ptail[j][0][:])])

    NMC = 9 * KD
    wada_v = wada_d.rearrange("(k p) n -> p k n", p=128)
    if not MSPLIT:
        scT, scT_b = sb("scT", [128, KD])
        badaT, badaT_b = sb("badaT", [128, 9 * KD])
        ld(V(scT_b, scT[:]), cT_d)
        ld(V(badaT_b, badaT[:]), badaT_d)
        act(V(scT_b, scT[:]), V(scT_b, scT[:]), AF.Silu)
        pm, pm_b = PS()
        for j in range(NMC):
            wt0, wb = xin[j % 2]
            wt = wt0[:].rearrange("p (k n) -> p k n", k=KD)
            S.dma("sp", wt, wada_v[:, :, j * 128:(j + 1) * 128], reads=[], writes=[V(wb, wt)], key=wb)
            for kc in range(KD):
                mm(V(pm_b, pm[:, j:j + 1]), V(wb, wt[:, kc, :]), V(scT_b, scT[:, kc:kc + 1]),
                   start=(kc == 0), stop=(kc == KD - 1))
        tt(V(modsT_b, modsT[:]), V(pm_b, pm[:, 0:NMC]), V(badaT_b, badaT[:]), ALU.add)
    else:
        NCR = min(4, cfg["ncores"])
        NB_ = NCR // 2
        NML = NMC // NCR
        assert NML * NCR == NMC
        scT, scT_b = sb("scT", [128, KD, NB_])
        badaT, badaT_b = sb("badaT", [128, NML])
        oh, oh_b = sb("oh", [128, NB_])
        mp, mp_b = sb("mp", [128, NML, NB_])
        gath, gath_b = sb("gath", [128, NCR * NML, NB_])
        ld(V(scT_b, scT[:]), cT_d)
        ld(V(badaT_b, badaT[:]), badaT_d)
        ld(V(oh_b, oh[:]), oh_d)
        act(V(scT_b, scT[:]), V(scT_b, scT[:]), AF.Silu)
        pm, pm_b = PS()
        for j in range(NML):
            wt0, wb = xin[j % 2]
            wt = wt0[:].rearrange("p (k n) -> p k n", k=KD)
            S.dma("sp", wt, wada_v[:, :, j * 128:(j + 1) * 128], reads=[], writes=[V(wb, wt)], key=wb)
            for kc in range(KD):
                mm(V(pm_b, pm[:, j * NB_:(j + 1) * NB_]), V(wb, wt[:, kc, :]), V(scT_b, scT[:, kc, :]),
                   start=(kc == 0), stop=(kc == KD - 1))
        for j in range(NML):
            ts(V(mp_b, mp[:, j, :]), V(pm_b, pm[:, j * NB_:(j + 1) * NB_]), V(badaT_b, badaT[:, j:j + 1]), None, ALU.add)
        mcin = (nc.dram_tensor("mx_cin", [128, NML * NB_], F32, kind="Internal").ap(), Buf("mx_cin"))
        mcout = (nc.dram_tensor("mx_cout", [NCR * 128, NML * NB_], F32, kind="Internal").ap(), Buf("mx_cout"))
        S.dma("sp", mcin[0], mp[:].rearrange("p j b -> p (j b)"), reads=[V(mp_b, mp[:])], writes=[V(mcin[1], mcin[0])], key=mp_b)
        S.coll(mcin[0], mcout[0], [list(range(g_ * NCR, (g_ + 1) * NCR)) for g_ in range(cfg["ncores"] // NCR)], reads=[V(mcin[1], mcin[0])], writes=[V(mcout[1], mcout[0])], key=mcout[1])
        S.dma("sp", gath[:].rearrange("p (r j) b -> p r (j b)", r=NCR), mcout[0].rearrange("(r p) c -> p r c", p=128),
              reads=[V(mcout[1], mcout[0])], writes=[V(gath_b, gath[:])], key=gath_b)
        ts(V(modsT_b, modsT[:]), V(gath_b, gath[:, :, 0]), V(oh_b, oh[:, 0:1]), None, ALU.mult)
        for bb in range(1, NB_):
            stt(V(modsT_b, modsT[:]), V(gath_b, gath[:, :, bb]), V(oh_b, oh[:, bb:bb + 1]), V(modsT_b, modsT[:]), ALU.mult, ALU.add)
    for i in range(3):
        sc = V(modsT_b, modsT[:, (3 * i + 1) * KD:(3 * i + 2) * KD])
        gt = V(modsT_b, modsT[:, (3 * i + 2) * KD:(3 * i + 3) * KD])
        stt(V(Acoef_b, Acoef[:, i, :]), sc, 1.0, V(normT_b, normT[:, i, :]), ALU.add, ALU.mult)
        ts(V(Gcoef_b, Gcoef[:, i, :]), gt, 0.5 if i != 1 else 1.0, None, ALU.mult)

    def shift_col(i, kc):
        return V(modsT_b, modsT[:, 3 * i * KD + kc:3 * i * KD + kc + 1])

    def rms_rstd():
        for kc in range(KD):
            act(V(sq_b, sq[:, kc, :]), V(xT_b, xT[:, kc, :]), AF.Square)
        p, pb = PS()
        for kc in range(KD):
            mm(V(pb, p[:, 0:TT]), V(ones_bf_b, ones_bf[:]), V(sq_b, sq[:, kc, :]), start=(kc == 0), stop=(kc == KD - 1))
        act(V(rstd_b, rstd[:]), V(pb, p[:, 0:TT]), AF.Sqrt, bias=EPS, scale=1.0 / D)
        S.op("dve", lambda e: e.reciprocal(rstd[:], rstd[:]), reads=[V(rstd_b, rstd[:])], writes=[V(rstd_b, rstd[:])])

    tmpn, tmpn_b = sb("tmpn", [128, TT])
    cacc, cacc_b = tmpn, tmpn_b

    def rms_mod(i):
        rms_rstd()
        for kc in range(KD):
            tt(V(tmpn_b, tmpn[:]), V(xT_b, xT[:, kc, :]), V(rstd_b, rstd[:]), ALU.mult)
            act(V(hT_b, hT[:, kc, :]), V(tmpn_b, tmpn[:]), AF.Identity,
                bias=shift_col(i, kc), scale=V(Acoef_b, Acoef[:, i, kc:kc + 1]))

    scr = {}
    cur_tile = [0]

    def wload(dst_pool, idx, src, K, tag):
        t, b = dst_pool[idx[0] % len(dst_pool)]
        idx[0] += 1
        if tag not in scr:
            dt_ = nc.dram_tensor("scr_" + tag, [128, K * 128], BF16, kind="Internal").ap()
            scr[tag] = (dt_, Buf("scr_" + tag))
        sap, sbuf_ = scr[tag]
        sview = sap.rearrange("p (k n) -> p k n", k=K)
        if cur_tile[0] == 0:
            S.dma("pool", t[:, 0:K, :], src, reads=[], writes=[V(b, t[:])], key=b)
            if NT > 1:
                S.dma("sp", sview, t[:, 0:K, :], reads=[V(b, t[:])], writes=[V(sbuf_, sap)], key=b)
        else:
            S.dma("sp", t[:, 0:K, :], sview, reads=[V(sbuf_, sap)], writes=[V(b, t[:])], key=b)
        return t, b

    sg, sg_b = sb("sg", [128, TT])

    def ffn(f, i):
        aT, aT_b = actT[0]
        wgv = wg_d[f].rearrange("(k p) n -> p k n", p=128)
        wuv = wu_d[f].rearrange("(k p) n -> p k n", p=128)
        wdv = wd_d[f].rearrange("(k p) n -> p k n", p=128)
        for hf in range(1 if FSPLIT else 2):
            for jj in range(KFH):
                j = hf * KFH + jj
                gt_, gb_ = wload(wa, wa_i, wgv[:, :, j * 128:(j + 1) * 128], KD, "g%d_%d" % (f, j))
                ut_, ub_ = wload(wa, wa_i, wuv[:, :, j * 128:(j + 1) * 128], KD, "u%d_%d" % (f, j))
                pg, pgb = PS()
                pu, pub = PS()
                for kc in range(KD):
                    mm(V(pgb, pg[:, 0:TT]), V(gb_, gt_[:, kc, :]), V(hT_b, hT[:, kc, :]), start=(kc == 0), stop=(kc == KD - 1))
                for kc in range(KD):
                    mm(V(pub, pu[:, 0:TT]), V(ub_, ut_[:, kc, :]), V(hT_b, hT[:, kc, :]), start=(kc == 0), stop=(kc == KD - 1))
                act(V(sg_b, sg[:]), V(pgb, pg[:, 0:TT]), AF.Silu)
                tt(V(aT_b, aT[:, jj, :]), V(pub, pu[:, 0:TT]), V(sg_b, sg[:]), ALU.mult)
            for m in range(KD):
                dt_, db_ = wload(wdp, wd_i, wdv[:, hf * KFH:(hf + 1) * KFH, m * 128:(m + 1) * 128], KFH, "d%d_%d_%d" % (f, hf, m))
                pd, pdb = PS()
                for kf in range(KFH):
                    mm(V(pdb, pd[:, 0:TT]), V(db_, dt_[:, kf, :]), V(aT_b, aT[:, kf, :]), start=(kf == 0), stop=(kf == KFH - 1))
                if not FSPLIT:
                    stt(V(xT_b, xT[:, m, :]), V(pdb, pd[:, 0:TT]), V(Gcoef_b, Gcoef[:, i, m:m + 1]), V(xT_b, xT[:, m, :]),
                        ALU.mult, ALU.add)
                    continue
                KH = KD // 2
                part, mo = m // KH, m % KH
                dsn = ("lnb", "beta")[m % 2]
                dsv = V(RB[dsn][1], RB[dsn][0][:])
                ts(dsv, V(pdb, pd[:, 0:TT]), V(Gcoef_b, Gcoef[:, i, m:m + 1]), None, ALU.mult)
                cinb = fx["cin%d" % part]
                S.dma("sp", cinb[0][:, mo * TT:(mo + 1) * TT], dsv.ap, reads=[dsv], writes=[V(cinb[1], cinb[0])], key=dsv.b)
                if mo == KH - 1:
                    coutb = fx["cout%d" % part]
                    S.coll(cinb[0], coutb[0], PAIRS, reads=[V(cinb[1], cinb[0])], writes=[V(coutb[1], coutb[0])], key=coutb[1])
            if FSPLIT:
                KH = KD // 2
                for m in range(KD):
                    part, mo = m // KH, m % KH
                    coutb = fx["cout%d" % part]
                    names = (("lsk", "Kd"), ("R", "P"))[m % 2]
                    for r_ in range(2):
                        tv_ = V(RB[names[r_]][1], RB[names[r_]][0][:])
                        S.dma("sp", tv_.ap, coutb[0][r_ * 128:(r_ + 1) * 128, mo * TT:(mo + 1) * TT],
                              reads=[V(coutb[1], coutb[0])], writes=[tv_], key=tv_.b)
                        tt(V(xT_b, xT[:, m, :]), V(xT_b, xT[:, m, :]), tv_, ALU.add)

    glu, glu_b = sb("glu", [128, 30 + TT])
    assert KCg * TT <= 2 * D
    ypre = xin_all[:].rearrange("p a d -> p (a d)")[:, 0:KCg * TT].rearrange("p (k t) -> p k t", k=KCg)
    ypre_b = xin_all_b
    pre, pre_b = glu, glu_b
    qkv = [sb("qkv%d" % i, [128, TT]) for i in range(3)]
    zs, zs_b = sb("zs", [128, TT])
    abT, abT_b = sb("abT", [64, TT])
    wab, wab_b = sb("wab", [128, KD, 32], BF16)
    RB = {n: sb("RB_" + n, [128, TT]) for n in ("lnb", "beta", "G", "lsk", "lsq", "R", "P", "Q", "Kd")}
    RB["t"] = RB["lsq"]
    RB["E"] = RB["lsq"]
    RB["g"] = RB["Kd"]
    RB["nEP"] = RB["lnb"]
    RB["nR"] = RB["lsk"]
    stk = [sb("stk%d" % i, [128, TT]) for i in range(2)]
    cols = [sb("cols%d" % i, [128, 128]) for i in range(2)]
    ktok, ktok_b = sb("ktok", [128, 128])
    vtok, vtok_b = sb("vtok", [128, 128])
    kd_t, kd_b = sb("kd_t", [128, 128])
    mA, mA_b = sb("mA", [128, 128])
    mB, mB_b = sb("mB", [128, 128])
    NAt, NA_b = sb("NAt", [128, 256])
    NBt, NB_b = sb("NBt", [128, 256])
    tA, tA_b = sb("tA", [128, 128])
    tB, tB_b = sb("tB", [128, 128])
    tY, tY_b = sb("tY", [128, 256])
    mean_t, mean_b = RB["R"]
    rs2, rs2_b = RB["P"]
    aTt, aTt_b = sb("aTt", [128, 128])
    qeT, qeT_b = sb("qeT", [128, 128])
    TbT, TbT_b = sb("TbT", [128, 128])
    TwT, TwT_b = sb("TwT", [128, 128])
    u_t, u_b = sb("u_t", [128, 128])
    nwT, nwT_b = sb("nwT", [128, 128])
    vnew, vnew_b = sb("vnew", [128, 128])
    on_t, on_b = TbT, TbT_b
    sml, sml_b = sb("sml", [128, 8])
    for i in range(2):
        S.op("dve", lambda e, i=i: e.memset(stk[i][0][:], 0.0), writes=[V(stk[i][1], stk[i][0][:])])
    S.op("dve", lambda e: e.memset(abT[:], 0.0), writes=[V(abT_b, abT[:])])
    selm, selm_b = sb("selm", [64, 2, 128])

    def build_sel(h):
        for r in range(2):
            rr = h if r == 0 else 32 + h
            ts(V(selm_b, selm[:, r, :]), V(cst_b, cst[0:64, 1, :]), V(cst_b, cst[0:64, 0, rr:rr + 1]), None, ALU.mult)

    gx = {}
    PAIRS = [[2 * i, 2 * i + 1] for i in range(cfg["ncores"] // 2)]
    if SPLIT:
        for nm_, shp_, dt_ in (("cin1", [128, KC * TT], F32), ("cout1", [256, KC * TT], F32),
                               ("cin2", [128, NH * TT], BF16), ("cout2", [256, NH * TT], BF16)):
            gx[nm_] = (nc.dram_tensor("gx_" + nm_, shp_, dt_, kind="Internal").ap(), Buf("gx_" + nm_))
    fx = {}
    if FSPLIT:
        for pi_ in range(2):
            fx["cin%d" % pi_] = (nc.dram_tensor("fx_cin%d" % pi_, [128, (KD // 2) * TT], F32, kind="Internal").ap(), Buf("fx_cin%d" % pi_))
            fx["cout%d" % pi_] = (nc.dram_tensor("fx_cout%d" % pi_, [256, (KD // 2) * TT], F32, kind="Internal").ap(), Buf("fx_cout%d" % pi_))
    win_v = win_d.rearrange("(k p) n -> p k n", p=128)
    wout_v = wout_d.rearrange("(k p) n -> p k n", p=128)
    QB = 2 * CW

    def proj(col0):
        wt, wb = wload(wa, wa_i, win_v[:, :, col0:col0 + 128], KD, "in%d" % col0)
        p, pb = PS()
        for kc in range(KD):
            mm(V(pb, p[:, 0:TT]), V(wb, wt[:, kc, :]), V(hT_b, hT[:, kc, :]), start=(kc == 0), stop=(kc == KD - 1))
        return V(pb, p[:, 0:TT])

    def rb(n, sl=None):
        t, b = RB[n]
        return V(b, t[:] if sl is None else t[:, sl])

    class DS:
        pass

    def mk_set0():
        d = DS()
        d.qkv = qkv
        d.zs = (zs, zs_b)
        d.RB = RB
        d.stk = stk
        d.cols = cols
        d.t = dict(ktok=(ktok, ktok_b), vtok=(vtok, vtok_b), kd=(kd_t, kd_b), mA=(mA, mA_b), mB=(mB, mB_b),
                   NA=(NAt, NA_b), NB=(NBt, NB_b), tA=(tA, tA_b), tB=(tB, tB_b), tY=(tY, tY_b), aT=(aTt, aTt_b),
                   qeT=(qeT, qeT_b), TbT=(TbT, TbT_b), TwT=(TwT, TwT_b), u=(u_t, u_b), nwT=(nwT, nwT_b),
                   vnew=(vnew, vnew_b), sml=(sml, sml_b))
        return d

    st1_bufs = []
    aT_alias_b = actT[0][1]

    def mk_set1():
        need_a = 11 * TT
        arenaA = actT[0][0][:].rearrange("p k t -> p (k t)").bitcast(F32)
        arenas = [arenaA] + [w[0][:].rearrange("p k n -> p (k n)").bitcast(F32) for w in wdp]
        sizes = [max(KFH, KD) * TT // 2] + [KFH * 64] * len(wdp)
        use_alias = (sizes[0] >= need_a) and (sizes[1] >= 1408) and len(wdp) >= 2 and TT == 512
        pos = [0] * len(arenas)

        def al(name, n, ai):
            if not use_alias:
                t_, b_ = sb("s1_" + name, [128, n])
                return t_[:], b_
            assert pos[ai] + n <= sizes[ai], (name, ai, pos[ai], n, sizes[ai])
            v_ = arenas[ai][:, pos[ai]:pos[ai] + n]
            pos[ai] += n
            b_ = Buf("s1_" + name)
            st1_bufs.append(b_)
            return v_, b_
        d = DS()
        d.qkv = [al("qkv%d" % i, TT, 0) for i in range(3)]
        d.zs = al("zs", TT, 0)
        d.RB = dict(RB)
        for n_ in ("R", "P", "Q", "lsq", "G"):
            d.RB[n_] = al("RB_" + n_, TT, 0)
        d.RB["t"] = d.RB["lsq"]
        d.RB["E"] = d.RB["lsq"]
        d.stk = [al("stk%d" % i, TT, 0) for i in range(2)]
        d.cols = [al("cols%d" % i, 128, 1) for i in range(2)]
        d.t = {}
        for n_ in ("ktok", "vtok", "kd", "mA", "mB"):
            d.t[n_] = al(n_, 128, 1)
        d.t["NA"] = al("NA", 256, 1)
        d.t["NB"] = al("NB", 256, 1)
        d.t["tA"] = al("tA", 128, 2)
        d.t["tB"] = al("tB", 128, 2)
        d.t["tY"] = al("tY", 256, 2)
        for n_ in ("aT", "qeT", "TbT", "TwT", "u", "nwT", "vnew"):
            d.t[n_] = al(n_, 128, 2)
        sm_, smb_ = sb("s1_sml", [128, 8])
        d.t["sml"] = (sm_[:], smb_)
        d.alias = use_alias
        return d

    def inherit(dst, src):
        acc = []
        for b_ in src:
            if b_.w is not None:
                acc.append(b_.w)
            acc.extend(b_.r)
        for b_ in dst:
            b_.r = list(b_.r) + acc

    dsets = [mk_set0(), mk_set1()]
    for i in range(2):
        S.op("dve", lambda e, i=i: e.memset(dsets[1].stk[i][0], 0.0), writes=[V(dsets[1].stk[i][1], dsets[1].stk[i][0])])

    def dn_prep(h, st):
        qkv_, RB_ = st.qkv, st.RB
        zs_, zs_b_ = st.zs

        def rbs(n, sl=None):
            t_, b_ = RB_[n]
            return V(b_, t_[:] if sl is None else t_[:, sl])
        for i3 in range(3):
            pp = proj(QB + i3 * DNW + h * 128)
            ci = i3 * NH + h
            cp(V(pre_b, pre[:, 0:3]), V(ptail[ci][1], ptail[ci][0][:]))
            cp(V(pre_b, pre[:, 3:3 + TT]), pp, eng="act")
            cp(V(ptail[ci][1], ptail[ci][0][:]), V(pre_b, pre[:, TT:TT + 3]))
            for k in range(4):
                wk = V(wshT_b, wshT[:, ci, k:k + 1])
                if k == 0:
                    ts(V(cacc_b, cacc[:]), V(pre_b, pre[:, 0:TT]), wk, None, ALU.mult)
                else:
                    stt(V(cacc_b, cacc[:]), V(pre_b, pre[:, k:k + TT]), wk, V(cacc_b, cacc[:]), ALU.mult, ALU.add)
            act(V(qkv_[i3][1], qkv_[i3][0][:]), V(cacc_b, cacc[:]), AF.Silu)
        pz = proj(QB + 3 * DNW + h * 128)
        act(V(zs_b_, zs_[:]), pz, AF.Silu)
        qT, kT, vT = [V(qkv_[i][1], qkv_[i][0][:]) for i in range(3)]
        pb1, pb1_b = PS()
        build_sel(h)
        mm(V(pb1_b, pb1[:, 0:TT]), V(selm_b, selm[:, 0, :]), V(abT_b, abT[:]))
        pa1, pa1_b = PS()
        mm(V(pa1_b, pa1[:, 0:TT]), V(selm_b, selm[:, 1, :]), V(abT_b, abT[:]))
        act(rbs("t"), V(pb1_b, pb1[:, 0:TT]), AF.Exp, scale=-1.0)
        act(rbs("lnb"), rbs("t"), AF.Ln, bias=1.0)
        ts(rbs("lnb"), rbs("lnb"), -1.0, None, ALU.mult)
        act(rbs("beta"), rbs("lnb"), AF.Exp)
        act(rbs("t"), V(pa1_b, pa1[:, 0:TT]), AF.Exp, bias=V(hvec_b, hvec[:, 1, h:h + 1]))
        act(rbs("g"), rbs("t"), AF.Ln, bias=1.0)
        ts(rbs("g"), rbs("g"), V(negA_b, negA[:, h:h + 1]), None, ALU.mult)
        S.op("dve", lambda e: e.tensor_tensor_scan(RB_["G"][0][:], rmask[:], RB_["g"][0][:], 0.0, ALU.mult, ALU.add),
             reads=[V(rmask_b, rmask[:]), rbs("g")], writes=[rbs("G")])
        for (src, dst) in ((kT, "lsk"), (qT, "lsq")):
            act(V(cacc_b, cacc[:]), src, AF.Square)
            pss, pss_b = PS()
            mm(V(pss_b, pss[:, 0:TT]), ONES, V(cacc_b, cacc[:]))
            act(rbs(dst), V(pss_b, pss[:, 0:TT]), AF.Ln, bias=EPS)
        stt(rbs("R"), rbs("lsk"), 0.5, rbs("G"), ALU.mult, ALU.add)
        stt(rbs("P"), rbs("lsk"), -0.5, rbs("G"), ALU.mult, ALU.add)
        tt(rbs("P"), rbs("P"), rbs("lnb"), ALU.add)
        stt(rbs("Q"), rbs("lsq"), -0.5, rbs("G"), ALU.mult, ALU.add)
        ts(rbs("Q"), rbs("Q"), float(np.log(128.0 ** -0.5)), None, ALU.add)
        act(rbs("E"), rbs("Q"), AF.Exp)
        act(rbs("nEP"), rbs("P"), AF.Exp)
        ts(rbs("nEP"), rbs("nEP"), -1.0, None, ALU.mult)
        ts(rbs("nR"), rbs("R"), -1.0, None, ALU.mult)
        for blk in range(NB):
            bs = slice(blk * 128, (blk + 1) * 128)
            last = blk * 128 + 127
            glast = V(RB_["G"][1], RB_["G"][0][:, last:last + 1])
            act(rbs("Kd", bs), rbs("nR", bs), AF.Exp, bias=glast)
        for (pi, nme) in ((0, "nR"), (32, "P"), (64, "beta"), (96, "nEP")):
            cp(V(st.stk[0][1], st.stk[0][0][pi:pi + 1, :]), V(RB_[nme][1], RB_[nme][0][pi:pi + 1, :]))
        cp(V(st.stk[1][1], st.stk[1][0][0:1, :]), V(RB_["Kd"][1], RB_["Kd"][0][0:1, :]))

    def dn_blocks(h, st):
        qkv_, RB_ = st.qkv, st.RB
        zs_, zs_b_ = st.zs
        T_ = st.t

        def rbs(n, sl=None):
            t_, b_ = RB_[n]
            return V(b_, t_[:] if sl is None else t_[:, sl])

        def tv(n, sl=None):
            t_, b_ = T_[n]
            return V(b_, t_[:] if sl is None else t_[:, sl])
        Sb, Sbb = Sst[h]
        SV = V(Sbb, Sb[:])
        for blk in range(NB):
            bs = slice(blk * 128, (blk + 1) * 128)
            last = blk * 128 + 127
            for i in range(2):
                pc, pcb = PS()
                tr(V(pcb, pc[:, 0:128]), V(st.stk[i][1], st.stk[i][0][:, bs]))
                cp(V(st.cols[i][1], st.cols[i][0][:]), V(pcb, pc[:, 0:128]), eng="act")
            c0, c0b = st.cols[0]
            cnR = V(c0b, c0[:, 0:1])
            cP = V(c0b, c0[:, 32:33])
            cbeta = V(c0b, c0[:, 64:65])
            cnEP = V(c0b, c0[:, 96:97])
            cKd = V(st.cols[1][1], st.cols[1][0][:, 0:1])
            qTb = V(qkv_[0][1], qkv_[0][0][:, bs])
            kTb = V(qkv_[1][1], qkv_[1][0][:, bs])
            vTb = V(qkv_[2][1], qkv_[2][0][:, bs])
            ptk, ptk_b = PS()
            tr(V(ptk_b, ptk[:, 0:128]), kTb)
            tr(V(ptk_b, ptk[:, 128:256]), vTb)
            pkk, pkk_b = PS()
            mm(V(pkk_b, pkk[:, 0:128]), kTb, kTb)
            mm(V(pkk_b, pkk[:, 128:256]), kTb, qTb)
            yield
            cp(tv("ktok"), V(ptk_b, ptk[:, 0:128]), eng="act")
            cp(tv("vtok"), V(ptk_b, ptk[:, 128:256]), eng="act")
            ts(tv("kd"), V(ptk_b, ptk[:, 0:128]), cKd, None, ALU.mult)
            tt(tv("mA"), NEGS, rbs("R", bs), ALU.subtract)
            act(tv("mA"), tv("mA"), AF.Exp, bias=cP)
            tt(tv("mB"), NEGST, rbs("P", bs), ALU.add)
            act(tv("mB"), tv("mB"), AF.Exp, bias=cnR)
            tt(tv("aT"), NEGTT, rbs("Q", bs), ALU.add)
            act(tv("aT"), tv("aT"), AF.Exp, bias=cnR)
            yield
            stt(tv("mA"), V(pkk_b, pkk[:, 0:128]), -1.0, tv("mA"), ALU.mult, ALU.mult)
            stt(tv("mB"), V(pkk_b, pkk[:, 0:128]), -1.0, tv("mB"), ALU.mult, ALU.mult)
            tt(tv("aT"), V(pkk_b, pkk[:, 128:256]), tv("aT"), ALU.mult)
            tt(tv("qeT"), qTb, rbs("E", bs), ALU.mult)
            XTv = tv("NA", slice(128, 256))
            XUv = tv("NB", slice(128, 256))
            tt(tv("NA", slice(0, 128)), tv("mA"), BD16, ALU.mult)
            tt(tv("NB", slice(0, 128)), tv("mB"), BD16, ALU.mult)
            cp(XTv, IDN)
            cp(XUv, IDN)
            yield
            for lev in range(4):
                p1, p1b = PS()
                p2, p2b = PS()
                if lev == 3:
                    mm(V(p1b, p1[:, 128:256]), tv("NA", slice(0, 128)), XUv)
                    mm(V(p2b, p2[:, 128:256]), tv("NB", slice(0, 128)), XTv)
                    yield
                else:
                    mm(V(p1b, p1[:, 0:256]), tv("NA", slice(0, 128)), tv("NB", slice(0, 256)))
                    mm(V(p2b, p2[:, 0:256]), tv("NB", slice(0, 128)), tv("NA", slice(0, 256)))
                    yield
                    cp(tv("NB", slice(0, 128)), V(p1b, p1[:, 0:128]), eng="act")
                    cp(tv("NA", slice(0, 128)), V(p2b, p2[:, 0:128]), eng="act")
                tt(XUv, V(p1b, p1[:, 128:256]), XUv, ALU.add)
                tt(XTv, V(p2b, p2[:, 128:256]), XTv, ALU.add)
                yield
            for li in range(3):
                lastm = (li == 2)
                tt(tv("tA"), tv("mA"), LLm[li], ALU.mult)
                pY, pYb = PS()
                mm(V(pYb, pY[:, 0:128]), tv("tA"), XUv)
                if not lastm:
                    tt(tv("tB"), tv("mB"), URm[li], ALU.mult)
                    mm(V(pYb, pY[:, 128:256]), tv("tB"), XTv)
                    yield
                    cp(tv("tY", slice(0, 256)), V(pYb, pY[:, 0:256]), eng="act")
                else:
                    yield
                    cp(tv("tY", slice(0, 128)), V(pYb, pY[:, 0:128]), eng="act")
                pZ, pZb = PS()
                mm(V(pZb, pZ[:, 0:128]), XTv, tv("tY", slice(0, 128)))
                if not lastm:
                    mm(V(pZb, pZ[:, 128:256]), XUv, tv("tY", slice(128, 256)))
                yield
                tt(XUv, V(pZb, pZ[:, 0:128]), XUv, ALU.add)
                if not lastm:
                    tt(XTv, V(pZb, pZ[:, 128:256]), XTv, ALU.add)
            ts(tv("TbT"), XUv, cbeta, None, ALU.mult)
            ts(tv("TwT"), XUv, cnEP, None, ALU.mult)
            pu_, pu_b = PS()
            mm(V(pu_b, pu_[:, 0:128]), tv("TbT"), tv("vtok"))
            mm(V(pu_b, pu_[:, 128:256]), tv("ktok"), tv("TwT"))
            yield
            cp(tv("u"), V(pu_b, pu_[:, 0:128]), eng="act")
            cp(tv("nwT"), V(pu_b, pu_[:, 128:256]), eng="act")
            pv, pvb = PS()
            mm(V(pvb, pv[:, 0:128]), tv("nwT"), SV)
            yield
            tt(tv("vnew"), V(pvb, pv[:, 0:128]), tv("u"), ALU.add)
            po, pob = PS()
            mm(V(pob, po[:, 0:128]), tv("qeT"), SV, start=True, stop=False)
            mm(V(pob, po[:, 0:128]), tv("aT"), tv("vnew"), start=False, stop=True)
            pds, pdsb = PS()
            mm(V(pdsb, pds[:, 0:128]), tv("kd"), tv("vnew"))
            act(tv("sml", slice(0, 1)), V(RB_["G"][1], RB_["G"][0][:, last:last + 1]), AF.Exp)
            yield
            stt(SV, SV, tv("sml", slice(0, 1)), V(pdsb, pds[:, 0:128]), ALU.mult, ALU.add)
            act(tv("TbT"), V(pob, po[:, 0:128]), AF.Square, accum=tv("sml", slice(1, 2)))
            act(tv("sml", slice(2, 3)), tv("sml", slice(1, 2)), AF.Sqrt, bias=EPS, scale=1.0 / 128)
            smt, smb = T_["sml"]
            S.op("dve", lambda e, smt=smt: e.reciprocal(smt[:, 3:4], smt[:, 2:3]), reads=[tv("sml", slice(2, 3))], writes=[tv("sml", slice(3, 4))])
            ts(tv("TbT"), V(pob, po[:, 0:128]), tv("sml", slice(3, 4)), None, ALU.mult)
            pt2, pt2b = PS()
            tr(V(pt2b, pt2[:, 0:128]), tv("TbT"))
            yield
            stt(V(yT_b, yT[:, KCg + h, bs]), V(pt2b, pt2[:, 0:128]), V(onw_b, onw[:, 0:1]), V(zs_b_, zs_[:, bs]), ALU.mult, ALU.mult)

    def mixer():
        for j in range(KC):
            pa = proj(j * 128)
            pgt = proj(CW + j * 128)
            act(V(sg_b, sg[:]), pgt, AF.Sigmoid)
            cp(V(glu_b, glu[:, 0:30]), V(gtail[j][1], gtail[j][0][:]))
            tt(V(glu_b, glu[:, 30:30 + TT]), pa, V(sg_b, sg[:]), ALU.mult)
            cp(V(gtail[j][1], gtail[j][0][:]), V(glu_b, glu[:, TT:TT + 30]))
            for k in range(31):
                wk = V(wdwT_b, wdwT[:, j, k:k + 1])
                if k == 0:
                    ts(V(cacc_b, cacc[:]), V(glu_b, glu[:, 0:TT]), wk, V(cvec_b, cvec[:, 0, j:j + 1]), ALU.mult, ALU.add)
                elif k < 30:
                    stt(V(cacc_b, cacc[:]), V(glu_b, glu[:, k:k + TT]), wk, V(cacc_b, cacc[:]), ALU.mult, ALU.add)
                else:
                    stt(V(ypre_b, ypre[:, j, :]), V(glu_b, glu[:, k:k + TT]), wk, V(cacc_b, cacc[:]), ALU.mult, ALU.add)
        if CUT <= 1:
            return
        S.dma("pool", wab[:, :, 0:NH], win_v[:, :, QB + 4 * DNW:QB + 4 * DNW + NH], reads=[], writes=[V(wab_b, wab[:])], key=wab_b, slow=True)
        S.dma("pool", wab[:, :, 16:16 + NH], win_v[:, :, QB + 4 * DNW + NH:QB + 4 * DNW + 2 * NH], reads=[], writes=[V(wab_b, wab[:])], key=wab_b, slow=True)
        pab, pab_b = PS()
        for kc in range(KD):
            mm(V(pab_b, pab[0:NH, 0:TT]), V(wab_b, wab[:, kc, 0:NH]), V(hT_b, hT[:, kc, :]), start=(kc == 0), stop=(kc == KD - 1))
        cp(V(abT_b, abT[0:NH, :]), V(pab_b, pab[0:NH, 0:TT]))
        pab2, pab2_b = PS()
        for kc in range(KD):
            mm(V(pab2_b, pab2[32:32 + NH, 0:TT]), V(wab_b, wab[:, kc, 16:16 + NH]), V(hT_b, hT[:, kc, :]), start=(kc == 0), stop=(kc == KD - 1))
        cp(V(abT_b, abT[32:32 + NH, :]), V(pab2_b, pab2[32:32 + NH, 0:TT]))

        if CUT <= 2:
            return
        inherit([bb for bb in st1_bufs], [aT_alias_b] + [w[1] for w in wdp])
        for hp in range(0, NH, 2):
            hs = [hh for hh in (hp, hp + 1) if hh < NH]
            for i, hh in enumerate(hs):
                dn_prep(hh, dsets[i])
            gens = [dn_blocks(hh, dsets[i]) for i, hh in enumerate(hs)]
            while gens:
                for g in list(gens):
                    try:
                        next(g)
                    except StopIteration:
                        gens.remove(g)
        inherit([aT_alias_b] + [w[1] for w in wdp], [bb for bb in st1_bufs])

        if SPLIT:
            yv = yT[:, KCg:KCg + NHg, :].rearrange("p k t -> p (k t)")
            ypv = ypre.rearrange("p k t -> p (k t)")
            S.dma("sp", gx["cin1"][0], ypv[:, 0:KC * TT], reads=[V(ypre_b, ypre)], writes=[V(gx["cin1"][1], gx["cin1"][0])], key=ypre_b)
            S.dma("sp", gx["cin2"][0], yv[:, 0:NH * TT], reads=[V(yT_b, yT[:])], writes=[V(gx["cin2"][1], gx["cin2"][0])], key=yT_b)
            S.coll(gx["cin1"][0], gx["cout1"][0], PAIRS, reads=[V(gx["cin1"][1], gx["cin1"][0])], writes=[V(gx["cout1"][1], gx["cout1"][0])], key=gx["cout1"][1])
            S.coll(gx["cin2"][0], gx["cout2"][0], PAIRS, reads=[V(gx["cin2"][1], gx["cin2"][0])], writes=[V(gx["cout2"][1], gx["cout2"][0])], key=gx["cout2"][1])
            S.dma("sp", ypv.rearrange("p (r c) -> p r c", r=2), gx["cout1"][0].rearrange("(r p) c -> p r c", p=128),
                  reads=[V(gx["cout1"][1], gx["cout1"][0])], writes=[V(ypre_b, ypre)], key=ypre_b)
            S.dma("sp", yv.rearrange("p (r c) -> p r c", r=2), gx["cout2"][0].rearrange("(r p) c -> p r c", p=128),
                  reads=[V(gx["cout2"][1], gx["cout2"][0])], writes=[V(yT_b, yT[:])], key=yT_b)
        pmn, pmn_b = PS()
        pvr, pvr_b = PS()
        for j in range(KCg):
            mm(V(pmn_b, pmn[:, 0:TT]), ONES, V(ypre_b, ypre[:, j, :]), start=(j == 0), stop=(j == KCg - 1))
        for j in range(KCg):
            yq, yqb = RB[("Q", "lsq")[j % 2]]
            act(V(yqb, yq[:]), V(ypre_b, ypre[:, j, :]), AF.Square)
            mm(V(pvr_b, pvr[:, 0:TT]), ONES, V(yqb, yq[:]), start=(j == 0), stop=(j == KCg - 1), sig=True)
        ts(V(mean_b, mean_t[:]), V(pmn_b, pmn[:, 0:TT]), 1.0 / (KCg * 128), None, ALU.mult)
        tt(V(rs2_b, rs2[:]), V(mean_b, mean_t[:]), V(mean_b, mean_t[:]), ALU.mult)
        stt(V(rs2_b, rs2[:]), V(pvr_b, pvr[:, 0:TT]), 1.0 / (KCg * 128), V(rs2_b, rs2[:]), ALU.mult, ALU.subtract)
        act(V(rs2_b, rs2[:]), V(rs2_b, rs2[:]), AF.Sqrt, bias=EPS, scale=1.0)
        S.op("dve", lambda e: e.reciprocal(rs2[:], rs2[:]), reads=[V(rs2_b, rs2[:])], writes=[V(rs2_b, rs2[:])])
        for j in range(KCg):
            tt(V(cacc_b, cacc[:]), V(ypre_b, ypre[:, j, :]), V(mean_b, mean_t[:]), ALU.subtract)
            tt(V(cacc_b, cacc[:]), V(cacc_b, cacc[:]), V(rs2_b, rs2[:]), ALU.mult)
            act(V(yT_b, yT[:, j, :]), V(cacc_b, cacc[:]), AF.Silu, bias=V(lnT_b, lnT[:, 1, j:j + 1]),
                scale=V(lnT_b, lnT[:, 0, j:j + 1]))

        if CUT <= 11:
            return
        for m in range(KD):
            wt, wb = wload(wa, wa_i, wout_v[:, :, m * 128:(m + 1) * 128], KY, "out%d" % m)
            p, pb = PS()
            for kc in range(KY):
                mm(V(pb, p[:, 0:TT]), V(wb, wt[:, kc, :]), V(yT_b, yT[:, kc, :]), start=(kc == 0), stop=(kc == KY - 1))
            stt(V(xT_b, xT[:, m, :]), V(pb, p[:, 0:TT]), V(Gcoef_b, Gcoef[:, 1, m:m + 1]), V(xT_b, xT[:, m, :]), ALU.mult, ALU.add)

    out_bufs = []
    for t in range(NT):
        cur_tile[0] = t
        for tb in range(NB):
            xt, xb = xin[tb % 2]
            r0 = t * TT + tb * 128
            ld(V(xb, xt[:]), x_d[r0:r0 + 128, :])
            for k0 in range(0, KD, 4):
                p, pb = PS()
                nk = min(4, KD - k0)
                for kk in range(nk):
                    tr(V(pb, p[:, kk * 128:(kk + 1) * 128]), V(xb, xt[:, (k0 + kk) * 128:(k0 + kk + 1) * 128]))
                S.op("dve", lambda e, p=p, k0=k0, nk=nk, tb=tb: e.tensor_copy(
                    xT[:, k0:k0 + nk, tb * 128:(tb + 1) * 128], p[:, 0:nk * 128].rearrange("p (k t) -> p k t", k=nk)),
                    reads=[V(pb, p[:])], writes=[V(xT_b, xT[:])])
        rms_mod(0)
        ffn(0, 0)
        if stop != "ffn1":
            rms_mod(1)
            mixer()
            if stop != "mixer":
                rms_mod(2)
                ffn(1, 2)
        if stop is None:
            rms_rstd()
            for kc in range(KD):
                tt(V(tmpn_b, tmpn[:]), V(xT_b, xT[:, kc, :]), V(rstd_b, rstd[:]), ALU.mult)
                ts(V(xT_b, xT[:, kc, :]), V(tmpn_b, tmpn[:]), V(normT_b, normT[:, 3, kc:kc + 1]), None, ALU.mult)
        for tb in range(NB):
            xt, xb = xin[tb % 2]
            r0 = t * TT + tb * 128
            for k0 in range(0, KD, 4):
                p, pb = PS()
                nk = min(4, KD - k0)
                for kk in range(nk):
                    tr(V(pb, p[:, kk * 128:(kk + 1) * 128]), V(xT_b, xT[:, k0 + kk, tb * 128:(tb + 1) * 128]))
                cp(V(xb, xt[:, k0 * 128:(k0 + nk) * 128]), V(pb, p[:, 0:nk * 128]), eng="act")
            S.dma("sp", y_d[r0:r0 + 128, :], xt[:], reads=[V(xb, xt[:])], writes=[], key=xb)
            if xb not in out_bufs:
                out_bufs.append(xb)
    S.emit(final_waits=out_bufs)
    es.close()
    return nc


def host_inputs(cfg, core, x, c, w_ada, b_ada, ffn1_norm, ffn1_wg, ffn1_wu, ffn1_wd, mix_norm, w_in,
                w_dw, b_dw, conv_ln_w, conv_ln_b, w_short, a_log, dt_bias, dn_norm_w, w_out,
                ffn2_norm, ffn2_wg, ffn2_wu, ffn2_wd, final_norm):
    KD, KC, NH, TT = cfg["KD"], cfg["KC"], cfg["NH"], cfg["TT"]
    split = cfg.get("split", 0)
    CWg, NHg, CW = cfg["CWg"], cfg["NHg"], cfg["CW"]
    KCg = CWg // 128
    if split:
        b, rank = core // 2, core % 2
    else:
        b, rank = core, 0
    f = np.float32
    A = np.ascontiguousarray

    def fm(v, k):
        return A(np.asarray(v, f).reshape(k, 128).T)

    consts = np.zeros((128, 12, 128), f)
    idx = np.arange(128)
    consts[:, 0, :] = np.eye(128, dtype=f)
    consts[:, 1, :] = 1.0
    consts[:, 2, :] = np.where(idx[:, None] > idx[None, :], 0.0, NEG)
    consts[:, 3, :] = np.where(idx[None, :] > idx[:, None], 0.0, NEG)
    consts[:, 4, :] = np.where(idx[None, :] >= idx[:, None], 0.0, NEG)
    consts[:, 5, :] = (idx[:, None] // 16 == idx[None, :] // 16)
    for i, bsz in enumerate((16, 32, 64)):
        ll = ((idx[:, None] // (2 * bsz) == idx[None, :] // (2 * bsz)) & (idx[:, None] % (2 * bsz) >= bsz)
              & (idx[None, :] % (2 * bsz) < bsz)).astype(f)
        consts[:, 6 + i, :] = ll
        consts[:, 9 + i, :] = ll.T
    rmask = np.ones((128, TT), f)
    rmask[:, ::128] = 0.0
    DNWg = NHg * 128
    DNW = NH * 128
    c0, h0 = rank * CW, rank * NH
    cols = np.concatenate([
        np.arange(c0, c0 + CW), CWg + np.arange(c0, c0 + CW),
        2 * CWg + 0 * DNWg + h0 * 128 + np.arange(DNW), 2 * CWg + 1 * DNWg + h0 * 128 + np.arange(DNW),
        2 * CWg + 2 * DNWg + h0 * 128 + np.arange(DNW), 2 * CWg + 3 * DNWg + h0 * 128 + np.arange(DNW),
        2 * CWg + 4 * DNWg + h0 + np.arange(NH), 2 * CWg + 4 * DNWg + NHg + h0 + np.arange(NH)])
    shc = np.concatenate([i3 * DNWg + h0 * 128 + np.arange(DNW) for i3 in range(3)])
    w_in_l = np.asarray(w_in[0], f)[:, cols]
    w_dw_l = np.asarray(w_dw[0], f)[:, c0:c0 + CW]
    w_sh_l = np.asarray(w_short[0], f)[:, shc]
    z1 = np.zeros(CW, f)
    if split:
        ncr = min(4, cfg["ncores"])
        nb = ncr // 2
        nml = 9 * KD // ncr
        gi, ri = core // ncr, core % ncr
        msl = slice(ri * nml * 128, (ri + 1) * nml * 128)
        ohm = np.zeros((128, nb), f)
        ohm[:, b - gi * nb] = 1.0
        cg_ = np.asarray(c, f)[gi * nb:(gi + 1) * nb]
        mods_in = {"cT": A(cg_.T.reshape(KD, 128, nb).transpose(1, 0, 2)),
                   "w_ada": A(np.asarray(w_ada[0], f)[:, msl]),
                   "b_adaT": fm(np.asarray(b_ada[0], f)[msl], nml),
                   "onehot": ohm}
    else:
        mods_in = {"cT": fm(c[b], KD), "w_ada": A(np.asarray(w_ada[0], f)), "b_adaT": fm(b_ada[0], 9 * KD)}
    DFFl = cfg["DFF"] // (1 + split)
    fs = slice(rank * DFFl, (rank + 1) * DFFl)
    return {
        "x": A(np.asarray(x[b], f)),
        **mods_in,
        "normT": A(np.stack([fm(ffn1_norm[0], KD), fm(mix_norm[0], KD), fm(ffn2_norm[0], KD), fm(final_norm, KD)], axis=1)),
        "ffn1_wg": A(np.asarray(ffn1_wg[0], f)[:, fs]), "ffn1_wu": A(np.asarray(ffn1_wu[0], f)[:, fs]), "ffn1_wd": A(np.asarray(ffn1_wd[0], f)[fs, :]),
        "ffn2_wg": A(np.asarray(ffn2_wg[0], f)[:, fs]), "ffn2_wu": A(np.asarray(ffn2_wu[0], f)[:, fs]), "ffn2_wd": A(np.asarray(ffn2_wd[0], f)[fs, :]),
        "w_in": A(w_in_l),
        "w_out": A(np.asarray(w_out[0], f)),
        "w_dwT": A(w_dw_l.T.reshape(KC, 128, 31).transpose(1, 0, 2)),
        "cvecT": A(np.stack([fm(np.asarray(b_dw[0], f)[c0:c0 + CW], KC), fm(z1, KC), fm(z1, KC)], axis=1)),
        "lnT": A(np.stack([fm(conv_ln_w[0], KCg), fm(conv_ln_b[0], KCg)], axis=1)),
        "w_shT": A(w_sh_l.T.reshape(3 * NH, 128, 4).transpose(1, 0, 2)),
        "hvec": A(np.broadcast_to(np.stack([np.asarray(a_log[0], f)[h0:h0 + NH], np.asarray(dt_bias[0], f)[h0:h0 + NH]], axis=0)[None], (128, 2, NH))),
        "onwT": A(np.asarray(dn_norm_w[0], f).reshape(128, 1)),
        "consts": consts,
        "rmask": rmask,
    }


_NC_CACHE = {}


def kernel(**inputs):
    cfg = FULL
    B = inputs["x"].shape[0]
    if "nc" not in _NC_CACHE:
        _NC_CACHE["nc"] = build(cfg)
    nc = _NC_CACHE["nc"]
    n = 8
    in_maps = [host_inputs(cfg, i, **inputs) for i in range(n)]
    res = run_bass_kernel_spmd(nc, in_maps, core_ids=list(range(n)))
    out = np.stack([res.results[2 * i]["y"] for i in range(B)], axis=0)
    return out.astype(np.float32)
```

```python
import numpy as np
import os
CUT = int(os.environ.get('KCUT', '99'))
from contextlib import ExitStack
import concourse.bass as bass
import concourse.mybir as mybir
from concourse.bass_utils import run_bass_kernel_spmd

F32 = mybir.dt.float32
BF16 = mybir.dt.bfloat16
AF = mybir.ActivationFunctionType
ALU = mybir.AluOpType
NEG = -1.0e9
EPS = 1e-6


class Buf:
    __slots__ = ("name", "w", "r", "dsem", "psum")

    def __init__(self, name, psum=False):
        self.name = name
        self.psum = psum
        self.w = None
        self.r = []
        self.dsem = None


class V:
    __slots__ = ("b", "ap")

    def __init__(self, b, ap):
        self.b = b
        self.ap = ap


class DSem:
    def __init__(self, sem):
        self.sem = sem
        self.count = 0


class Rec:
    __slots__ = ("eng", "fn", "deps", "sig", "idx", "dma", "dval", "dinc")

    def __init__(self, eng, fn, deps, sig, dma=None):
        self.eng = eng
        self.fn = fn
        self.deps = deps
        self.sig = sig
        self.dma = dma
        self.dval = 0
        self.dinc = 16
        self.idx = -1


class Sched:
    ENGS = ("pe", "dve", "act", "pool", "sp")

    def __init__(self, nc, es):
        self.nc = nc
        self.es = es
        self.q = {e: [] for e in self.ENGS}
        self.esem = {e: es.enter_context(nc.semaphore("s_" + e)) for e in ("pe", "dve", "act", "pool")}
        self.nsem = 0

    def _deps(self, eng, reads, writes, is_dma):
        deps = []
        for v in reads:
            w = v.b.w
            if w is not None:
                deps.append((w, "raw"))
            if v.b.psum:
                for r in v.b.r:
                    if r.eng != eng:
                        deps.append((r, "rr"))
        for v in writes:
            b = v.b
            if b.w is not None:
                deps.append((b.w, "waw"))
            for r in b.r:
                deps.append((r, "war"))
        out = []
        for (d, kind) in deps:
            if d.dma is None and not is_dma and d.eng == eng:
                if eng == "pe":
                    continue
            if d.dma is not None:
                out.append((d, d.dma.count))
            else:
                out.append((d, None))
        return out

    def _commit(self, rec, reads, writes):
        for v in writes:
            v.b.w = rec
            v.b.r = []
        for v in reads:
            if v.b.w is not rec:
                v.b.r.append(rec)

    def op(self, eng, fn, reads=(), writes=(), sig=True):
        rec = Rec(eng, fn, self._deps(eng, reads, writes, False), sig)
        rec.idx = len(self.q[eng])
        self.q[eng].append(rec)
        self._commit(rec, reads, writes)
        return rec

    def dma(self, queue, out, in_, reads, writes, key, slow=False):
        if key.dsem is None:
            key.dsem = {}
        if queue not in key.dsem:
            key.dsem[queue] = DSem(self.es.enter_context(self.nc.semaphore("d%d" % self.nsem)))
            self.nsem += 1
        ds = key.dsem[queue]
        rec = Rec(queue, lambda e: e.dma_start(out=out, in_=in_, allow_slow_non_contiguous=slow), self._deps(queue, reads, writes, True), True, dma=ds)
        ds.count += 16
        rec.dval = ds.count
        rec.idx = len(self.q[queue])
        self.q[queue].append(rec)
        self._commit(rec, reads, writes)
        return rec

    def coll(self, ins_ap, outs_ap, groups, reads, writes, key):
        if key.dsem is None:
            key.dsem = {}
        if "coll" not in key.dsem:
            key.dsem["coll"] = DSem(self.es.enter_context(self.nc.semaphore("c%d" % self.nsem)))
            self.nsem += 1
        ds = key.dsem["coll"]
        rec = Rec("pool", lambda e: e.collective_compute("AllGather", ALU.bypass, replica_groups=groups,
                                                         ins=[ins_ap], outs=[outs_ap]),
                  self._deps("pool", reads, writes, True), True, dma=ds)
        rec.dinc = 1
        ds.count += 1
        rec.dval = ds.count
        rec.idx = len(self.q["pool"])
        self.q["pool"].append(rec)
        self._commit(rec, reads, writes)
        return rec

    def emit(self, final_waits=()):
        nc = self.nc
        cnt = {}
        for e in ("pe", "dve", "act", "pool"):
            arr = []
            c = 0
            for r in self.q[e]:
                if r.dma is None and r.sig:
                    c += 1
                arr.append(c)
            res = [0] * len(arr)
            nxt = None
            for i in range(len(arr) - 1, -1, -1):
                r = self.q[e][i]
                if r.dma is None and r.sig:
                    nxt = arr[i]
                res[i] = nxt
            cnt[e] = res
        handles = {"pe": nc.tensor, "dve": nc.vector, "act": nc.scalar, "pool": nc.gpsimd, "sp": nc.sync}
        with nc.Block() as block:
            def run(e, h):
                waited = {}
                for r in self.q[e]:
                    for (d, dv) in r.deps:
                        if d.dma is not None:
                            sem, val = d.dma.sem, dv
                        else:
                            val = cnt[d.eng][d.idx]
                            assert val is not None, "dep on op with no later signal"
                            sem = self.esem[d.eng]
                        k = id(sem)
                        if waited.get(k, 0) >= val:
                            continue
                        waited[k] = val
                        h.wait_ge(sem, val)
                    ins = r.fn(h)
                    if r.dma is not None:
                        ins.then_inc(r.dma.sem, r.dinc)
                    elif r.sig:
                        ins.then_inc(self.esem[e], 1)
                if e == "sp":
                    for b in final_waits:
                        for ds_ in b.dsem.values():
                            h.wait_ge(ds_.sem, ds_.count)

            @block.tensor
            def _(h):
                run("pe", h)

            @block.vector
            def _(h):
                run("dve", h)

            @block.scalar
            def _(h):
                run("act", h)

            @block.gpsimd
            def _(h):
                run("pool", h)

            @block.sync
            def _(h):
                run("sp", h)


def make_cfg(D, DFF, CW, NH, T, split=0):
    f = 1 + split
    CWl, NHl = CW // f, NH // f
    return dict(D=D, DFF=DFF, CW=CWl, NH=NHl, T=T, KD=D // 128, KF=DFF // 128, KC=CWl // 128, split=split,
                CWg=CW, NHg=NH, TT=min(512, T), INC=2 * CWl + 4 * NHl * 128 + 2 * NHl, ncores=8, B=4)


FULL = make_cfg(2048, 5632, 1024, 8, 4096, split=1)


def build(cfg, stop=None):
    D, DFF, CW, NH, T = cfg["D"], cfg["DFF"], cfg["CW"], cfg["NH"], cfg["T"]
    KD, KF, KC, TT, INC = cfg["KD"], cfg["KF"], cfg["KC"], cfg["TT"], cfg["INC"]
    NT = T // TT
    NB = TT // 128
    SPLIT = cfg.get("split", 0)
    KCg = cfg["CWg"] // 128
    NHg = cfg["NHg"]
    KY = KCg + NHg
    DNW = NH * 128
    nc = bass.Bass("TRN2", target_bir_lowering=False)
    es = ExitStack()
    S = Sched(nc, es)

    def din(name, shape, dt=F32):
        return nc.dram_tensor(name, list(shape), dt, kind="ExternalInput").ap()

    x_d = din("x", [T, D])
    y_d = nc.dram_tensor("y", [T, D], F32, kind="ExternalOutput").ap()
    MSPLIT = cfg.get("split", 0)
    if MSPLIT:
        MG_ = min(4, cfg["ncores"])
        NML_ = 9 * KD // MG_
        cT_d = din("cT", [128, KD, MG_ // 2])
        wada_d = din("w_ada", [D, NML_ * 128])
        badaT_d = din("b_adaT", [128, NML_])
        oh_d = din("onehot", [128, MG_ // 2])
    else:
        cT_d = din("cT", [128, KD])
        wada_d = din("w_ada", [D, 9 * D])
        badaT_d = din("b_adaT", [128, 9 * KD])
    normT_d = din("normT", [128, 4, KD])
    FSPLIT = cfg.get("split", 0)
    DFFl = DFF // (1 + FSPLIT)
    wg_d = [din("ffn1_wg", [D, DFFl]), din("ffn2_wg", [D, DFFl])]
    wu_d = [din("ffn1_wu", [D, DFFl]), din("ffn2_wu", [D, DFFl])]
    wd_d = [din("ffn1_wd", [DFFl, D]), din("ffn2_wd", [DFFl, D])]
    win_d = din("w_in", [D, INC])
    wout_d = din("w_out", [D, D])
    wdwT_d = din("w_dwT", [128, KC, 31])
    cvec_d = din("cvecT", [128, 3, KC])
    lnT_d = din("lnT", [128, 2, KCg])
    wshT_d = din("w_shT", [128, 3 * NH, 4])
    hvec_d = din("hvec", [128, 2, NH])
    onw_d = din("onwT", [128, 1])
    cst_d = din("consts", [128, 12, 128])
    rmask_d = din("rmask", [128, TT])

    SBTOT = [0]

    def sb(name, shape, dt=F32):
        t = es.enter_context(nc.sbuf_tensor("sb_" + name, list(shape), dt))
        nbytes = int(np.prod(shape[1:])) * (2 if dt == BF16 else 4)
        SBTOT[0] += nbytes
        if os.environ.get("KSB"):
            print("SB", name, nbytes, SBTOT[0])
        return t, Buf(name)

    cst, cst_b = sb("cst", [128, 12, 128])
    rmask, rmask_b = sb("rmask", [128, TT])
    ones_bf, ones_bf_b = sb("ones_bf", [128, 128], BF16)
    normT, normT_b = sb("normT", [128, 4, KD])
    wdwT, wdwT_b = sb("wdwT", [128, KC, 31])
    cvec, cvec_b = sb("cvec", [128, 3, KC])
    lnT, lnT_b = sb("lnT", [128, 2, KCg])
    wshT, wshT_b = sb("wshT", [128, 3 * NH, 4])
    hvec, hvec_b = sb("hvec", [128, 2, NH])
    negA, negA_b = sb("negA", [128, NH])
    onw, onw_b = sb("onw", [128, 1])
    modsT, modsT_b = sb("modsT", [128, 9 * KD])
    Acoef, Acoef_b = sb("Acoef", [128, 3, KD])
    Gcoef, Gcoef_b = sb("Gcoef", [128, 3, KD])
    finw_b = normT_b
    xT, xT_b = sb("xT", [128, KD, TT])
    hT, hT_b = sb("hT", [128, KD, TT], BF16)
    KFH = KF // 2
    actT = [sb("actT%d" % i, [128, max(KFH, KD), TT], BF16) for i in range(1)]
    yT, yT_b = sb("yT", [128, KY, TT], BF16)
    sq, sq_b = actT[0]
    rstd, rstd_b = sb("rstd", [128, TT])
    xin_all, xin_all_b = sb("xin_all", [128, 2, D])
    xin = [(xin_all[:, i, :], xin_all_b) for i in range(2)]
    Sst = [sb("Sst%d" % h, [128, 128]) for h in range(NH)]
    gtail = [sb("gtail%d" % j, [128, 30]) for j in range(KC)]
    ptail = [sb("ptail%d" % j, [128, 3]) for j in range(3 * NH)]
    NWA, NWD = 6, 2
    wa = [sb("wa%d" % i, [128, KD, 128], BF16) for i in range(NWA)]
    wdp = [sb("wd%d" % i, [128, KF // 2, 128], BF16) for i in range(NWD)]
    wa_i = [0]
    wd_i = [0]
    psum = []
    for i in range(8):
        t = es.enter_context(nc.psum_tensor("ps%d" % i, [128, 512], F32))
        psum.append((t, Buf("ps%d" % i, psum=True)))
    ps_i = [0]

    def PS():
        t, b = psum[ps_i[0] % 8]
        ps_i[0] += 1
        return t, b

    IDN = V(cst_b, cst[:, 0, :])
    ONES = V(cst_b, cst[:, 1, :])
    NEGS = V(cst_b, cst[:, 2, :])
    NEGST = V(cst_b, cst[:, 3, :])
    NEGTT = V(cst_b, cst[:, 4, :])
    BD16 = V(cst_b, cst[:, 5, :])
    LLm = [V(cst_b, cst[:, 6 + i, :]) for i in range(3)]
    URm = [V(cst_b, cst[:, 9 + i, :]) for i in range(3)]

    def mm(out, lhsT, rhs, start=True, stop=True, sig=None):
        if sig is None:
            sig = stop
        return S.op("pe", lambda e: e.matmul(out.ap, lhsT.ap, rhs.ap, start=start, stop=stop),
                    reads=[lhsT, rhs], writes=[out], sig=sig)

    def tr(out, in_):
        return S.op("pe", lambda e: e.transpose(out.ap, in_.ap, IDN.ap), reads=[in_, IDN], writes=[out])

    def act(out, in_, func, bias=None, scale=None, accum=None, eng="act"):
        rd = [in_]
        kw = {}
        if bias is not None:
            if isinstance(bias, V):
                rd.append(bias)
                kw["bias"] = bias.ap
            else:
                kw["bias"] = float(bias)
        if scale is not None:
            if isinstance(scale, V):
                rd.append(scale)
                kw["scale"] = scale.ap
            else:
                kw["scale"] = float(scale)
        wr = [out]
        if accum is not None:
            kw["accum_out"] = accum.ap
            wr.append(accum)
        return S.op("act", lambda e: e.activation(out.ap, in_.ap, func, **kw), reads=rd, writes=wr)

    def tt(out, a, b, op, eng="dve"):
        return S.op(eng, lambda e: e.tensor_tensor(out.ap, a.ap, b.ap, op), reads=[a, b], writes=[out])

    def ts(out, a, s1, s2, op0, op1=None, eng="dve"):
        rd = [a]
        s1a = s1.ap if isinstance(s1, V) else s1
        s2a = s2.ap if isinstance(s2, V) else s2
        if isinstance(s1, V):
            rd.append(s1)
        if isinstance(s2, V):
            rd.append(s2)
        if op1 is None:
            return S.op(eng, lambda e: e.tensor_scalar(out.ap, a.ap, s1a, None, op0), reads=rd, writes=[out])
        return S.op(eng, lambda e: e.tensor_scalar(out.ap, a.ap, s1a, s2a, op0, op1), reads=rd, writes=[out])

    def stt(out, a, s, b, op0, op1):
        rd = [a, b]
        sa = s.ap if isinstance(s, V) else s
        if isinstance(s, V):
            rd.append(s)
        return S.op("dve", lambda e: e.scalar_tensor_tensor(out.ap, a.ap, sa, b.ap, op0, op1), reads=rd, writes=[out])

    def cp(out, in_, eng="dve"):
        if eng == "act":
            return act(out, in_, AF.Copy)
        return S.op(eng, lambda e: e.tensor_copy(out.ap, in_.ap), reads=[in_], writes=[out])

    def ld(out, src_ap, queue="sp"):
        return S.dma(queue, out.ap, src_ap, reads=[], writes=[out], key=out.b)

    ld(V(cst_b, cst[:]), cst_d)
    ld(V(rmask_b, rmask[:]), rmask_d)
    ld(V(normT_b, normT[:]), normT_d)
    ld(V(wdwT_b, wdwT[:]), wdwT_d)
    ld(V(cvec_b, cvec[:]), cvec_d)
    ld(V(lnT_b, lnT[:]), lnT_d)
    ld(V(wshT_b, wshT[:]), wshT_d)
    ld(V(hvec_b, hvec[:]), hvec_d)
    ld(V(onw_b, onw[:]), onw_d)
    cp(V(ones_bf_b, ones_bf[:]), ONES)
    act(V(negA_b, negA[:]), V(hvec_b, hvec[:, 0, :]), AF.Exp)
    ts(V(negA_b, negA[:]), V(negA_b, negA[:]), -1.0, None, ALU.mult)
    for h in range(NH):
        S.op("dve", lambda e, h=h: e.memset(Sst[h][0][:], 0.0), writes=[V(Sst[h][1], Sst[h][0][:])])
    for j in range(KC):
        S.op("dve", lambda e, j=j: e.memset(gtail[j][0][:], 0.0), writes=[V(gtail[j][1], gtail[j][0][:])])
    for j in range(3 * NH):
        S.op("dve", lambda e, j=j: e.memset(ptail[j][0][:], 0.0), writes=[V(ptail[j][1], ptail[j][0][:])])

    NMC = 9 * KD
    wada_v = wada_d.rearrange("(k p) n -> p k n", p=128)
    if not MSPLIT:
        scT, scT_b = sb("scT", [128, KD])
        badaT, badaT_b = sb("badaT", [128, 9 * KD])
        ld(V(scT_b, scT[:]), cT_d)
        ld(V(badaT_b, badaT[:]), badaT_d)
        act(V(scT_b, scT[:]), V(scT_b, scT[:]), AF.Silu)
        pm, pm_b = PS()
        for j in range(NMC):
            wt0, wb = xin[j % 2]
            wt = wt0[:].rearrange("p (k n) -> p k n", k=KD)
            S.dma("sp", wt, wada_v[:, :, j * 128:(j + 1) * 128], reads=[], writes=[V(wb, wt)], key=wb)
            for kc in range(KD):
                mm(V(pm_b, pm[:, j:j + 1]), V(wb, wt[:, kc, :]), V(scT_b, scT[:, kc:kc + 1]),
                   start=(kc == 0), stop=(kc == KD - 1))
        tt(V(modsT_b, modsT[:]), V(pm_b, pm[:, 0:NMC]), V(badaT_b, badaT[:]), ALU.add)
    else:
        NCR = min(4, cfg["ncores"])
        NB_ = NCR // 2
        NML = NMC // NCR
        assert NML * NCR == NMC
        scT, scT_b = sb("scT", [128, KD, NB_])
        badaT, badaT_b = sb("badaT", [128, NML])
        oh, oh_b = sb("oh", [128, NB_])
        mp, mp_b = sb("mp", [128, NML, NB_])
        gath, gath_b = sb("gath", [128, NCR * NML, NB_])
        ld(V(scT_b, scT[:]), cT_d)
        ld(V(badaT_b, badaT[:]), badaT_d)
        ld(V(oh_b, oh[:]), oh_d)
        act(V(scT_b, scT[:]), V(scT_b, scT[:]), AF.Silu)
        pm, pm_b = PS()
        for j in range(NML):
            wt0, wb = xin[j % 2]
            wt = wt0[:].rearrange("p (k n) -> p k n", k=KD)
            S.dma("sp", wt, wada_v[:, :, j * 128:(j + 1) * 128], reads=[], writes=[V(wb, wt)], key=wb)
            for kc in range(KD):
                mm(V(pm_b, pm[:, j * NB_:(j + 1) * NB_]), V(wb, wt[:, kc, :]), V(scT_b, scT[:, kc, :]),
                   start=(kc == 0), stop=(kc == KD - 1))
        for j in range(NML):
            ts(V(mp_b, mp[:, j, :]), V(pm_b, pm[:, j * NB_:(j + 1) * NB_]), V(badaT_b, badaT[:, j:j + 1]), None, ALU.add)
        mcin = (nc.dram_tensor("mx_cin", [128, NML * NB_], F32, kind="Internal").ap(), Buf("mx_cin"))
        mcout = (nc.dram_tensor("mx_cout", [NCR * 128, NML * NB_], F32, kind="Internal").ap(), Buf("mx_cout"))
        S.dma("sp", mcin[0], mp[:].rearrange("p j b -> p (j b)"), reads=[V(mp_b, mp[:])], writes=[V(mcin[1], mcin[0])], key=mp_b)
        S.coll(mcin[0], mcout[0], [list(range(g_ * NCR, (g_ + 1) * NCR)) for g_ in range(cfg["ncores"] // NCR)], reads=[V(mcin[1], mcin[0])], writes=[V(mcout[1], mcout[0])], key=mcout[1])
        S.dma("sp", gath[:].rearrange("p (r j) b -> p r (j b)", r=NCR), mcout[0].rearrange("(r p) c -> p r c", p=128),
              reads=[V(mcout[1], mcout[0])], writes=[V(gath_b, gath[:])], key=gath_b)
        ts(V(modsT_b, modsT[:]), V(gath_b, gath[:, :, 0]), V(oh_b, oh[:, 0:1]), None, ALU.mult)
        for bb in range(1, NB_):
            stt(V(modsT_b, modsT[:]), V(gath_b, gath[:, :, bb]), V(oh_b, oh[:, bb:bb + 1]), V(modsT_b, modsT[:]), ALU.mult, ALU.add)
    for i in range(3):
        sc = V(modsT_b, modsT[:, (3 * i + 1) * KD:(3 * i + 2) * KD])
        gt = V(modsT_b, modsT[:, (3 * i + 2) * KD:(3 * i + 3) * KD])
        stt(V(Acoef_b, Acoef[:, i, :]), sc, 1.0, V(normT_b, normT[:, i, :]), ALU.add, ALU.mult)
        ts(V(Gcoef_b, Gcoef[:, i, :]), gt, 0.5 if i != 1 else 1.0, None, ALU.mult)

    def shift_col(i, kc):
        return V(modsT_b, modsT[:, 3 * i * KD + kc:3 * i * KD + kc + 1])

    def rms_rstd():
        for kc in range(KD):
            act(V(sq_b, sq[:, kc, :]), V(xT_b, xT[:, kc, :]), AF.Square)
        p, pb = PS()
        for kc in range(KD):
            mm(V(pb, p[:, 0:TT]), V(ones_bf_b, ones_bf[:]), V(sq_b, sq[:, kc, :]), start=(kc == 0), stop=(kc == KD - 1))
        act(V(rstd_b, rstd[:]), V(pb, p[:, 0:TT]), AF.Sqrt, bias=EPS, scale=1.0 / D)
        S.op("dve", lambda e: e.reciprocal(rstd[:], rstd[:]), reads=[V(rstd_b, rstd[:])], writes=[V(rstd_b, rstd[:])])

    tmpn, tmpn_b = sb("tmpn", [128, TT])
    cacc, cacc_b = tmpn, tmpn_b

    def rms_mod(i):
        rms_rstd()
        for kc in range(KD):
            tt(V(tmpn_b, tmpn[:]), V(xT_b, xT[:, kc, :]), V(rstd_b, rstd[:]), ALU.mult)
            act(V(hT_b, hT[:, kc, :]), V(tmpn_b, tmpn[:]), AF.Identity,
                bias=shift_col(i, kc), scale=V(Acoef_b, Acoef[:, i, kc:kc + 1]))

    scr = {}
    cur_tile = [0]

    def wload(dst_pool, idx, src, K, tag):
        t, b = dst_pool[idx[0] % len(dst_pool)]
        idx[0] += 1
        if tag not in scr:
            dt_ = nc.dram_tensor("scr_" + tag, [128, K * 128], BF16, kind="Internal").ap()
            scr[tag] = (dt_, Buf("scr_" + tag))
        sap, sbuf_ = scr[tag]
        sview = sap.rearrange("p (k n) -> p k n", k=K)
        if cur_tile[0] == 0:
            S.dma("pool", t[:, 0:K, :], src, reads=[], writes=[V(b, t[:])], key=b)
            if NT > 1:
                S.dma("sp", sview, t[:, 0:K, :], reads=[V(b, t[:])], writes=[V(sbuf_, sap)], key=b)
        else:
            S.dma("sp", t[:, 0:K, :], sview, reads=[V(sbuf_, sap)], writes=[V(b, t[:])], key=b)
        return t, b

    sg, sg_b = sb("sg", [128, TT])

    def ffn(f, i):
        aT, aT_b = actT[0]
        wgv = wg_d[f].rearrange("(k p) n -> p k n", p=128)
        wuv = wu_d[f].rearrange("(k p) n -> p k n", p=128)
        wdv = wd_d[f].rearrange("(k p) n -> p k n", p=128)
        for hf in range(1 if FSPLIT else 2):
            for jj in range(KFH):
                j = hf * KFH + jj
                gt_, gb_ = wload(wa, wa_i, wgv[:, :, j * 128:(j + 1) * 128], KD, "g%d_%d" % (f, j))
                ut_, ub_ = wload(wa, wa_i, wuv[:, :, j * 128:(j + 1) * 128], KD, "u%d_%d" % (f, j))
                pg, pgb = PS()
                pu, pub = PS()
                for kc in range(KD):
                    mm(V(pgb, pg[:, 0:TT]), V(gb_, gt_[:, kc, :]), V(hT_b, hT[:, kc, :]), start=(kc == 0), stop=(kc == KD - 1))
                for kc in range(KD):
                    mm(V(pub, pu[:, 0:TT]), V(ub_, ut_[:, kc, :]), V(hT_b, hT[:, kc, :]), start=(kc == 0), stop=(kc == KD - 1))
                act(V(sg_b, sg[:]), V(pgb, pg[:, 0:TT]), AF.Silu)
                tt(V(aT_b, aT[:, jj, :]), V(pub, pu[:, 0:TT]), V(sg_b, sg[:]), ALU.mult)
            for m in range(KD):
                dt_, db_ = wload(wdp, wd_i, wdv[:, hf * KFH:(hf + 1) * KFH, m * 128:(m + 1) * 128], KFH, "d%d_%d_%d" % (f, hf, m))
                pd, pdb = PS()
                for kf in range(KFH):
                    mm(V(pdb, pd[:, 0:TT]), V(db_, dt_[:, kf, :]), V(aT_b, aT[:, kf, :]), start=(kf == 0), stop=(kf == KFH - 1))
                if not FSPLIT:
                    stt(V(xT_b, xT[:, m, :]), V(pdb, pd[:, 0:TT]), V(Gcoef_b, Gcoef[:, i, m:m + 1]), V(xT_b, xT[:, m, :]),
                        ALU.mult, ALU.add)
                    continue
                KH = KD // 2
                part, mo = m // KH, m % KH
                dsn = ("lnb", "beta")[m % 2]
                dsv = V(RB[dsn][1], RB[dsn][0][:])
                ts(dsv, V(pdb, pd[:, 0:TT]), V(Gcoef_b, Gcoef[:, i, m:m + 1]), None, ALU.mult)
                cinb = fx["cin%d" % part]
                S.dma("sp", cinb[0][:, mo * TT:(mo + 1) * TT], dsv.ap, reads=[dsv], writes=[V(cinb[1], cinb[0])], key=dsv.b)
                if mo == KH - 1:
                    coutb = fx["cout%d" % part]
                    S.coll(cinb[0], coutb[0], PAIRS, reads=[V(cinb[1], cinb[0])], writes=[V(coutb[1], coutb[0])], key=coutb[1])
            if FSPLIT:
                KH = KD // 2
                for m in range(KD):
                    part, mo = m // KH, m % KH
                    coutb = fx["cout%d" % part]
                    names = (("lsk", "Kd"), ("R", "P"))[m % 2]
                    for r_ in range(2):
                        tv_ = V(RB[names[r_]][1], RB[names[r_]][0][:])
                        S.dma("sp", tv_.ap, coutb[0][r_ * 128:(r_ + 1) * 128, mo * TT:(mo + 1) * TT],
                              reads=[V(coutb[1], coutb[0])], writes=[tv_], key=tv_.b)
                        tt(V(xT_b, xT[:, m, :]), V(xT_b, xT[:, m, :]), tv_, ALU.add)

    glu, glu_b = sb("glu", [128, 30 + TT])
    assert KCg * TT <= 2 * D
    ypre = xin_all[:].rearrange("p a d -> p (a d)")[:, 0:KCg * TT].rearrange("p (k t) -> p k t", k=KCg)
    ypre_b = xin_all_b
    pre, pre_b = glu, glu_b
    qkv = [sb("qkv%d" % i, [128, TT]) for i in range(3)]
    zs, zs_b = sb("zs", [128, TT])
    abT, abT_b = sb("abT", [64, TT])
    wab, wab_b = sb("wab", [128, KD, 32], BF16)
    RB = {n: sb("RB_" + n, [128, TT]) for n in ("lnb", "beta", "G", "lsk", "lsq", "R", "P", "Q", "Kd")}
    RB["t"] = RB["lsq"]
    RB["E"] = RB["lsq"]
    RB["g"] = RB["Kd"]
    RB["nEP"] = RB["lnb"]
    RB["nR"] = RB["lsk"]
    stk = [sb("stk%d" % i, [128, TT]) for i in range(2)]
    cols = [sb("cols%d" % i, [128, 128]) for i in range(2)]
    ktok, ktok_b = sb("ktok", [128, 128])
    vtok, vtok_b = sb("vtok", [128, 128])
    kd_t, kd_b = sb("kd_t", [128, 128])
    mA, mA_b = sb("mA", [128, 128])
    mB, mB_b = sb("mB", [128, 128])
    NAt, NA_b = sb("NAt", [128, 256])
    NBt, NB_b = sb("NBt", [128, 256])
    tA, tA_b = sb("tA", [128, 128])
    tB, tB_b = sb("tB", [128, 128])
    tY, tY_b = sb("tY", [128, 256])
    mean_t, mean_b = RB["R"]
    rs2, rs2_b = RB["P"]
    aTt, aTt_b = sb("aTt", [128, 128])
    qeT, qeT_b = sb("qeT", [128, 128])
    TbT, TbT_b = sb("TbT", [128, 128])
    TwT, TwT_b = sb("TwT", [128, 128])
    u_t, u_b = sb("u_t", [128, 128])
    nwT, nwT_b = sb("nwT", [128, 128])
    vnew, vnew_b = sb("vnew", [128, 128])
    on_t, on_b = TbT, TbT_b
    sml, sml_b = sb("sml", [128, 8])
    for i in range(2):
        S.op("dve", lambda e, i=i: e.memset(stk[i][0][:], 0.0), writes=[V(stk[i][1], stk[i][0][:])])
    S.op("dve", lambda e: e.memset(abT[:], 0.0), writes=[V(abT_b, abT[:])])
    selm, selm_b = sb("selm", [64, 2, 128])

    def build_sel(h):
        for r in range(2):
            rr = h if r == 0 else 32 + h
            ts(V(selm_b, selm[:, r, :]), V(cst_b, cst[0:64, 1, :]), V(cst_b, cst[0:64, 0, rr:rr + 1]), None, ALU.mult)

    gx = {}
    PAIRS = [[2 * i, 2 * i + 1] for i in range(cfg["ncores"] // 2)]
    if SPLIT:
        for nm_, shp_, dt_ in (("cin1", [128, KC * TT], F32), ("cout1", [256, KC * TT], F32),
                               ("cin2", [128, NH * TT], BF16), ("cout2", [256, NH * TT], BF16)):
            gx[nm_] = (nc.dram_tensor("gx_" + nm_, shp_, dt_, kind="Internal").ap(), Buf("gx_" + nm_))
    fx = {}
    if FSPLIT:
        for pi_ in range(2):
            fx["cin%d" % pi_] = (nc.dram_tensor("fx_cin%d" % pi_, [128, (KD // 2) * TT], F32, kind="Internal").ap(), Buf("fx_cin%d" % pi_))
            fx["cout%d" % pi_] = (nc.dram_tensor("fx_cout%d" % pi_, [256, (KD // 2) * TT], F32, kind="Internal").ap(), Buf("fx_cout%d" % pi_))
    win_v = win_d.rearrange("(k p) n -> p k n", p=128)
    wout_v = wout_d.rearrange("(k p) n -> p k n", p=128)
    QB = 2 * CW

    def proj(col0):
        wt, wb = wload(wa, wa_i, win_v[:, :, col0:col0 + 128], KD, "in%d" % col0)
        p, pb = PS()
        for kc in range(KD):
            mm(V(pb, p[:, 0:TT]), V(wb, wt[:, kc, :]), V(hT_b, hT[:, kc, :]), start=(kc == 0), stop=(kc == KD - 1))
        return V(pb, p[:, 0:TT])

    def rb(n, sl=None):
        t, b = RB[n]
        return V(b, t[:] if sl is None else t[:, sl])

    class DS:
        pass

    def mk_set0():
        d = DS()
        d.qkv = qkv
        d.zs = (zs, zs_b)
        d.RB = RB
        d.stk = stk
        d.cols = cols
        d.t = dict(ktok=(ktok, ktok_b), vtok=(vtok, vtok_b), kd=(kd_t, kd_b), mA=(mA, mA_b), mB=(mB, mB_b),
                   NA=(NAt, NA_b), NB=(NBt, NB_b), tA=(tA, tA_b), tB=(tB, tB_b), tY=(tY, tY_b), aT=(aTt, aTt_b),
                   qeT=(qeT, qeT_b), TbT=(TbT, TbT_b), TwT=(TwT, TwT_b), u=(u_t, u_b), nwT=(nwT, nwT_b),
                   vnew=(vnew, vnew_b), sml=(sml, sml_b))
        return d

    st1_bufs = []
    aT_alias_b = actT[0][1]

    def mk_set1():
        need_a = 11 * TT
        arenaA = actT[0][0][:].rearrange("p k t -> p (k t)").bitcast(F32)
        arenas = [arenaA] + [w[0][:].rearrange("p k n -> p (k n)").bitcast(F32) for w in wdp]
        sizes = [max(KFH, KD) * TT // 2] + [KFH * 64] * len(wdp)
        use_alias = (sizes[0] >= need_a) and (sizes[1] >= 1408) and len(wdp) >= 2 and TT == 512
        pos = [0] * len(arenas)

        def al(name, n, ai):
            if not use_alias:
                t_, b_ = sb("s1_" + name, [128, n])
                return t_[:], b_
            assert pos[ai] + n <= sizes[ai], (name, ai, pos[ai], n, sizes[ai])
            v_ = arenas[ai][:, pos[ai]:pos[ai] + n]
            pos[ai] += n
            b_ = Buf("s1_" + name)
            st1_bufs.append(b_)
            return v_, b_
        d = DS()
        d.qkv = [al("qkv%d" % i, TT, 0) for i in range(3)]
        d.zs = al("zs", TT, 0)
        d.RB = dict(RB)
        for n_ in ("R", "P", "Q", "lsq", "G"):
            d.RB[n_] = al("RB_" + n_, TT, 0)
        d.RB["t"] = d.RB["lsq"]
        d.RB["E"] = d.RB["lsq"]
        d.stk = [al("stk%d" % i, TT, 0) for i in range(2)]
        d.cols = [al("cols%d" % i, 128, 1) for i in range(2)]
        d.t = {}
        for n_ in ("ktok", "vtok", "kd", "mA", "mB"):
            d.t[n_] = al(n_, 128, 1)
        d.t["NA"] = al("NA", 256, 1)
        d.t["NB"] = al("NB", 256, 1)
        d.t["tA"] = al("tA", 128, 2)
        d.t["tB"] = al("tB", 128, 2)
        d.t["tY"] = al("tY", 256, 2)
        for n_ in ("aT", "qeT", "TbT", "TwT", "u", "nwT", "vnew"):
            d.t[n_] = al(n_, 128, 2)
        sm_, smb_ = sb("s1_sml", [128, 8])
        d.t["sml"] = (sm_[:], smb_)
        d.alias = use_alias
        return d

    def inherit(dst, src):
        acc = []
        for b_ in src:
            if b_.w is not None:
                acc.append(b_.w)
            acc.extend(b_.r)
        for b_ in dst:
            b_.r = list(b_.r) + acc

    dsets = [mk_set0(), mk_set1()]
    for i in range(2):
        S.op("dve", lambda e, i=i: e.memset(dsets[1].stk[i][0], 0.0), writes=[V(dsets[1].stk[i][1], dsets[1].stk[i][0])])

    def dn_prep(h, st):
        qkv_, RB_ = st.qkv, st.RB
        zs_, zs_b_ = st.zs

        def rbs(n, sl=None):
            t_, b_ = RB_[n]
            return V(b_, t_[:] if sl is None else t_[:, sl])
        for i3 in range(3):
            pp = proj(QB + i3 * DNW + h * 128)
            ci = i3 * NH + h
            cp(V(pre_b, pre[:, 0:3]), V(ptail[ci][1], ptail[ci][0][:]))
            cp(V(pre_b, pre[:, 3:3 + TT]), pp, eng="act")
            cp(V(ptail[ci][1], ptail[ci][0][:]), V(pre_b, pre[:, TT:TT + 3]))
            for k in range(4):
                wk = V(wshT_b, wshT[:, ci, k:k + 1])
                if k == 0:
                    ts(V(cacc_b, cacc[:]), V(pre_b, pre[:, 0:TT]), wk, None, ALU.mult)
                else:
                    stt(V(cacc_b, cacc[:]), V(pre_b, pre[:, k:k + TT]), wk, V(cacc_b, cacc[:]), ALU.mult, ALU.add)
            act(V(qkv_[i3][1], qkv_[i3][0][:]), V(cacc_b, cacc[:]), AF.Silu)
        pz = proj(QB + 3 * DNW + h * 128)
        act(V(zs_b_, zs_[:]), pz, AF.Silu)
        qT, kT, vT = [V(qkv_[i][1], qkv_[i][0][:]) for i in range(3)]
        pb1, pb1_b = PS()
        build_sel(h)
        mm(V(pb1_b, pb1[:, 0:TT]), V(selm_b, selm[:, 0, :]), V(abT_b, abT[:]))
        pa1, pa1_b = PS()
        mm(V(pa1_b, pa1[:, 0:TT]), V(selm_b, selm[:, 1, :]), V(abT_b, abT[:]))
        act(rbs("t"), V(pb1_b, pb1[:, 0:TT]), AF.Exp, scale=-1.0)
        act(rbs("lnb"), rbs("t"), AF.Ln, bias=1.0)
        ts(rbs("lnb"), rbs("lnb"), -1.0, None, ALU.mult)
        act(rbs("beta"), rbs("lnb"), AF.Exp)
        act(rbs("t"), V(pa1_b, pa1[:, 0:TT]), AF.Exp, bias=V(hvec_b, hvec[:, 1, h:h + 1]))
        act(rbs("g"), rbs("t"), AF.Ln, bias=1.0)
        ts(rbs("g"), rbs("g"), V(negA_b, negA[:, h:h + 1]), None, ALU.mult)
        S.op("dve", lambda e: e.tensor_tensor_scan(RB_["G"][0][:], rmask[:], RB_["g"][0][:], 0.0, ALU.mult, ALU.add),
             reads=[V(rmask_b, rmask[:]), rbs("g")], writes=[rbs("G")])
        for (src, dst) in ((kT, "lsk"), (qT, "lsq")):
            act(V(cacc_b, cacc[:]), src, AF.Square)
            pss, pss_b = PS()
            mm(V(pss_b, pss[:, 0:TT]), ONES, V(cacc_b, cacc[:]))
            act(rbs(dst), V(pss_b, pss[:, 0:TT]), AF.Ln, bias=EPS)
        stt(rbs("R"), rbs("lsk"), 0.5, rbs("G"), ALU.mult, ALU.add)
        stt(rbs("P"), rbs("lsk"), -0.5, rbs("G"), ALU.mult, ALU.add)
        tt(rbs("P"), rbs("P"), rbs("lnb"), ALU.add)
        stt(rbs("Q"), rbs("lsq"), -0.5, rbs("G"), ALU.mult, ALU.add)
        ts(rbs("Q"), rbs("Q"), float(np.log(128.0 ** -0.5)), None, ALU.add)
        act(rbs("E"), rbs("Q"), AF.Exp)
        act(rbs("nEP"), rbs("P"), AF.Exp)
        ts(rbs("nEP"), rbs("nEP"), -1.0, None, ALU.mult)
        ts(rbs("nR"), rbs("R"), -1.0, None, ALU.mult)
        for blk in range(NB):
            bs = slice(blk * 128, (blk + 1) * 128)
            last = blk * 128 + 127
            glast = V(RB_["G"][1], RB_["G"][0][:, last:last + 1])
            act(rbs("Kd", bs), rbs("nR", bs), AF.Exp, bias=glast)
        for (pi, nme) in ((0, "nR"), (32, "P"), (64, "beta"), (96, "nEP")):
            cp(V(st.stk[0][1], st.stk[0][0][pi:pi + 1, :]), V(RB_[nme][1], RB_[nme][0][pi:pi + 1, :]))
        cp(V(st.stk[1][1], st.stk[1][0][0:1, :]), V(RB_["Kd"][1], RB_["Kd"][0][0:1, :]))

    def dn_blocks(h, st):
        qkv_, RB_ = st.qkv, st.RB
        zs_, zs_b_ = st.zs
        T_ = st.t

        def rbs(n, sl=None):
            t_, b_ = RB_[n]
            return V(b_, t_[:] if sl is None else t_[:, sl])

        def tv(n, sl=None):
            t_, b_ = T_[n]
            return V(b_, t_[:] if sl is None else t_[:, sl])
        Sb, Sbb = Sst[h]
        SV = V(Sbb, Sb[:])
        for blk in range(NB):
            bs = slice(blk * 128, (blk + 1) * 128)
            last = blk * 128 + 127
            for i in range(2):
                pc, pcb = PS()
                tr(V(pcb, pc[:, 0:128]), V(st.stk[i][1], st.stk[i][0][:, bs]))
                cp(V(st.cols[i][1], st.cols[i][0][:]), V(pcb, pc[:, 0:128]), eng="act")
            c0, c0b = st.cols[0]
            cnR = V(c0b, c0[:, 0:1])
            cP = V(c0b, c0[:, 32:33])
            cbeta = V(c0b, c0[:, 64:65])
            cnEP = V(c0b, c0[:, 96:97])
            cKd = V(st.cols[1][1], st.cols[1][0][:, 0:1])
            qTb = V(qkv_[0][1], qkv_[0][0][:, bs])
            kTb = V(qkv_[1][1], qkv_[1][0][:, bs])
            vTb = V(qkv_[2][1], qkv_[2][0][:, bs])
            ptk, ptk_b = PS()
            tr(V(ptk_b, ptk[:, 0:128]), kTb)
            tr(V(ptk_b, ptk[:, 128:256]), vTb)
            pkk, pkk_b = PS()
            mm(V(pkk_b, pkk[:, 0:128]), kTb, kTb)
            mm(V(pkk_b, pkk[:, 128:256]), kTb, qTb)
            yield
            cp(tv("ktok"), V(ptk_b, ptk[:, 0:128]), eng="act")
            cp(tv("vtok"), V(ptk_b, ptk[:, 128:256]), eng="act")
            ts(tv("kd"), V(ptk_b, ptk[:, 0:128]), cKd, None, ALU.mult)
            tt(tv("mA"), NEGS, rbs("R", bs), ALU.subtract)
            act(tv("mA"), tv("mA"), AF.Exp, bias=cP)
            tt(tv("mB"), NEGST, rbs("P", bs), ALU.add)
            act(tv("mB"), tv("mB"), AF.Exp, bias=cnR)
            tt(tv("aT"), NEGTT, rbs("Q", bs), ALU.add)
            act(tv("aT"), tv("aT"), AF.Exp, bias=cnR)
            yield
            stt(tv("mA"), V(pkk_b, pkk[:, 0:128]), -1.0, tv("mA"), ALU.mult, ALU.mult)
            stt(tv("mB"), V(pkk_b, pkk[:, 0:128]), -1.0, tv("mB"), ALU.mult, ALU.mult)
            tt(tv("aT"), V(pkk_b, pkk[:, 128:256]), tv("aT"), ALU.mult)
            tt(tv("qeT"), qTb, rbs("E", bs), ALU.mult)
            XTv = tv("NA", slice(128, 256))
            XUv = tv("NB", slice(128, 256))
            tt(tv("NA", slice(0, 128)), tv("mA"), BD16, ALU.mult)
            tt(tv("NB", slice(0, 128)), tv("mB"), BD16, ALU.mult)
            cp(XTv, IDN)
            cp(XUv, IDN)
            yield
            for lev in range(4):
                p1, p1b = PS()
                p2, p2b = PS()
                if lev == 3:
                    mm(V(p1b, p1[:, 128:256]), tv("NA", slice(0, 128)), XUv)
                    mm(V(p2b, p2[:, 128:256]), tv("NB", slice(0, 128)), XTv)
                    yield
                else:
                    mm(V(p1b, p1[:, 0:256]), tv("NA", slice(0, 128)), tv("NB", slice(0, 256)))
                    mm(V(p2b, p2[:, 0:256]), tv("NB", slice(0, 128)), tv("NA", slice(0, 256)))
                    yield
                    cp(tv("NB", slice(0, 128)), V(p1b, p1[:, 0:128]), eng="act")
                    cp(tv("NA", slice(0, 128)), V(p2b, p2[:, 0:128]), eng="act")
                tt(XUv, V(p1b, p1[:, 128:256]), XUv, ALU.add)
                tt(XTv, V(p2b, p2[:, 128:256]), XTv, ALU.add)
                yield
            for li in range(3):
                lastm = (li == 2)
                tt(tv("tA"), tv("mA"), LLm[li], ALU.mult)
                pY, pYb = PS()
                mm(V(pYb, pY[:, 0:128]), tv("tA"), XUv)
                if not lastm:
                    tt(tv("tB"), tv("mB"), URm[li], ALU.mult)
                    mm(V(pYb, pY[:, 128:256]), tv("tB"), XTv)
                    yield
                    cp(tv("tY", slice(0, 256)), V(pYb, pY[:, 0:256]), eng="act")
                else:
                    yield
                    cp(tv("tY", slice(0, 128)), V(pYb, pY[:, 0:128]), eng="act")
                pZ, pZb = PS()
                mm(V(pZb, pZ[:, 0:128]), XTv, tv("tY", slice(0, 128)))
                if not lastm:
                    mm(V(pZb, pZ[:, 128:256]), XUv, tv("tY", slice(128, 256)))
                yield
                tt(XUv, V(pZb, pZ[:, 0:128]), XUv, ALU.add)
                if not lastm:
                    tt(XTv, V(pZb, pZ[:, 128:256]), XTv, ALU.add)
            ts(tv("TbT"), XUv, cbeta, None, ALU.mult)
            ts(tv("TwT"), XUv, cnEP, None, ALU.mult)
            pu_, pu_b = PS()
            mm(V(pu_b, pu_[:, 0:128]), tv("TbT"), tv("vtok"))
            mm(V(pu_b, pu_[:, 128:256]), tv("ktok"), tv("TwT"))
            yield
            cp(tv("u"), V(pu_b, pu_[:, 0:128]), eng="act")
            cp(tv("nwT"), V(pu_b, pu_[:, 128:256]), eng="act")
            pv, pvb = PS()
            mm(V(pvb, pv[:, 0:128]), tv("nwT"), SV)
            yield
            tt(tv("vnew"), V(pvb, pv[:, 0:128]), tv("u"), ALU.add)
            po, pob = PS()
            mm(V(pob, po[:, 0:128]), tv("qeT"), SV, start=True, stop=False)
            mm(V(pob, po[:, 0:128]), tv("aT"), tv("vnew"), start=False, stop=True)
            pds, pdsb = PS()
            mm(V(pdsb, pds[:, 0:128]), tv("kd"), tv("vnew"))
            act(tv("sml", slice(0, 1)), V(RB_["G"][1], RB_["G"][0][:, last:last + 1]), AF.Exp)
            yield
            stt(SV, SV, tv("sml", slice(0, 1)), V(pdsb, pds[:, 0:128]), ALU.mult, ALU.add)
            act(tv("TbT"), V(pob, po[:, 0:128]), AF.Square, accum=tv("sml", slice(1, 2)))
            act(tv("sml", slice(2, 3)), tv("sml", slice(1, 2)), AF.Sqrt, bias=EPS, scale=1.0 / 128)
            smt, smb = T_["sml"]
            S.op("dve", lambda e, smt=smt: e.reciprocal(smt[:, 3:4], smt[:, 2:3]), reads=[tv("sml", slice(2, 3))], writes=[tv("sml", slice(3, 4))])
            ts(tv("TbT"), V(pob, po[:, 0:128]), tv("sml", slice(3, 4)), None, ALU.mult)
            pt2, pt2b = PS()
            tr(V(pt2b, pt2[:, 0:128]), tv("TbT"))
            yield
            stt(V(yT_b, yT[:, KCg + h, bs]), V(pt2b, pt2[:, 0:128]), V(onw_b, onw[:, 0:1]), V(zs_b_, zs_[:, bs]), ALU.mult, ALU.mult)

    def mixer():
        for j in range(KC):
            pa = proj(j * 128)
            pgt = proj(CW + j * 128)
            act(V(sg_b, sg[:]), pgt, AF.Sigmoid)
            cp(V(glu_b, glu[:, 0:30]), V(gtail[j][1], gtail[j][0][:]))
            tt(V(glu_b, glu[:, 30:30 + TT]), pa, V(sg_b, sg[:]), ALU.mult)
            cp(V(gtail[j][1], gtail[j][0][:]), V(glu_b, glu[:, TT:TT + 30]))
            for k in range(31):
                wk = V(wdwT_b, wdwT[:, j, k:k + 1])
                if k == 0:
                    ts(V(cacc_b, cacc[:]), V(glu_b, glu[:, 0:TT]), wk, V(cvec_b, cvec[:, 0, j:j + 1]), ALU.mult, ALU.add)
                elif k < 30:
                    stt(V(cacc_b, cacc[:]), V(glu_b, glu[:, k:k + TT]), wk, V(cacc_b, cacc[:]), ALU.mult, ALU.add)
                else:
                    stt(V(ypre_b, ypre[:, j, :]), V(glu_b, glu[:, k:k + TT]), wk, V(cacc_b, cacc[:]), ALU.mult, ALU.add)
        if CUT <= 1:
            return
        S.dma("pool", wab[:, :, 0:NH], win_v[:, :, QB + 4 * DNW:QB + 4 * DNW + NH], reads=[], writes=[V(wab_b, wab[:])], key=wab_b, slow=True)
        S.dma("pool", wab[:, :, 16:16 + NH], win_v[:, :, QB + 4 * DNW + NH:QB + 4 * DNW + 2 * NH], reads=[], writes=[V(wab_b, wab[:])], key=wab_b, slow=True)
        pab, pab_b = PS()
        for kc in range(KD):
            mm(V(pab_b, pab[0:NH, 0:TT]), V(wab_b, wab[:, kc, 0:NH]), V(hT_b, hT[:, kc, :]), start=(kc == 0), stop=(kc == KD - 1))
        cp(V(abT_b, abT[0:NH, :]), V(pab_b, pab[0:NH, 0:TT]))
        pab2, pab2_b = PS()
        for kc in range(KD):
            mm(V(pab2_b, pab2[32:32 + NH, 0:TT]), V(wab_b, wab[:, kc, 16:16 + NH]), V(hT_b, hT[:, kc, :]), start=(kc == 0), stop=(kc == KD - 1))
        cp(V(abT_b, abT[32:32 + NH, :]), V(pab2_b, pab2[32:32 + NH, 0:TT]))

        if CUT <= 2:
            return
        inherit([bb for bb in st1_bufs], [aT_alias_b] + [w[1] for w in wdp])
        for hp in range(0, NH, 2):
            hs = [hh for hh in (hp, hp + 1) if hh < NH]
            for i, hh in enumerate(hs):
                dn_prep(hh, dsets[i])
            gens = [dn_blocks(hh, dsets[i]) for i, hh in enumerate(hs)]
            while gens:
                for g in list(gens):
                    try:
                        next(g)
                    except StopIteration:
                        gens.remove(g)
        inherit([aT_alias_b] + [w[1] for w in wdp], [bb for bb in st1_bufs])

        if SPLIT:
            yv = yT[:, KCg:KCg + NHg, :].rearrange("p k t -> p (k t)")
            ypv = ypre.rearrange("p k t -> p (k t)")
            S.dma("sp", gx["cin1"][0], ypv[:, 0:KC * TT], reads=[V(ypre_b, ypre)], writes=[V(gx["cin1"][1], gx["cin1"][0])], key=ypre_b)
            S.dma("sp", gx["cin2"][0], yv[:, 0:NH * TT], reads=[V(yT_b, yT[:])], writes=[V(gx["cin2"][1], gx["cin2"][0])], key=yT_b)
            S.coll(gx["cin1"][0], gx["cout1"][0], PAIRS, reads=[V(gx["cin1"][1], gx["cin1"][0])], writes=[V(gx["cout1"][1], gx["cout1"][0])], key=gx["cout1"][1])
            S.coll(gx["cin2"][0], gx["cout2"][0], PAIRS, reads=[V(gx["cin2"][1], gx["cin2"][0])], writes=[V(gx["cout2"][1], gx["cout2"][0])], key=gx["cout2"][1])
            S.dma("sp", ypv.rearrange("p (r c) -> p r c", r=2), gx["cout1"][0].rearrange("(r p) c -> p r c", p=128),
                  reads=[V(gx["cout1"][1], gx["cout1"][0])], writes=[V(ypre_b, ypre)], key=ypre_b)
            S.dma("sp", yv.rearrange("p (r c) -> p r c", r=2), gx["cout2"][0].rearrange("(r p) c -> p r c", p=128),
                  reads=[V(gx["cout2"][1], gx["cout2"][0])], writes=[V(yT_b, yT[:])], key=yT_b)
        pmn, pmn_b = PS()
        pvr, pvr_b = PS()
        for j in range(KCg):
            mm(V(pmn_b, pmn[:, 0:TT]), ONES, V(ypre_b, ypre[:, j, :]), start=(j == 0), stop=(j == KCg - 1))
        for j in range(KCg):
            yq, yqb = RB[("Q", "lsq")[j % 2]]
            act(V(yqb, yq[:]), V(ypre_b, ypre[:, j, :]), AF.Square)
            mm(V(pvr_b, pvr[:, 0:TT]), ONES, V(yqb, yq[:]), start=(j == 0), stop=(j == KCg - 1), sig=True)
        ts(V(mean_b, mean_t[:]), V(pmn_b, pmn[:, 0:TT]), 1.0 / (KCg * 128), None, ALU.mult)
        tt(V(rs2_b, rs2[:]), V(mean_b, mean_t[:]), V(mean_b, mean_t[:]), ALU.mult)
        stt(V(rs2_b, rs2[:]), V(pvr_b, pvr[:, 0:TT]), 1.0 / (KCg * 128), V(rs2_b, rs2[:]), ALU.mult, ALU.subtract)
        act(V(rs2_b, rs2[:]), V(rs2_b, rs2[:]), AF.Sqrt, bias=EPS, scale=1.0)
        S.op("dve", lambda e: e.reciprocal(rs2[:], rs2[:]), reads=[V(rs2_b, rs2[:])], writes=[V(rs2_b, rs2[:])])
        for j in range(KCg):
            tt(V(cacc_b, cacc[:]), V(ypre_b, ypre[:, j, :]), V(mean_b, mean_t[:]), ALU.subtract)
            tt(V(cacc_b, cacc[:]), V(cacc_b, cacc[:]), V(rs2_b, rs2[:]), ALU.mult)
            act(V(yT_b, yT[:, j, :]), V(cacc_b, cacc[:]), AF.Silu, bias=V(lnT_b, lnT[:, 1, j:j + 1]),
                scale=V(lnT_b, lnT[:, 0, j:j + 1]))

        if CUT <= 11:
            return
        for m in range(KD):
            wt, wb = wload(wa, wa_i, wout_v[:, :, m * 128:(m + 1) * 128], KY, "out%d" % m)
            p, pb = PS()
            for kc in range(KY):
                mm(V(pb, p[:, 0:TT]), V(wb, wt[:, kc, :]), V(yT_b, yT[:, kc, :]), start=(kc == 0), stop=(kc == KY - 1))
            stt(V(xT_b, xT[:, m, :]), V(pb, p[:, 0:TT]), V(Gcoef_b, Gcoef[:, 1, m:m + 1]), V(xT_b, xT[:, m, :]), ALU.mult, ALU.add)

    out_bufs = []
    for t in range(NT):
        cur_tile[0] = t
        for tb in range(NB):
            xt, xb = xin[tb % 2]
            r0 = t * TT + tb * 128
            ld(V(xb, xt[:]), x_d[r0:r0 + 128, :])
            for k0 in range(0, KD, 4):
                p, pb = PS()
                nk = min(4, KD - k0)
                for kk in range(nk):
                    tr(V(pb, p[:, kk * 128:(kk + 1) * 128]), V(xb, xt[:, (k0 + kk) * 128:(k0 + kk + 1) * 128]))
                S.op("dve", lambda e, p=p, k0=k0, nk=nk, tb=tb: e.tensor_copy(
                    xT[:, k0:k0 + nk, tb * 128:(tb + 1) * 128], p[:, 0:nk * 128].rearrange("p (k t) -> p k t", k=nk)),
                    reads=[V(pb, p[:])], writes=[V(xT_b, xT[:])])
        rms_mod(0)
        ffn(0, 0)
        if stop != "ffn1":
            rms_mod(1)
            mixer()
            if stop != "mixer":
                rms_mod(2)
                ffn(1, 2)
        if stop is None:
            rms_rstd()
            for kc in range(KD):
                tt(V(tmpn_b, tmpn[:]), V(xT_b, xT[:, kc, :]), V(rstd_b, rstd[:]), ALU.mult)
                ts(V(xT_b, xT[:, kc, :]), V(tmpn_b, tmpn[:]), V(normT_b, normT[:, 3, kc:kc + 1]), None, ALU.mult)
        for tb in range(NB):
            xt, xb = xin[tb % 2]
            r0 = t * TT + tb * 128
            for k0 in range(0, KD, 4):
                p, pb = PS()
                nk = min(4, KD - k0)
                for kk in range(nk):
                    tr(V(pb, p[:, kk * 128:(kk + 1) * 128]), V(xT_b, xT[:, k0 + kk, tb * 128:(tb + 1) * 128]))
                cp(V(xb, xt[:, k0 * 128:(k0 + nk) * 128]), V(pb, p[:, 0:nk * 128]), eng="act")
            S.dma("sp", y_d[r0:r0 + 128, :], xt[:], reads=[V(xb, xt[:])], writes=[], key=xb)
            if xb not in out_bufs:
                out_bufs.append(xb)
    S.emit(final_waits=out_bufs)
    es.close()
    return nc


def host_inputs(cfg, core, x, c, w_ada, b_ada, ffn1_norm, ffn1_wg, ffn1_wu, ffn1_wd, mix_norm, w_in,
                w_dw, b_dw, conv_ln_w, conv_ln_b, w_short, a_log, dt_bias, dn_norm_w, w_out,
                ffn2_norm, ffn2_wg, ffn2_wu, ffn2_wd, final_norm):
    KD, KC, NH, TT = cfg["KD"], cfg["KC"], cfg["NH"], cfg["TT"]
    split = cfg.get("split", 0)
    CWg, NHg, CW = cfg["CWg"], cfg["NHg"], cfg["CW"]
    KCg = CWg // 128
    if split:
        b, rank = core // 2, core % 2
    else:
        b, rank = core, 0
    f = np.float32
    A = np.ascontiguousarray

    def fm(v, k):
        return A(np.asarray(v, f).reshape(k, 128).T)

    consts = np.zeros((128, 12, 128), f)
    idx = np.arange(128)
    consts[:, 0, :] = np.eye(128, dtype=f)
    consts[:, 1, :] = 1.0
    consts[:, 2, :] = np.where(idx[:, None] > idx[None, :], 0.0, NEG)
    consts[:, 3, :] = np.where(idx[None, :] > idx[:, None], 0.0, NEG)
    consts[:, 4, :] = np.where(idx[None, :] >= idx[:, None], 0.0, NEG)
    consts[:, 5, :] = (idx[:, None] // 16 == idx[None, :] // 16)
    for i, bsz in enumerate((16, 32, 64)):
        ll = ((idx[:, None] // (2 * bsz) == idx[None, :] // (2 * bsz)) & (idx[:, None] % (2 * bsz) >= bsz)
              & (idx[None, :] % (2 * bsz) < bsz)).astype(f)
        consts[:, 6 + i, :] = ll
        consts[:, 9 + i, :] = ll.T
    rmask = np.ones((128, TT), f)
    rmask[:, ::128] = 0.0
    DNWg = NHg * 128
    DNW = NH * 128
    c0, h0 = rank * CW, rank * NH
    cols = np.concatenate([
        np.arange(c0, c0 + CW), CWg + np.arange(c0, c0 + CW),
        2 * CWg + 0 * DNWg + h0 * 128 + np.arange(DNW), 2 * CWg + 1 * DNWg + h0 * 128 + np.arange(DNW),
        2 * CWg + 2 * DNWg + h0 * 128 + np.arange(DNW), 2 * CWg + 3 * DNWg + h0 * 128 + np.arange(DNW),
        2 * CWg + 4 * DNWg + h0 + np.arange(NH), 2 * CWg + 4 * DNWg + NHg + h0 + np.arange(NH)])
    shc = np.concatenate([i3 * DNWg + h0 * 128 + np.arange(DNW) for i3 in range(3)])
    w_in_l = np.asarray(w_in[0], f)[:, cols]
    w_dw_l = np.asarray(w_dw[0], f)[:, c0:c0 + CW]
    w_sh_l = np.asarray(w_short[0], f)[:, shc]
    z1 = np.zeros(CW, f)
    if split:
        ncr = min(4, cfg["ncores"])
        nb = ncr // 2
        nml = 9 * KD // ncr
        gi, ri = core // ncr, core % ncr
        msl = slice(ri * nml * 128, (ri + 1) * nml * 128)
        ohm = np.zeros((128, nb), f)
        ohm[:, b - gi * nb] = 1.0
        cg_ = np.asarray(c, f)[gi * nb:(gi + 1) * nb]
        mods_in = {"cT": A(cg_.T.reshape(KD, 128, nb).transpose(1, 0, 2)),
                   "w_ada": A(np.asarray(w_ada[0], f)[:, msl]),
                   "b_adaT": fm(np.asarray(b_ada[0], f)[msl], nml),
                   "onehot": ohm}
    else:
        mods_in = {"cT": fm(c[b], KD), "w_ada": A(np.asarray(w_ada[0], f)), "b_adaT": fm(b_ada[0], 9 * KD)}
    DFFl = cfg["DFF"] // (1 + split)
    fs = slice(rank * DFFl, (rank + 1) * DFFl)
    return {
        "x": A(np.asarray(x[b], f)),
        **mods_in,
        "normT": A(np.stack([fm(ffn1_norm[0], KD), fm(mix_norm[0], KD), fm(ffn2_norm[0], KD), fm(final_norm, KD)], axis=1)),
        "ffn1_wg": A(np.asarray(ffn1_wg[0], f)[:, fs]), "ffn1_wu": A(np.asarray(ffn1_wu[0], f)[:, fs]), "ffn1_wd": A(np.asarray(ffn1_wd[0], f)[fs, :]),
        "ffn2_wg": A(np.asarray(ffn2_wg[0], f)[:, fs]), "ffn2_wu": A(np.asarray(ffn2_wu[0], f)[:, fs]), "ffn2_wd": A(np.asarray(ffn2_wd[0], f)[fs, :]),
        "w_in": A(w_in_l),
        "w_out": A(np.asarray(w_out[0], f)),
        "w_dwT": A(w_dw_l.T.reshape(KC, 128, 31).transpose(1, 0, 2)),
        "cvecT": A(np.stack([fm(np.asarray(b_dw[0], f)[c0:c0 + CW], KC), fm(z1, KC), fm(z1, KC)], axis=1)),
        "lnT": A(np.stack([fm(conv_ln_w[0], KCg), fm(conv_ln_b[0], KCg)], axis=1)),
        "w_shT": A(w_sh_l.T.reshape(3 * NH, 128, 4).transpose(1, 0, 2)),
        "hvec": A(np.broadcast_to(np.stack([np.asarray(a_log[0], f)[h0:h0 + NH], np.asarray(dt_bias[0], f)[h0:h0 + NH]], axis=0)[None], (128, 2, NH))),
        "onwT": A(np.asarray(dn_norm_w[0], f).reshape(128, 1)),
        "consts": consts,
        "rmask": rmask,
    }


_NC_CACHE = {}


def kernel(**inputs):
    cfg = FULL
    B = inputs["x"].shape[0]
    if "nc" not in _NC_CACHE:
        _NC_CACHE["nc"] = build(cfg)
    nc = _NC_CACHE["nc"]
    n = 8
    in_maps = [host_inputs(cfg, i, **inputs) for i in range(n)]
    res = run_bass_kernel_spmd(nc, in_maps, core_ids=list(range(n)))
    out = np.stack([res.results[2 * i]["y"] for i in range(B)], axis=0)
    return out.astype(np.float32)
```
